# Optimizing a Trainium2 kernel written in Bass

```python
import jax, jax.numpy as jnp
from jax import lax
import numpy as np

D_MODEL = 4096
BATCH = 4
SEQ = 2048
DEPTH = 2

GRID_W = 64
CTX_LEN = 256
LRU_W = D_MODEL // 4
LRU_BLOCKS = 8
LRU_BW = LRU_W // LRU_BLOCKS
LRU_C = 8.0
CONV_W = 4
NA_DH = 128
NA_W = D_MODEL // 2
NA_HEADS = NA_W // NA_DH
WIN_R = 8
WIN_C = 16
ROPE_BASE = 10000.0
FFT_W = D_MODEL // 4
FFT_GROUPS = 4
FFT_CG = FFT_W // FFT_GROUPS
N_BRANCH = 3
IN_W = LRU_W + 3 * NA_W + FFT_W + N_BRANCH * D_MODEL
IN_SPLITS = [LRU_W, LRU_W + NA_W, LRU_W + 2 * NA_W, LRU_W + 3 * NA_W, LRU_W + 3 * NA_W + FFT_W]
D_FF = 3 * D_MODEL // 2
N_MOD = 9
EPS = 1e-6

kernel_name = "hybrid_lru_natten_fnet_macaron_dit"


def _rmsnorm(x, g):
    xf = x.astype(jnp.float32)
    y = xf * lax.rsqrt(jnp.mean(xf * xf, axis=-1, keepdims=True) + EPS)
    return (y * g.astype(jnp.float32)).astype(x.dtype)


def _modulate(h, shift, scale):
    return h * (1.0 + scale) + shift


def _swiglu(h, w13, w2):
    a, b = jnp.split(h @ w13, 2, axis=-1)
    return (jax.nn.silu(a) * b) @ w2


def _dwconv(u, w, b):
    left = CONV_W // 2
    y = lax.conv_general_dilated(u, w[:, None, :], window_strides=(1,), padding=[(left, CONV_W - 1 - left)],
                                 dimension_numbers=("NWC", "WIO", "NWC"), feature_group_count=u.shape[-1])
    return y + b


def _lin_combine(left, right):
    a1, b1 = left
    a2, b2 = right
    return a1 * a2, a2 * b1 + b2


def _rglru_scan(u, w_a, b_a, w_i, b_i, lam, h0):
    bsz, L, _ = u.shape
    ub = u.reshape(bsz, L, LRU_BLOCKS, LRU_BW)
    r = jax.nn.sigmoid(jnp.einsum("blhi,hij->blhj", ub, w_a).reshape(bsz, L, LRU_W) + b_a)
    i = jax.nn.sigmoid(jnp.einsum("blhi,hij->blhj", ub, w_i).reshape(bsz, L, LRU_W) + b_i)
    log_a = -LRU_C * r * jax.nn.softplus(-lam)
    a = jnp.exp(log_a)
    inp = jnp.sqrt(-jnp.expm1(2.0 * log_a)) * (i * u)
    a_cum, h = lax.associative_scan(_lin_combine, (a, inp), axis=1)
    h = h + a_cum * h0[:, None, :]
    return h, h[:, -1]


def _bidir_rglru(u, uc, conv_w, conv_b, wa, ba, wi, bi, lam, with_ctx_out):
    f32 = jnp.float32
    cw, cb = conv_w.astype(f32), conv_b.astype(f32)
    v_lat = _dwconv(u.astype(f32), cw, cb)
    v_ctx = _dwconv(uc.astype(f32), cw, cb)
    fwd = (wa[0].astype(f32), ba[0].astype(f32), wi[0].astype(f32), bi[0].astype(f32), lam[0].astype(f32))
    bwd = (wa[1].astype(f32), ba[1].astype(f32), wi[1].astype(f32), bi[1].astype(f32), lam[1].astype(f32))
    zeros = jnp.zeros((u.shape[0], LRU_W), f32)
    hc_f, s_f = _rglru_scan(v_ctx, *fwd, zeros)
    hc_b, s_b = _rglru_scan(v_ctx[:, ::-1], *bwd, zeros)
    h_f, _ = _rglru_scan(v_lat, *fwd, s_f)
    h_b, _ = _rglru_scan(v_lat[:, ::-1], *bwd, s_b)
    y = (h_f + h_b[:, ::-1]).astype(u.dtype)
    yc = (hc_f + hc_b[:, ::-1]).astype(u.dtype) if with_ctx_out else None
    return y, yc


def _rope_1d(x, pos):
    d1 = x.shape[-1]
    inv = ROPE_BASE ** (-jnp.arange(0, d1, 2, dtype=jnp.float32) / d1)
    ang = pos[:, None] * inv[None, :]
    cos, sin = jnp.cos(ang)[:, None, :], jnp.sin(ang)[:, None, :]
    x1, x2 = x[..., : d1 // 2], x[..., d1 // 2:]
    return jnp.concatenate([x1 * cos - x2 * sin, x2 * cos + x1 * sin], axis=-1)


def _axial_rope(x):
    L = x.shape[1]
    t = jnp.arange(L)
    row = (t // GRID_W).astype(jnp.float32)
    col = (t % GRID_W).astype(jnp.float32)
    xf = x.astype(jnp.float32)
    half = x.shape[-1] // 2
    return jnp.concatenate([_rope_1d(xf[..., :half], row), _rope_1d(xf[..., half:], col)], axis=-1).astype(x.dtype)


def _neighborhood_attention(q, k, v, k_ctx, v_ctx, rpb):
    bsz, L, H, dh = q.shape
    rows = L // GRID_W
    kr = min(WIN_R, rows)
    scale = dh ** -0.5
    qg = q.reshape(bsz, rows, GRID_W, H, dh).transpose(1, 0, 2, 3, 4)
    kg = k.reshape(bsz, rows, GRID_W, H, dh)
    vg = v.reshape(bsz, rows, GRID_W, H, dh)
    row_ids = jnp.arange(rows)
    row_start = jnp.clip(row_ids - kr // 2, 0, rows - kr)
    cols = jnp.arange(GRID_W)
    col_start = jnp.clip(cols - WIN_C // 2, 0, GRID_W - WIN_C)
    col_in = (cols[None, :] >= col_start[:, None]) & (cols[None, :] < col_start[:, None] + WIN_C)
    col_idx = jnp.clip(cols[None, :] - cols[:, None] + WIN_C - 1, 0, 2 * WIN_C - 2)
    rpb_c = rpb[:, :, col_idx]

    def one_row(args):
        q_r, r, rs = args
        k_blk = lax.dynamic_slice_in_dim(kg, rs, kr, axis=1)
        v_blk = lax.dynamic_slice_in_dim(vg, rs, kr, axis=1)
        row_idx = rs + jnp.arange(kr) - r + WIN_R - 1
        bias = rpb_c[:, row_idx].transpose(0, 2, 1, 3).astype(jnp.float32)
        s_win = jnp.einsum("bqhd,bikhd->bhqik", q_r, k_blk).astype(jnp.float32) * scale + bias[None]
        s_win = jnp.where(col_in[None, None, :, None, :], s_win, -jnp.inf)
        s_ctx = jnp.einsum("bqhd,bkhd->bhqk", q_r, k_ctx).astype(jnp.float32) * scale
        s = jnp.concatenate([s_win.reshape(bsz, H, GRID_W, kr * GRID_W), s_ctx], axis=-1)
        p = jax.nn.softmax(s, axis=-1).astype(q.dtype)
        p_win = p[..., : kr * GRID_W].reshape(bsz, H, GRID_W, kr, GRID_W)
        p_ctx = p[..., kr * GRID_W:]
        return (jnp.einsum("bhqik,bikhd->bqhd", p_win, v_blk)
                + jnp.einsum("bhqk,bkhd->bqhd", p_ctx, v_ctx))

    out = lax.map(one_row, (qg, row_ids, row_start))
    return out.transpose(1, 0, 2, 3, 4).reshape(bsz, L, H * dh)


def _context_attention(q, k, v):
    bsz, Lc, H, dh = q.shape
    s = jnp.einsum("bqhd,bkhd->bhqk", q, k).astype(jnp.float32) * dh ** -0.5
    p = jax.nn.softmax(s, axis=-1).astype(q.dtype)
    return jnp.einsum("bhqk,bkhd->bqhd", p, v).reshape(bsz, Lc, H * dh)


def _fourier_mix(f):
    bsz, L, _ = f.shape
    fg = f.astype(jnp.float32).reshape(bsz, L, FFT_GROUPS, FFT_CG)
    y = jnp.fft.fft2(fg, axes=(1, 3), norm="ortho").real
    return y.reshape(bsz, L, FFT_W).astype(f.dtype)


def _merge(g, y_lru, y_na, y_fft, w_out_lru, w_out_na, w_out_fft, w_o):
    ga, gb, gc = jnp.split(jax.nn.sigmoid(g.astype(jnp.float32)).astype(g.dtype), N_BRANCH, axis=-1)
    m = ga * (y_lru @ w_out_lru) + gb * (y_na @ w_out_na) + gc * (y_fft @ w_out_fft)
    return m @ w_o


def _token_mixer(h, hc, w_in, conv_w, conv_b, lru_wa, lru_ba, lru_wi, lru_bi, lru_lam, rpb,
                 w_out_lru, w_out_na, w_out_fft, w_o, with_ctx_out):
    u, q, k, v, f, g = jnp.split(h @ w_in, IN_SPLITS, axis=-1)
    uc, qc, kc, vc, fc, gc = jnp.split(hc @ w_in, IN_SPLITS, axis=-1)

    def heads(t):
        return t.reshape(t.shape[0], t.shape[1], NA_HEADS, NA_DH)

    y_lru, y_lru_c = _bidir_rglru(u, uc, conv_w, conv_b, lru_wa, lru_ba, lru_wi, lru_bi, lru_lam, with_ctx_out)
    kc_h, vc_h = heads(kc), heads(vc)
    y_na = _neighborhood_attention(_axial_rope(heads(q)), _axial_rope(heads(k)), heads(v), kc_h, vc_h, rpb)
    y = _merge(g, y_lru, y_na, _fourier_mix(f), w_out_lru, w_out_na, w_out_fft, w_o)
    if not with_ctx_out:
        return y, None
    y_na_c = _context_attention(heads(qc), kc_h, vc_h)
    yc = _merge(gc, y_lru_c, y_na_c, _fourier_mix(fc), w_out_lru, w_out_na, w_out_fft, w_o)
    return y, yc


def setup_inputs(seed: int = 0) -> dict:
    key = jax.random.key(seed)
    ks = jax.random.split(key, 25)
    f32 = jnp.float32
    D = D_MODEL

    def nrm(k, shape, s):
        return jax.random.normal(k, shape, f32) * s

    u = jax.random.uniform(ks[17], (DEPTH, 2, LRU_W), f32, minval=0.9, maxval=0.999)
    sa = u ** (1.0 / LRU_C)
    lam = jnp.log(sa) - jnp.log1p(-sa)
    return {
        "x": nrm(ks[0], (BATCH, SEQ, D), 1.0),
        "c": nrm(ks[1], (BATCH, D), 1.0),
        "ctx": nrm(ks[2], (BATCH, CTX_LEN, D), 1.0),
        "c_ctx": nrm(ks[3], (D,), 1.0),
        "w_ada": nrm(ks[4], (DEPTH, D, N_MOD * D), 0.5 * D ** -0.5),
        "b_ada": nrm(ks[5], (DEPTH, N_MOD * D), 0.01),
        "norm_g": 1.0 + nrm(ks[6], (DEPTH, 3, D), 0.02),
        "ffn1_w13": nrm(ks[7], (DEPTH, D, 2 * D_FF), D ** -0.5),
        "ffn1_w2": nrm(ks[8], (DEPTH, D_FF, D), D_FF ** -0.5),
        "ffn2_w13": nrm(ks[9], (DEPTH, D, 2 * D_FF), D ** -0.5),
        "ffn2_w2": nrm(ks[10], (DEPTH, D_FF, D), D_FF ** -0.5),
        "w_in": nrm(ks[11], (DEPTH, D, IN_W), D ** -0.5),
        "conv_w": nrm(ks[12], (DEPTH, CONV_W, LRU_W), CONV_W ** -0.5),
        "conv_b": nrm(ks[13], (DEPTH, LRU_W), 0.01),
        "lru_wa": nrm(ks[14], (DEPTH, 2, LRU_BLOCKS, LRU_BW, LRU_BW), LRU_BW ** -0.5),
        "lru_ba": nrm(ks[15], (DEPTH, 2, LRU_W), 0.01),
        "lru_wi": nrm(ks[16], (DEPTH, 2, LRU_BLOCKS, LRU_BW, LRU_BW), LRU_BW ** -0.5),
        "lru_bi": nrm(ks[18], (DEPTH, 2, LRU_W), 0.01),
        "lru_lam": lam,
        "na_rpb": nrm(ks[19], (DEPTH, NA_HEADS, 2 * WIN_R - 1, 2 * WIN_C - 1), 0.1),
        "w_out_lru": nrm(ks[20], (DEPTH, LRU_W, D), LRU_W ** -0.5),
        "w_out_na": nrm(ks[21], (DEPTH, NA_W, D), NA_W ** -0.5),
        "w_out_fft": nrm(ks[22], (DEPTH, FFT_W, D), FFT_W ** -0.5),
        "w_o": nrm(ks[23], (DEPTH, D, D), D ** -0.5),
        "final_g": 1.0 + nrm(ks[24], (D,), 0.02),
    }


def reference(x, c, ctx, c_ctx, w_ada, b_ada, norm_g, ffn1_w13, ffn1_w2, ffn2_w13, ffn2_w2, w_in, conv_w, conv_b,
              lru_wa, lru_ba, lru_wi, lru_bi, lru_lam, na_rpb, w_out_lru, w_out_na, w_out_fft, w_o, final_g):
    bsz, L, D = x.shape
    xc = ctx
    for l in range(DEPTH):
        last = l == DEPTH - 1
        ml = (jax.nn.silu(c) @ w_ada[l] + b_ada[l]).reshape(bsz, N_MOD, 1, D)
        mc = (jax.nn.silu(c_ctx) @ w_ada[l] + b_ada[l]).reshape(N_MOD, D)
        x = x + 0.5 * ml[:, 2] * _swiglu(_modulate(_rmsnorm(x, norm_g[l, 0]), ml[:, 0], ml[:, 1]),
                                         ffn1_w13[l], ffn1_w2[l])
        xc = xc + 0.5 * mc[2] * _swiglu(_modulate(_rmsnorm(xc, norm_g[l, 0]), mc[0], mc[1]),
                                        ffn1_w13[l], ffn1_w2[l])
        y, yc = _token_mixer(_modulate(_rmsnorm(x, norm_g[l, 1]), ml[:, 3], ml[:, 4]),
                             _modulate(_rmsnorm(xc, norm_g[l, 1]), mc[3], mc[4]),
                             w_in[l], conv_w[l], conv_b[l], lru_wa[l], lru_ba[l], lru_wi[l], lru_bi[l], lru_lam[l],
                             na_rpb[l], w_out_lru[l], w_out_na[l], w_out_fft[l], w_o[l], not last)
        x = x + ml[:, 5] * y
        x = x + 0.5 * ml[:, 8] * _swiglu(_modulate(_rmsnorm(x, norm_g[l, 2]), ml[:, 6], ml[:, 7]),
                                         ffn2_w13[l], ffn2_w2[l])
        if not last:
            xc = xc + mc[5] * yc
            xc = xc + 0.5 * mc[8] * _swiglu(_modulate(_rmsnorm(xc, norm_g[l, 2]), mc[6], mc[7]),
                                            ffn2_w13[l], ffn2_w2[l])
    return _rmsnorm(x, final_g)
```

```python
import numpy as np
import os
import concourse.bass as bass
import concourse.mybir as mybir
from concourse.bass import ds
from concourse.bass_utils import run_bass_kernel_spmd
from contextlib import ExitStack

F32 = mybir.dt.float32
BF16 = mybir.dt.bfloat16
AF = mybir.ActivationFunctionType
ALU = mybir.AluOpType
EPS = 1e-6
NEG = -30000.0
PIECE = 16384 * 1024
WNAMES = ("ffn1_w13", "ffn1_w2", "w_in", "w_out_lru", "w_out_na", "w_out_fft", "w_o", "ffn2_w13", "ffn2_w2")


class Cfg:
    def __init__(s, D=4096, L=2048, LC=256, NG=4, DEPTH=2, dbg=False, stop=99):
        s.D, s.L, s.LC, s.NG, s.DEPTH, s.dbg = D, L, LC, NG, DEPTH, dbg
        s.stop = stop
        s.dbgvc = 3
        s.DC = D // 128
        s.LW = D // 4
        s.NB = s.LW // 128
        s.NAW = D // 2
        s.H = s.NAW // 128
        s.FW = D // 4
        s.CG = s.FW // NG
        s.CGC = s.CG // 128
        s.DFF = 3 * D // 2
        s.FC = s.DFF // 128
        s.INW = s.LW + 3 * s.NAW + s.FW + 3 * D
        s.NIN = s.INW // 128
        s.TL, s.TC = L // 2, LC // 2
        s.T = s.TL + s.TC
        s.LT = L + LC
        s.ROWS = L // 64
        s.NBh, s.Hh, s.NGh = s.NB, s.H, NG
        s.FCh = s.NGh * s.CGC
        s.MH = s.NBh + 3 * s.Hh + s.FCh
        s.YH = s.NBh + s.Hh + s.FCh
        s.MIXC = s.MH
        s.J = 9 * s.DC
        s.tiles = []
        t = 0
        while t < s.TL:
            n = min(512, s.TL - t)
            s.tiles.append((t, n))
            t += n
        s.tiles.append((s.TL, s.TC))
        s.wshape = {"ffn1_w13": (D, 2 * s.DFF), "ffn1_w2": (s.DFF, D), "w_in": (D, s.INW), "w_out_lru": (s.LW, D),
                    "w_out_na": (s.NAW, D), "w_out_fft": (s.FW, D), "w_o": (D, D), "ffn2_w13": (D, 2 * s.DFF),
                    "ffn2_w2": (s.DFF, D)}
        s.woff = {}
        off = 0
        for nm in WNAMES:
            K, N = s.wshape[nm]
            s.woff[nm] = off
            off += K * N
        s.EL = off
        assert s.EL % 16384 == 0
        s.kwL, s.kwC = min(512, L), min(512, LC)
        s.coff = {"dftL": 0, "dftC": L * 2 * L, "dftG": L * 2 * L + LC * 2 * LC}
        ec = s.coff["dftG"] + s.CG * 2 * s.CG
        s.EC = (ec + 16383) // 16384 * 16384
        c = 0
        s.vo = {}
        for nm, n in (("norm_g", DEPTH * 3 * s.DC), ("final_g", s.DC), ("conv_w", DEPTH * s.NBh * 4),
                      ("conv_b", DEPTH * s.NBh), ("ba", DEPTH * 2 * s.NBh), ("bi", DEPTH * 2 * s.NBh),
                      ("lam", DEPTH * 2 * s.NBh), ("permT", 128), ("bada", DEPTH * s.J), ("cv", s.DC * 5)):
            s.vo[nm] = c
            c += n
        s.NV = c


def pieces(E):
    out = []
    a = 0
    while a < E:
        b = min(E, a + PIECE)
        out.append((a, b))
        a = b
    return out


def tile_w(W):
    K, N = W.shape
    return np.ascontiguousarray(W.reshape(K // 128, 128, N // 128, 128).transpose(2, 1, 0, 3)).reshape(-1)


def shard_flat(flat, E):
    parts = [[] for _ in range(8)]
    for a, b in pieces(E):
        n = (b - a) // 8
        for r in range(8):
            parts[r].append(flat[a + r * n: a + (r + 1) * n])
    return [np.concatenate(p).reshape(-1, 2048) for p in parts]


def pvec(v):
    return np.ascontiguousarray(np.asarray(v, np.float32).reshape(-1, 128).T)


def host_consts(cf):
    L, LC, CG = cf.L, cf.LC, cf.CG
    flat = np.zeros(cf.EC, np.float32)

    def dft(n, kw):
        t = np.arange(n, dtype=np.int64)
        ang = 2.0 * np.pi * ((t[:, None] * t[None, :]) % n).astype(np.float64) / n
        tab = np.stack([np.cos(ang), -np.sin(ang)], 1)
        tab = tab.reshape(n // 128, 128, 2, n // kw, kw).transpose(3, 1, 0, 2, 4)
        return np.ascontiguousarray(tab).reshape(-1).astype(np.float32)
    flat[0: L * 2 * L] = dft(L, cf.kwL)
    flat[cf.coff["dftC"]: cf.coff["dftC"] + LC * 2 * LC] = dft(LC, cf.kwC)
    c = np.arange(CG, dtype=np.int64)
    ang = 2.0 * np.pi * ((c[:, None] * c[None, :]) % CG).astype(np.float64) / CG
    g = np.concatenate([np.cos(ang), np.sin(ang)], 1)
    g = g.reshape(cf.CGC, 128, 2 * CG).transpose(1, 0, 2)
    flat[cf.coff["dftG"]: cf.coff["dftG"] + CG * 2 * CG] = np.ascontiguousarray(g).reshape(-1)
    return flat


def rope_tables(cf, s):
    t = np.arange(cf.TL) + s * cf.TL
    row = (t // 64).astype(np.float64)
    col = (t % 64).astype(np.float64)
    d = np.arange(128)
    i = (d % 64) % 32
    inv = 10000.0 ** (-(2.0 * i) / 64.0)
    pos = np.where((d < 64)[:, None], row[None, :], col[None, :])
    ang = pos * inv[:, None]
    sgn = np.where((d % 64) < 32, -1.0, 1.0)[:, None]
    return np.concatenate([np.cos(ang), np.sin(ang) * sgn], 1).astype(np.float32)


def perm_T():
    m = np.arange(128)
    src = (m // 64) * 64 + ((m % 64) + 32) % 64
    PT = np.zeros((128, 128), np.float32)
    PT[src, m] = 1.0
    return PT


def bias_tiles(rpb_h):
    cols = np.arange(64)
    cs = np.clip(cols - 8, 0, 48)
    col_in = (cols[None, :] >= cs[:, None]) & (cols[None, :] < cs[:, None] + 16)
    cidx = np.clip(cols[None, :] - cols[:, None] + 15, 0, 30)

    def B(dl):
        if abs(dl) > 7:
            return np.full((64, 64), NEG, np.float32)
        b = rpb_h[dl + 7][cidx]
        b = np.where(col_in, b, NEG)
        return b.T.astype(np.float32)

    def U(d0, mask=()):
        t = np.zeros((128, 128), np.float32)
        for i in range(2):
            for j in range(2):
                blk = B(d0 + i - j)
                if (i, j) in mask:
                    blk = np.full((64, 64), NEG, np.float32)
                t[i * 64:(i + 1) * 64, j * 64:(j + 1) * 64] = blk
        return t
    tl = [U(d0) for d0 in (-6, -4, -2, 0, 2, 4, 6)]
    tl.append(U(-4, mask=((0, 1),)))
    tl.append(U(4, mask=((0, 0), (1, 0), (1, 1))))
    return np.concatenate(tl, 1)


def host_inputs(cf, inp):
    D, DC, DEPTH = cf.D, cf.DC, cf.DEPTH
    f = lambda k: np.asarray(inp[k], np.float32)
    x, c, ctx, c_ctx = f("x"), f("c"), f("ctx"), f("c_ctx")
    w_ada, b_ada = f("w_ada"), f("b_ada")
    wfl = []
    for l in range(DEPTH):
        flat = np.empty(cf.EL, np.float32)
        for nm in WNAMES:
            K, N = cf.wshape[nm]
            flat[cf.woff[nm]: cf.woff[nm] + K * N] = tile_w(f(nm)[l])
        wfl.append(flat.reshape(-1, 2048))
    cv = np.concatenate([c, c_ctx[None, :]], 0)
    cvT = np.ascontiguousarray(cv.T.reshape(DC, 128, 5).transpose(1, 0, 2)).reshape(128, DC * 5)
    PT = perm_T()
    m = {}
    xin = np.empty((8, D, cf.T), np.float32)
    for vc in range(8):
        b, s = vc // 2, vc % 2
        xin[vc, :, :cf.TL] = x[b, s * cf.TL:(s + 1) * cf.TL, :].T
        xin[vc, :, cf.TL:] = ctx[b, s * cf.TC:(s + 1) * cf.TC, :].T
    m["xin"] = xin.reshape(8 * D, cf.T)
    for l in range(DEPTH):
        m["w%d" % l] = wfl[l]
        m["wada%d" % l] = np.ascontiguousarray(w_ada[l].reshape(DC, 128, cf.J, 128).transpose(2, 1, 0, 3)).reshape(cf.J * 128, DC * 128)
    m["cst"] = host_consts(cf).reshape(-1, 2048)
    vec = np.zeros((128, cf.NV), np.float32)

    def put(nm, arr):
        arr = np.asarray(arr, np.float32)
        vec[:, cf.vo[nm]: cf.vo[nm] + arr.shape[1]] = arr
    put("norm_g", pvec(f("norm_g").reshape(-1)))
    put("final_g", pvec(f("final_g")))
    put("conv_w", pvec(f("conv_w").reshape(DEPTH, 4, cf.NB, 128).transpose(0, 2, 1, 3).reshape(-1)))
    put("conv_b", pvec(f("conv_b").reshape(-1)))
    put("ba", pvec(f("lru_ba").reshape(-1)))
    put("bi", pvec(f("lru_bi").reshape(-1)))
    put("lam", pvec(f("lru_lam").reshape(-1)))
    put("permT", PT)
    put("bada", pvec(b_ada.reshape(-1)))
    put("cv", cvT)
    m["vec"] = vec
    g = np.stack([f("lru_wa"), f("lru_wi")], 3)
    m["lruw"] = np.ascontiguousarray(g.transpose(4, 0, 1, 2, 3, 5)).reshape(128, -1)
    m["rope"] = np.concatenate([rope_tables(cf, 0), rope_tables(cf, 1)], 1)
    rpb = f("na_rpb")
    nab = np.stack([np.stack([bias_tiles(rpb[l, hh]) for hh in range(cf.H)], 0) for l in range(DEPTH)], 0)
    m["nab"] = np.ascontiguousarray(nab).reshape(DEPTH * cf.H * 128, 9 * 128)
    return [m]


class Buf:
    __slots__ = ("w", "rd", "excl")

    def __init__(self, excl=False):
        self.w = None
        self.rd = {}
        self.excl = excl


class Rec:
    def __init__(self):
        self.calls = []

    def __getattr__(self, name):
        def f(*a, **k):
            self.calls.append((name, a, k))
            return self
        return f


def replay(e, calls):
    last = None
    for name, a, k in calls:
        last = getattr(e, name)(*a, **k)
    return last


class Prog:
    ENG = ("pe", "act", "dve", "pool", "sp")
    NDS = 6
    NCS = 4

    def __init__(self, nc, es):
        self.nc = nc
        self.q = {e: [] for e in self.ENG}
        self.sem = {e: es.enter_context(nc.semaphore("s_" + e)) for e in self.ENG}
        self.cnt = {e: 0 for e in self.ENG}
        self.seen = {e: {} for e in self.ENG}
        self.dsem = {}
        for qn in ("sp", "pool"):
            self.dsem[qn] = [[es.enter_context(nc.semaphore("d_%s%d" % (qn, i))), 0] for i in range(self.NDS)]
        self.dk = {"sp": 0, "pool": 0}
        self.csem = [[es.enter_context(nc.semaphore("c_%d" % i)), 0] for i in range(self.NCS)]
        self.ck = 0

    def _wait(self, eng, ev):
        if ev is None:
            return
        sem, val = ev
        k = id(sem)
        if self.seen[eng].get(k, 0) >= val:
            return
        self.seen[eng][k] = val
        self.q[eng].append(lambda e, s=sem, v=val: e.wait_ge(s, v))

    def _deps(self, eng, r, w):
        for b in r:
            self._wait(eng, b.w)
        for b in w:
            self._wait(eng, b.w)
            for ev in b.rd.values():
                self._wait(eng, ev)

    def _mark(self, ev, r, w):
        for b in w:
            b.w = ev
            b.rd = {}
        for b in r:
            if b.w is not ev:
                b.rd[id(ev[0])] = ev

    def op(self, eng, fn, r=(), w=()):
        if any(b.excl for b in r):
            w = list(w) + [b for b in r if b.excl]
            r = [b for b in r if not b.excl]
        self._deps(eng, r, w)
        self.cnt[eng] += 1
        s = self.sem[eng]
        ev = (s, self.cnt[eng])
        rec = Rec()
        fn(rec)
        self.q[eng].append(lambda e, calls=rec.calls, s=s: replay(e, calls).then_inc(s, 1))
        self._mark(ev, r, w)
        return ev

    def dma(self, qn, out, in_, r=(), w=(), fn=None):
        self._deps(qn, r, w)
        slot = self.dsem[qn][self.dk[qn] % self.NDS]
        self.dk[qn] += 1
        self._wait(qn, (slot[0], slot[1]))
        slot[1] += 16
        ev = (slot[0], slot[1])
        if fn is None:
            self.q[qn].append(lambda e, o=out, i=in_, s=slot[0]: e.dma_start(out=o, in_=i).then_inc(s, 16))
        else:
            self.q[qn].append(lambda e, f=fn, s=slot[0]: f(e).then_inc(s, 16))
        self._mark(ev, r, w)
        return ev

    def coll(self, groups, in_ap, out_ap, r=(), w=()):
        self._deps("pool", r, w)
        slot = self.csem[self.ck % self.NCS]
        self.ck += 1
        self._wait("pool", (slot[0], slot[1]))
        slot[1] += 1
        ev = (slot[0], slot[1])
        self.q["pool"].append(lambda e, i=in_ap, o=out_ap, s=slot[0], g=groups: e.collective_compute(
            "AllGather", ALU.bypass, replica_groups=g, ins=[i], outs=[o]).then_inc(s))
        self._mark(ev, r, w)
        return ev

    def pid(self, e, kind):
        if kind not in self._pidcache:
            v = e.partition_id()
            self._pidcache[kind] = (v % 2) if kind == "s" else (v // 2)
        return self._pidcache[kind]

    def all_events(self):
        evs = [(self.sem[e], self.cnt[e]) for e in self.ENG if self.cnt[e] > 0]
        for qn in self.dsem:
            for s, c in self.dsem[qn]:
                if c > 0:
                    evs.append((s, c))
        for s, c in self.csem:
            if c > 0:
                evs.append((s, c))
        return evs

    def barrier(self):
        evs = self.all_events()
        for e in self.ENG:
            for ev in evs:
                self._wait(e, ev)

    def flush(self):
        with self.nc.Block() as block:
            for name, deco in (("pe", block.tensor), ("act", block.scalar), ("dve", block.vector),
                               ("pool", block.gpsimd), ("sp", block.sync)):
                ops = self.q[name]
                self.q[name] = []

                def run(e, ops=ops):
                    self._pidcache = {}
                    for f in ops:
                        f(e)
                deco(run)


def build(cf):
    nc = bass.Bass("TRN2", target_bir_lowering=False)
    D, DC, T, TL, TC, L, LC, LT = cf.D, cf.DC, cf.T, cf.TL, cf.TC, cf.L, cf.LC, cf.LT
    DEPTH, FC, J = cf.DEPTH, cf.FC, cf.J
    FH = FC // 2
    ALL8 = [list(range(8))]
    PAIRS = [[0, 1], [2, 3], [4, 5], [6, 7]]

    def din(name, shape):
        return nc.dram_tensor(name, list(shape), F32, kind="ExternalInput").ap()
    xin = din("xin", [8 * D, T])
    w_in_d = [din("w%d" % l, [cf.EL // 2048, 2048]) for l in range(DEPTH)]
    wada_d = [din("wada%d" % l, [J * 128, DC * 128]) for l in range(DEPTH)]
    cst_d = din("cst", [cf.EC // 2048, 2048])
    vec_d = din("vec", [128, cf.NV])
    lruw_d = din("lruw", [128, DEPTH * 2 * cf.NBh * 2 * 128])
    rope_d = din("rope", [128, 4 * TL])
    nab_d = din("nab", [DEPTH * cf.Hh * 128, 9 * 128])
    out_d = nc.dram_tensor("out", [8 * D, TL], F32, kind="ExternalOutput").ap()

    def dint(name, shape, dt):
        return nc.dram_tensor(name, list(shape), dt)
    xd_all = dint("xd", [8 * D, T], F32).ap()

    wfull = [{nm: dint("wf%d_%s" % (l, nm), [cf.wshape[nm][0] * cf.wshape[nm][1] // 2048, 2048], BF16) for nm in WNAMES} for l in range(DEPTH)]
    cfull = dint("cfull", [cf.EC // 2048, 2048], BF16)
    mixl = dint("mixl", [8 * cf.MH * 128, T], BF16)
    gat_all = dint("gat", [8 * 3 * DC * 128, T], BF16).ap()
    yhd = dint("yhd", [8 * cf.YH * 128, T], BF16)
    dbg = {}
    if cf.dbg:
        for nm, shp in (("d_x1", [D, T]), ("d_mix", [cf.MH * 128, T]), ("d_y", [cf.YH * 128, T]),
                        ("d_x2", [D, T]), ("d_mod", [128, 18 * DC])):
            dbg[nm] = nc.dram_tensor(nm, shp, F32, kind="ExternalOutput").ap()

    def wtile(l, nm, n, lo=0, hi=None):
        K, N = cf.wshape[nm]
        sz = K * 128
        off = n * sz
        ap = wfull[l][nm].ap().rearrange("r e -> (r e)")[off: off + sz].rearrange("(p f) -> p f", p=128)
        return ap if hi is None else ap[:, lo:hi]

    def ctile(nm, off, sz, p=128):
        o = cf.coff[nm] + off
        return cfull.ap().rearrange("r e -> (r e)")[o: o + sz].rearrange("(p f) -> p f", p=p)

    with ExitStack() as es:
        P = Prog(nc, es)

        sbn = [0]

        def SB(st, name, shape, dt=F32):
            sbn[0] += 1
            return st.enter_context(nc.sbuf_tensor("%s_s%d" % (name, sbn[0]), list(shape), dt))
        banks = [es.enter_context(nc.psum_tensor("ps%d" % i, [128, 512], F32)) for i in range(8)]
        bankB = [Buf(excl=True) for _ in range(8)]
        bk = [0]

        def bank():
            i = bk[0] % 8
            bk[0] += 1
            return banks[i], bankB[i]
        vec = SB(es, "vec", [128, cf.NV])
        onesf = SB(es, "onesf", [128, 128])
        onesb = SB(es, "onesb", [128, 128], BF16)
        identb = SB(es, "identb", [128, 128], BF16)
        permb = SB(es, "permb", [128, 128], BF16)
        mt = SB(es, "mt", [128, 2, 3, 3, DC])
        nsp8 = SB(es, "nsp8", [128, DEPTH * 2 * cf.NBh])
        cB, mtB, xBall = Buf(), Buf(), Buf()
        xBv = [[[Buf() for _ in cf.tiles] for _ in range(DC)] for _ in range(8)]
        cur = {'vc': 0}
        XD = lambda: xd_all[cur['vc'] * D:(cur['vc'] + 1) * D, :]
        GAT = lambda: gat_all[cur['vc'] * 3 * DC * 128:(cur['vc'] + 1) * 3 * DC * 128, :]
        mlall = [SB(es, 'mlall%d' % l, [128, 5, J]) for l in range(DEPTH)]
        mlB = Buf()
        vo = cf.vo

        def V(nm, a, n=1):
            return vec[:, vo[nm] + a: vo[nm] + a + n]

        with ExitStack() as ph:
            tmpf = SB(ph, "tmpf", [128, 128])
            e_ = SB(ph, "e_", [128, DEPTH * 2 * cf.NBh])
            p_ = SB(ph, "p_", [128, DEPTH * 2 * cf.NBh])
            t_ = SB(ph, "t_", [128, DEPTH * 2 * cf.NBh])
            P.dma("sp", vec[:], vec_d, w=[cB])
            P.op("pool", lambda e: e.memset(onesf[:], 1.0), w=[cB])
            P.op("pool", lambda e: e.memset(onesb[:], 1.0), w=[cB])
            P.op("pool", lambda e: e.affine_select(out=tmpf[:], in_=onesf[:], pattern=[[-1, 128]], compare_op=ALU.is_equal,
                                                   fill=0.0, base=0, channel_multiplier=1), r=[cB], w=[cB])
            P.op("dve", lambda e: e.tensor_copy(out=identb[:], in_=tmpf[:]), r=[cB], w=[cB])
            P.op("dve", lambda e: e.tensor_copy(out=permb[:], in_=V("permT", 0, 128)), r=[cB], w=[cB])
            nl = DEPTH * 2 * cf.NBh
            P.op("act", lambda e: e.activation(out=e_[:], in_=V("lam", 0, nl), func=AF.Exp, scale=-1.0), r=[cB], w=[cB])
            P.op("dve", lambda e: e.tensor_scalar(out=p_[:], in0=e_[:], scalar1=-0.2, scalar2=0.25, op0=ALU.mult, op1=ALU.add), r=[cB], w=[cB])
            for cst in (1.0 / 3.0, 0.5, 1.0):
                P.op("dve", lambda e: e.tensor_tensor(out=t_[:], in0=e_[:], in1=p_[:], op=ALU.mult), r=[cB], w=[cB])
                P.op("dve", lambda e, c=cst: e.tensor_scalar(out=p_[:], in0=t_[:], scalar1=-1.0, scalar2=c, op0=ALU.mult, op1=ALU.add), r=[cB], w=[cB])
            P.op("dve", lambda e: e.tensor_tensor(out=t_[:], in0=e_[:], in1=p_[:], op=ALU.mult), r=[cB], w=[cB])
            P.op("dve", lambda e: e.tensor_scalar(out=nsp8[:], in0=t_[:], scalar1=-8.0, scalar2=None, op0=ALU.mult), r=[cB], w=[cB])
            for r0 in range(0, 8 * D, 512):
                P.dma("sp", xd_all[r0:r0 + 512, :], xin[r0:r0 + 512, :], w=[xBall])
            wB = [Buf() for _ in range(DEPTH)]
            cstB = Buf()

            def cast(src_d, so, full, E, B_):
                R = E // 2048
                r0 = 0
                while r0 < R:
                    r1 = min(R, r0 + 1024)
                    P.dma("pool", full.ap()[r0:r1, :], src_d[so + r0:so + r1, :], w=[B_])
                    r0 = r1
            cast(cst_d, 0, cfull, cf.EC, cstB)
            for l in range(DEPTH):
                for nm in WNAMES:
                    cast(w_in_d[l], cf.woff[nm] // 2048, wfull[l][nm], cf.wshape[nm][0] * cf.wshape[nm][1], wB[l])
            P.barrier()
            P.flush()

        with ExitStack() as ph:
            sc = SB(ph, "sc", [128, DC, 8])
            wt = [SB(ph, "wt%d" % i, [128, DC * 128]) for i in range(2)]
            wtB = [Buf(), Buf()]
            scB = Buf()
            P.op("pool", lambda e: e.memset(sc[:], 0.0), w=[scB])
            P.op("act", lambda e: e.activation(out=sc[:, :, 0:5], in_=V("cv", 0, DC * 5).rearrange("p (c o) -> p c o", o=5), func=AF.Silu), r=[cB, scB], w=[scB])
            for l in range(DEPTH):
                for j in range(J):
                    k = (l * J + j) % 2
                    P.dma("sp", wt[k][:], wada_d[l][j * 128:(j + 1) * 128, :], w=[wtB[k]])
                    ps, psB = bank()

                    def mm(e, k=k, ps=ps):
                        for kc in range(DC):
                            last = e.matmul(ps[:, 0:8], wt[k][:, kc * 128:(kc + 1) * 128], sc[:, kc, :],
                                            start=(kc == 0), stop=(kc == DC - 1))
                        return last
                    P.op("pe", mm, r=[wtB[k], scB], w=[psB])
                    P.op("dve", lambda e, ps=ps, l=l, j=j: e.tensor_scalar(out=mlall[l][:, :, j], in0=ps[:, 0:5], scalar1=V("bada", l * J + j),
                                                                            scalar2=None, op0=ALU.add), r=[psB, cB], w=[mlB])
            P.barrier()
            P.flush()

        def load_mods(l):
            bb = cur['vc'] // 2
            for ci, row in enumerate((bb, 4)):
                mv = mlall[l][:, row, :]
                for i in range(3):
                    sh = mv[:, (3 * i) * DC:(3 * i + 1) * DC]
                    scl = mv[:, (3 * i + 1) * DC:(3 * i + 2) * DC]
                    gt = mv[:, (3 * i + 2) * DC:(3 * i + 3) * DC]
                    g = V("norm_g", (l * 3 + i) * DC, DC)
                    P.op("dve", lambda e, ci=ci, i=i, scl=scl, g=g: e.scalar_tensor_tensor(out=mt[:, ci, 0, i, :], in0=scl, scalar=1.0, in1=g,
                                                                                           op0=ALU.add, op1=ALU.mult), r=[mlB, cB], w=[mtB])
                    P.op("dve", lambda e, ci=ci, i=i, sh=sh: e.tensor_copy(out=mt[:, ci, 1, i, :], in_=sh), r=[mlB], w=[mtB])
                    P.op("dve", lambda e, ci=ci, i=i, gt=gt: e.tensor_scalar(out=mt[:, ci, 2, i, :], in0=gt, scalar1=(1.0 if i == 1 else 0.5),
                                                                             scalar2=None, op0=ALU.mult), r=[mlB], w=[mtB])

        def xr(c, ti):
            return [xBv[cur['vc']][c][ti], xBall]

        def sumsq_rstd(ph, xt, tn, xtB, rstd, rB, scratch):
            sq, sqB = scratch
            ps, psB = bank()
            for c in range(DC):
                k = c % 2
                P.op("act", lambda e, c=c, k=k: e.activation(out=sq[:, k, :tn], in_=xt[:, c, :tn], func=AF.Square), r=[xtB], w=[sqB[k]])
                P.op("pe", lambda e, c=c, k=k, ps=ps: e.matmul(ps[:, :tn], onesf[:], sq[:, k, :tn], start=(c == 0), stop=(c == DC - 1)),
                     r=[sqB[k], cB], w=[psB])
            P.op("act", lambda e, ps=ps: e.activation(out=rstd[:, :tn], in_=ps[:, :tn], func=AF.Sqrt, scale=1.0 / D, bias=EPS), r=[psB], w=[rB])
            P.op("dve", lambda e: e.reciprocal(out=rstd[:, :tn], in_=rstd[:, :tn]), r=[rB], w=[rB])

        def norm_mod(i, h, hB):
            with ExitStack() as ph:
                xt = SB(ph, "xt", [128, DC, 512])
                sq = SB(ph, "sq", [128, 2, 512])
                rstd = SB(ph, "rstd", [128, 512])
                tmp = SB(ph, "tmp", [128, 2, 512])
                xtB, rB = Buf(), Buf()
                sqB, tmpB = [Buf(), Buf()], [Buf(), Buf()]
                for ti, (t0, tn) in enumerate(cf.tiles):
                    ci = 1 if t0 >= TL else 0
                    P.dma("sp", xt[:, :, :tn], XD().rearrange("(c p) t -> p c t", p=128)[:, :, t0:t0 + tn],
                          r=[b for c in range(DC) for b in xr(c, ti)], w=[xtB])
                    sumsq_rstd(ph, xt, tn, xtB, rstd, rB, (sq, sqB))
                    for c in range(DC):
                        k = c % 2
                        P.op("dve", lambda e, c=c, k=k, ci=ci: e.scalar_tensor_tensor(out=tmp[:, k, :tn], in0=xt[:, c, :tn], scalar=mt[:, ci, 0, i, c:c + 1],
                                                                                      in1=rstd[:, :tn], op0=ALU.mult, op1=ALU.mult),
                             r=[xtB, rB, mtB], w=[tmpB[k]])
                        P.op("act", lambda e, c=c, k=k, ci=ci, t0=t0: e.activation(out=h[:, c, t0:t0 + tn], in_=tmp[:, k, :tn], func=AF.Identity,
                                                                                   bias=mt[:, ci, 1, i, c:c + 1]), r=[tmpB[k], mtB], w=[hB])
                P.barrier()
                P.flush()

        def x_update(st, i, c, ti, t0, tn, ps, psB, xc, xcB, k):
            ci = 1 if t0 >= TL else 0
            P.dma("sp", xc[:, k, :tn], XD()[c * 128:(c + 1) * 128, t0:t0 + tn], r=xr(c, ti), w=[xcB[k]])
            P.op("dve", lambda e: e.scalar_tensor_tensor(out=xc[:, k, :tn], in0=ps[:, :tn], scalar=mt[:, ci, 2, i, c:c + 1], in1=xc[:, k, :tn],
                                                         op0=ALU.mult, op1=ALU.add), r=[psB, xcB[k], mtB], w=[xcB[k]])
            P.dma("pool", XD()[c * 128:(c + 1) * 128, t0:t0 + tn], xc[:, k, :tn], r=[xcB[k]], w=[xBv[cur['vc']][c][ti]])

        def ffn(l, i, n13, n2):
            with ExitStack() as ph:
                h = SB(ph, "h", [128, DC, T], BF16)
                hB = Buf()
                norm_mod(i, h, hB)
                G = SB(ph, "G", [128, FH, T], BF16)
                GB = Buf()
                wa = [SB(ph, "wa%d" % k, [128, DC * 128], BF16) for k in range(2)]
                wb = [SB(ph, "wb%d" % k, [128, DC * 128], BF16) for k in range(2)]
                w2 = [SB(ph, "w2%d" % k, [128, FH * 128], BF16) for k in range(2)]
                sa = SB(ph, "sa", [128, 2, 512])
                xc = SB(ph, "xc", [128, 2, 512])
                waB, wbB, w2B, saB, xcB = ([Buf(), Buf()] for _ in range(5))
                cnt = 0
                for half in range(2):
                    for jj in range(FH):
                        j = half * FH + jj
                        k = jj % 2
                        P.dma("sp", wa[k][:], wtile(l, n13, j), r=[wB[l]], w=[waB[k]])
                        P.dma("sp", wb[k][:], wtile(l, n13, FC + j), r=[wB[l]], w=[wbB[k]])
                        for ti, (t0, tn) in enumerate(cf.tiles):
                            pa, paB = bank()
                            pb, pbB = bank()

                            def mm(e, k=k, t0=t0, tn=tn, pa=pa, pb=pb):
                                for kc in range(DC):
                                    e.matmul(pa[:, :tn], wa[k][:, kc * 128:(kc + 1) * 128], h[:, kc, t0:t0 + tn], start=(kc == 0), stop=(kc == DC - 1))
                                for kc in range(DC):
                                    last = e.matmul(pb[:, :tn], wb[k][:, kc * 128:(kc + 1) * 128], h[:, kc, t0:t0 + tn], start=(kc == 0), stop=(kc == DC - 1))
                                return last
                            P.op("pe", mm, r=[waB[k], wbB[k], hB], w=[paB, pbB])
                            s_ = cnt % 2
                            cnt += 1
                            P.op("act", lambda e, s_=s_, tn=tn, pa=pa: e.activation(out=sa[:, s_, :tn], in_=pa[:, :tn], func=AF.Silu), r=[paB], w=[saB[s_]])
                            P.op("dve", lambda e, s_=s_, tn=tn, t0=t0, jj=jj, pb=pb: e.tensor_tensor(out=G[:, jj, t0:t0 + tn], in0=sa[:, s_, :tn], in1=pb[:, :tn], op=ALU.mult),
                                 r=[saB[s_], pbB], w=[GB])
                    for c in range(DC):
                        k = c % 2
                        P.dma("sp", w2[k][:], wtile(l, n2, c, half * FH * 128, (half + 1) * FH * 128), r=[wB[l]], w=[w2B[k]])
                        for ti, (t0, tn) in enumerate(cf.tiles):
                            ps, psB = bank()

                            def mm2(e, k=k, t0=t0, tn=tn, ps=ps):
                                for kk in range(FH):
                                    last = e.matmul(ps[:, :tn], w2[k][:, kk * 128:(kk + 1) * 128], G[:, kk, t0:t0 + tn], start=(kk == 0), stop=(kk == FH - 1))
                                return last
                            P.op("pe", mm2, r=[w2B[k], GB], w=[psB])
                            s_ = cnt % 2
                            cnt += 1
                            x_update(ph, i, c, ti, t0, tn, ps, psB, xc, xcB, s_)
                P.barrier()
                P.flush()

        def mixdst(n):
            NB, H = cf.NB, cf.H
            if n < NB:
                return n // cf.NBh, n % cf.NBh, None
            if n < NB + 3 * H:
                kind = (n - NB) // H
                hq = (n - NB) % H
                return hq // cf.Hh, cf.NBh + kind * cf.Hh + hq % cf.Hh, kind
            fc = n - NB - 3 * H
            return fc // cf.FCh, cf.NBh + 3 * cf.Hh + fc % cf.FCh, None

        mixBv = [Buf() for _ in range(8)]
        gatBv = [Buf() for _ in range(8)]
        yBv = [Buf() for _ in range(8)]

        def in_proj(l):
            with ExitStack() as ph:
                h = SB(ph, "h", [128, DC, T], BF16)
                hB = Buf()
                norm_mod(1, h, hB)
                w = [SB(ph, "w%d" % k, [128, DC * 128], BF16) for k in range(2)]
                st = [SB(ph, "st%d" % k, [128, T], BF16) for k in range(3)]
                qb = SB(ph, "qb", [128, 2, 512], BF16)
                t1 = SB(ph, "t1", [128, 2, 512])
                t2 = SB(ph, "t2", [128, 2, 512])
                rope = SB(ph, "rope", [128, 2 * TL])
                wBf, stB, qbB, t1B, t2B = [Buf(), Buf()], [Buf() for _ in range(3)], [Buf(), Buf()], [Buf(), Buf()], [Buf(), Buf()]
                rpB = Buf()
                s_ = cur["vc"] % 2
                P.dma("sp", rope[:], rope_d[:, s_ * 2 * TL:(s_ + 1) * 2 * TL], w=[rpB])
                cnt = 0
                for n in range(cf.NIN):
                    k = n % 2
                    sk = n % 3
                    P.dma("sp", w[k][:], wtile(l, "w_in", n), r=[wB[l]], w=[wBf[k]])
                    isg = n >= cf.MIXC
                    kind = None if isg else mixdst(n)[2]
                    for ti, (t0, tn) in enumerate(cf.tiles):
                        ps, psB = bank()

                        def mm(e, k=k, t0=t0, tn=tn, ps=ps):
                            for kc in range(DC):
                                last = e.matmul(ps[:, :tn], w[k][:, kc * 128:(kc + 1) * 128], h[:, kc, t0:t0 + tn], start=(kc == 0), stop=(kc == DC - 1))
                            return last
                        P.op("pe", mm, r=[wBf[k], hB], w=[psB])
                        if isg:
                            P.op("act", lambda e, sk=sk, t0=t0, tn=tn, ps=ps: e.activation(out=st[sk][:, t0:t0 + tn], in_=ps[:, :tn], func=AF.Sigmoid), r=[psB], w=[stB[sk]])
                        elif kind in (0, 1) and t0 < TL:
                            s_ = cnt % 2
                            cnt += 1
                            P.op("act", lambda e, s_=s_, tn=tn, ps=ps: e.activation(out=qb[:, s_, :tn], in_=ps[:, :tn], func=AF.Copy), r=[psB], w=[qbB[s_]])
                            p2, p2B = bank()
                            P.op("pe", lambda e, s_=s_, tn=tn, p2=p2: e.matmul(p2[:, :tn], permb[:], qb[:, s_, :tn], start=True, stop=True), r=[qbB[s_], cB], w=[p2B])
                            P.op("dve", lambda e, s_=s_, tn=tn, t0=t0, ps=ps: e.tensor_tensor(out=t1[:, s_, :tn], in0=ps[:, :tn], in1=rope[:, t0:t0 + tn], op=ALU.mult),
                                 r=[psB, rpB], w=[t1B[s_]])
                            P.op("dve", lambda e, s_=s_, tn=tn, t0=t0, p2=p2: e.tensor_tensor(out=t2[:, s_, :tn], in0=p2[:, :tn], in1=rope[:, TL + t0:TL + t0 + tn], op=ALU.mult),
                                 r=[p2B, rpB], w=[t2B[s_]])
                            P.op("dve", lambda e, s_=s_, tn=tn, t0=t0, sk=sk: e.tensor_tensor(out=st[sk][:, t0:t0 + tn], in0=t1[:, s_, :tn], in1=t2[:, s_, :tn], op=ALU.add),
                                 r=[t1B[s_], t2B[s_]], w=[stB[sk]])
                        else:
                            P.op("act", lambda e, sk=sk, t0=t0, tn=tn, ps=ps: e.activation(out=st[sk][:, t0:t0 + tn], in_=ps[:, :tn], func=AF.Copy), r=[psB], w=[stB[sk]])
                    if isg:
                        g0 = (n - cf.MIXC) * 128
                        P.dma("pool", GAT()[g0:g0 + 128, :], st[sk][:], r=[stB[sk]], w=[gatBv[cur["vc"]]])
                    else:
                        hf, idx, _ = mixdst(n)
                        r0 = (cur["vc"] * cf.MH + idx) * 128
                        P.dma("pool", mixl.ap()[r0:r0 + 128, :], st[sk][:], r=[stB[sk]], w=[mixBv[cur["vc"]]])
                if cf.dbg and l == 0 and cur['vc'] == cf.dbgvc:
                    P.dma("pool", dbg["d_mix"], mixl.ap()[cur["vc"] * cf.MH * 128:(cur["vc"] + 1) * cf.MH * 128, :], r=[mixBv[cur["vc"]]])
                P.barrier()
                P.flush()

        mixl4 = mixl.ap().rearrange("(v m p) t -> p v m t", v=8, p=128)
        yhd4 = yhd.ap().rearrange("(v m p) t -> p v m t", v=8, p=128)

        def load_mix(dst, idx, B_):
            bb = cur['b']
            for r in range(2):
                P.dma("sp", dst[:, r * TL:(r + 1) * TL], mixl4[:, 2 * bb + r, idx, 0:TL], r=[mixBv[2 * bb + r]], w=[B_])
                P.dma("sp", dst[:, L + r * TC:L + (r + 1) * TC], mixl4[:, 2 * bb + r, idx, TL:T], r=[mixBv[2 * bb + r]], w=[B_])

        def store_y(src, idx, B_):
            bb = cur['b']
            for j in range(2):
                P.dma("sp", yhd4[:, 2 * bb + j, idx, 0:TL], src[:, j * TL:(j + 1) * TL], r=[B_], w=[yBv[2 * bb + j]])
                P.dma("sp", yhd4[:, 2 * bb + j, idx, TL:T], src[:, L + j * TC:L + (j + 1) * TC], r=[B_], w=[yBv[2 * bb + j]])

        def lru_phase(l):
            with ExitStack() as ph:
                u = SB(ph, "u", [128, LT], BF16)
                vf = SB(ph, "vf", [128, LT])
                vb = SB(ph, "vb", [128, LT], BF16)
                rt = SB(ph, "rt", [128, LT])
                it = SB(ph, "it", [128, LT])
                at = SB(ph, "at", [128, LT])
                tm = SB(ph, "tm", [128, LT])
                hh = [SB(ph, "hh%d" % d, [128, LT]) for d in range(2)]
                yo = SB(ph, "yo", [128, LT], BF16)
                lw = SB(ph, "lw", [128, 2 * 2 * 128])
                lwb = SB(ph, "lwb", [128, 2 * 2 * 128], BF16)
                uB, vfB, vbB, rB, iB, aB, tB, yoB, lwB = (Buf() for _ in range(9))
                hB2 = [Buf(), Buf()]
                segs = ((0, L), (L, LT))
                for blk in range(cf.NBh):
                    load_mix(u, blk, uB)
                    cw = lambda j_, blk=blk: V("conv_w", (l * cf.NBh + blk) * 4 + j_)
                    P.op("dve", lambda e, cw=cw, blk=blk: e.tensor_scalar(out=vf[:], in0=u[:], scalar1=cw(2), scalar2=V("conv_b", l * cf.NBh + blk),
                                                                          op0=ALU.mult, op1=ALU.add), r=[uB, cB], w=[vfB])
                    for j_, off in ((0, -2), (1, -1), (3, 1)):
                        for s0, s1 in segs:
                            a, b = s0 + max(0, -off), s1 - max(0, off)
                            P.op("dve", lambda e, cw=cw, j_=j_, a=a, b=b, off=off: e.scalar_tensor_tensor(out=vf[:, a:b], in0=u[:, a + off:b + off], scalar=cw(j_),
                                                                                                          in1=vf[:, a:b], op0=ALU.mult, op1=ALU.add), r=[uB, cB, vfB], w=[vfB])
                    P.op("act", lambda e: e.activation(out=vb[:], in_=vf[:], func=AF.Copy), r=[vfB], w=[vbB])
                    for d in range(2):
                        wi0 = ((l * 2 + d) * cf.NBh + blk) * 2 * 128
                        P.dma("sp", lw[:, 0:256], lruw_d[:, wi0:wi0 + 256], w=[lwB])
                        P.op("dve", lambda e: e.tensor_copy(out=lwb[:, 0:256], in_=lw[:, 0:256]), r=[lwB], w=[lwB])
                        vi = (l * 2 + d) * cf.NBh + blk
                        t0 = 0
                        while t0 < LT:
                            tn = min(512, LT - t0)
                            pr, prB = bank()
                            pi, piB = bank()
                            P.op("pe", lambda e, t0=t0, tn=tn, pr=pr: e.matmul(pr[:, :tn], lwb[:, 0:128], vb[:, t0:t0 + tn], start=True, stop=True), r=[lwB, vbB], w=[prB])
                            P.op("pe", lambda e, t0=t0, tn=tn, pi=pi: e.matmul(pi[:, :tn], lwb[:, 128:256], vb[:, t0:t0 + tn], start=True, stop=True), r=[lwB, vbB], w=[piB])
                            P.op("act", lambda e, t0=t0, tn=tn, pr=pr, vi=vi: e.activation(out=rt[:, t0:t0 + tn], in_=pr[:, :tn], func=AF.Sigmoid, bias=V("ba", vi)), r=[prB, cB], w=[rB])
                            P.op("act", lambda e, t0=t0, tn=tn, pi=pi, vi=vi: e.activation(out=it[:, t0:t0 + tn], in_=pi[:, :tn], func=AF.Sigmoid, bias=V("bi", vi)), r=[piB, cB], w=[iB])
                            t0 += tn
                        P.op("act", lambda e, vi=vi: e.activation(out=at[:], in_=rt[:], func=AF.Exp, scale=nsp8[:, vi:vi + 1]), r=[rB, cB], w=[aB])
                        P.op("act", lambda e: e.activation(out=tm[:], in_=at[:], func=AF.Square), r=[aB], w=[tB])
                        P.op("act", lambda e: e.activation(out=tm[:], in_=tm[:], func=AF.Sqrt, scale=-1.0, bias=1.0), r=[tB], w=[tB])
                        P.op("dve", lambda e: e.tensor_tensor(out=it[:], in0=it[:], in1=vf[:], op=ALU.mult), r=[iB, vfB], w=[iB])
                        P.op("dve", lambda e: e.tensor_tensor(out=it[:], in0=it[:], in1=tm[:], op=ALU.mult), r=[iB, tB], w=[iB])
                        hd = hh[d]
                        if d == 0:
                            P.op("dve", lambda e, hd=hd: e.tensor_tensor_scan(out=hd[:, L:LT], data0=at[:, L:LT], data1=it[:, L:LT], initial=0.0, op0=ALU.mult, op1=ALU.add),
                                 r=[aB, iB], w=[hB2[d]])
                            P.op("dve", lambda e, hd=hd: e.tensor_tensor_scan(out=hd[:, 0:L], data0=at[:, 0:L], data1=it[:, 0:L], initial=hd[:, LT - 1:LT], op0=ALU.mult, op1=ALU.add),
                                 r=[aB, iB, hB2[d]], w=[hB2[d]])
                        else:
                            P.op("dve", lambda e, hd=hd: e.tensor_tensor_scan(out=hd[:, L:LT][:, ::-1], data0=at[:, L:LT][:, ::-1], data1=it[:, L:LT][:, ::-1], initial=0.0,
                                                                              op0=ALU.mult, op1=ALU.add), r=[aB, iB], w=[hB2[d]])
                            P.op("dve", lambda e, hd=hd: e.tensor_tensor_scan(out=hd[:, 0:L][:, ::-1], data0=at[:, 0:L][:, ::-1], data1=it[:, 0:L][:, ::-1], initial=hd[:, L:L + 1],
                                                                              op0=ALU.mult, op1=ALU.add), r=[aB, iB, hB2[d]], w=[hB2[d]])
                    P.op("dve", lambda e: e.tensor_tensor(out=yo[:], in0=hh[0][:], in1=hh[1][:], op=ALU.add), r=hB2, w=[yoB])
                    store_y(yo, blk, yoB)
                P.barrier()
                P.flush()

        def na_phase(l):
            scale = 128.0 ** -0.5
            NCK = LT // 128
            with ExitStack() as ph:
                qs = [SB(ph, "q%d" % k, [128, LT], BF16) for k in range(2)]
                ks = [SB(ph, "k%d" % k, [128, LT], BF16) for k in range(2)]
                vs = [SB(ph, "v%d" % k, [128, LT], BF16) for k in range(2)]
                bs = [SB(ph, "b%d" % k, [128, 9 * 128]) for k in range(2)]
                vt = SB(ph, "vt", [128, NCK, 128], BF16)
                yo = SB(ph, "yo", [128, LT], BF16)
                ssb = [SB(ph, "ssb%d" % k, [128, 5 * 128]) for k in range(2)]
                pt = [SB(ph, "pt%d" % k, [128, 5 * 128 + max(LC, 512 - 640 + 640)], BF16) for k in range(2)]
                rd = [SB(ph, "rd%d" % k, [128, 256]) for k in range(2)]
                qB, kB, vB, bB = ([Buf(), Buf()] for _ in range(4))
                vtB, yoB = Buf(), Buf()
                ssB, ptB, rdB = [Buf(), Buf()], [Buf(), Buf()], [Buf(), Buf()]
                cnt = 0
                NCC = LC // 128
                for hh in range(cf.Hh):
                    k = hh % 2
                    q, kk_, v, bt = qs[k], ks[k], vs[k], bs[k]
                    load_mix(q, cf.NBh + hh, qB[k])
                    load_mix(kk_, cf.NBh + cf.Hh + hh, kB[k])
                    load_mix(v, cf.NBh + 2 * cf.Hh + hh, vB[k])
                    r0 = (l * cf.Hh + hh) * 128
                    P.dma("sp", bt[:], nab_d[r0:r0 + 128, :], w=[bB[k]])
                    c0 = 0
                    while c0 < NCK:
                        n = min(4, NCK - c0)
                        ps, psB = bank()

                        def tr(e, c0=c0, n=n, ps=ps, v=v):
                            for j in range(n):
                                last = e.matmul(ps[:, j * 128:(j + 1) * 128], v[:, (c0 + j) * 128:(c0 + j + 1) * 128], identb[:], start=True, stop=True)
                            return last
                        P.op("pe", tr, r=[vB[k], cB], w=[psB])
                        P.op("act", lambda e, c0=c0, n=n, ps=ps: e.activation(out=vt[:, c0:c0 + n, :].rearrange("p c d -> p (c d)"), in_=ps[:, :n * 128], func=AF.Copy),
                             r=[psB], w=[vtB])
                        c0 += n
                    for pr_ in range(cf.ROWS // 2):
                        r = 2 * pr_
                        if r < 4:
                            base, tl = 0, [(2 * c - r + 6) // 2 for c in range(4)]
                        elif r >= cf.ROWS - 4:
                            base = cf.ROWS - 8
                            tl = [(base + 2 * c - r + 6) // 2 for c in range(4)]
                        else:
                            base, tl = r - 4, [7, 2, 3, 4, 8]
                        nch = len(tl)
                        s_ = cnt % 2
                        cnt += 1
                        pA, pAB = bank()
                        pB_, pBB = bank()

                        def smm(e, base=base, nch=nch, r=r, pA=pA, pB_=pB_, q=q, kk_=kk_):
                            for c in range(nch):
                                dst = pA[:, c * 128:(c + 1) * 128] if c < 4 else pB_[:, 0:128]
                                e.matmul(dst, kk_[:, (base + 2 * c) * 64:(base + 2 * c) * 64 + 128], q[:, r * 64:r * 64 + 128], start=True, stop=True)
                            for cc in range(NCC):
                                last = e.matmul(pB_[:, 128 + cc * 128:256 + cc * 128], kk_[:, L + cc * 128:L + (cc + 1) * 128], q[:, r * 64:r * 64 + 128], start=True, stop=True)
                            return last
                        P.op("pe", smm, r=[qB[k], kB[k]], w=[pAB, pBB])
                        for c in range(nch):
                            src = pA[:, c * 128:(c + 1) * 128] if c < 4 else pB_[:, 0:128]
                            P.op("dve", lambda e, c=c, src=src, s_=s_, ti_=tl[c], bt=bt: e.scalar_tensor_tensor(out=ssb[s_][:, c * 128:(c + 1) * 128], in0=src, scalar=scale,
                                                                                                              in1=bt[:, ti_ * 128:(ti_ + 1) * 128], op0=ALU.mult, op1=ALU.add),
                                 r=[pAB if c < 4 else pBB, bB[k]], w=[ssB[s_]])
                        P.op("act", lambda e, s_=s_, nch=nch: e.activation(out=pt[s_][:, 0:nch * 128], in_=ssb[s_][:, 0:nch * 128], func=AF.Exp), r=[ssB[s_]], w=[ptB[s_]])
                        P.op("act", lambda e, s_=s_, pB_=pB_: e.activation(out=pt[s_][:, 640:640 + LC], in_=pB_[:, 128:128 + LC], func=AF.Exp, scale=scale), r=[pBB], w=[ptB[s_]])
                        pO, pOB = bank()
                        pD, pDB = bank()

                        def pv(e, base=base, nch=nch, s_=s_, pO=pO, pD=pD):
                            tot = nch + NCC
                            for m_, lhs in ((pO, None), (pD, onesb)):
                                for c in range(tot):
                                    if c < nch:
                                        rhs = pt[s_][:, c * 128:(c + 1) * 128]
                                        vch = base // 2 + c
                                    else:
                                        rhs = pt[s_][:, 640 + (c - nch) * 128:640 + (c - nch + 1) * 128]
                                        vch = L // 128 + (c - nch)
                                    last = e.matmul(m_[:, 0:128], vt[:, vch, :] if lhs is None else lhs[:], rhs, start=(c == 0), stop=(c == tot - 1))
                            return last
                        P.op("pe", pv, r=[ptB[s_], vtB, cB], w=[pOB, pDB])
                        P.op("dve", lambda e, s_=s_, pD=pD: e.reciprocal(out=rd[s_][:, 0:128], in_=pD[:, 0:128]), r=[pDB], w=[rdB[s_]])
                        P.op("dve", lambda e, s_=s_, pO=pO, r=r: e.tensor_tensor(out=yo[:, r * 64:r * 64 + 128], in0=pO[:, 0:128], in1=rd[s_][:, 0:128], op=ALU.mult),
                             r=[pOB, rdB[s_]], w=[yoB])
                    s_ = cnt % 2
                    cnt += 1
                    pA, pAB = bank()

                    def cmm(e, pA=pA, q=q, kk_=kk_):
                        for cc in range(NCC):
                            last = e.matmul(pA[:, cc * LC:(cc + 1) * LC], kk_[:, L + cc * 128:L + (cc + 1) * 128], q[:, L:LT], start=True, stop=True)
                        return last
                    P.op("pe", cmm, r=[qB[k], kB[k]], w=[pAB])
                    P.op("act", lambda e, s_=s_, pA=pA: e.activation(out=pt[s_][:, 0:NCC * LC], in_=pA[:, 0:NCC * LC], func=AF.Exp, scale=scale), r=[pAB], w=[ptB[s_]])
                    pO, pOB = bank()
                    pD, pDB = bank()

                    def cpv(e, s_=s_, pO=pO, pD=pD):
                        for m_, lhs in ((pO, None), (pD, onesb)):
                            for cc in range(NCC):
                                last = e.matmul(m_[:, 0:LC], vt[:, L // 128 + cc, :] if lhs is None else lhs[:], pt[s_][:, cc * LC:(cc + 1) * LC],
                                                start=(cc == 0), stop=(cc == NCC - 1))
                        return last
                    P.op("pe", cpv, r=[ptB[s_], vtB, cB], w=[pOB, pDB])
                    P.op("dve", lambda e, s_=s_, pD=pD: e.reciprocal(out=rd[s_][:, 0:LC], in_=pD[:, 0:LC]), r=[pDB], w=[rdB[s_]])
                    P.op("dve", lambda e, s_=s_, pO=pO: e.tensor_tensor(out=yo[:, L:LT], in0=pO[:, 0:LC], in1=rd[s_][:, 0:LC], op=ALU.mult), r=[pOB, rdB[s_]], w=[yoB])
                    store_y(yo, cf.NBh + hh, yoB)
                P.barrier()
                P.flush()

        def fft_phase(l):
            CG, CGC = cf.CG, cf.CGC
            with ExitStack() as ph:
                f_ = [SB(ph, "f%d" % c, [128, LT], BF16) for c in range(CGC)]
                yo = [SB(ph, "yo%d" % c, [128, LT], BF16) for c in range(CGC)]
                gd = SB(ph, "gd", [128, CGC, 2 * CG], BF16)
                Z = SB(ph, "Z", [128, L // 128, 2 * CG], BF16)
                tab = [SB(ph, "tab%d" % k, [128, (L // 128) * 2 * cf.kwL], BF16) for k in range(2)]
                fB, yoB = [Buf() for _ in range(CGC)], [Buf() for _ in range(CGC)]
                gdB, ZB = Buf(), Buf()
                tabB = [Buf(), Buf()]
                P.dma("sp", gd[:].rearrange("p c m -> p (c m)"), ctile("dftG", 0, CG * 2 * CG), r=[cstB], w=[gdB])
                tk = 0
                for g in range(cf.NGh):
                    for cc in range(CGC):
                        load_mix(f_[cc], cf.NBh + 3 * cf.Hh + g * CGC + cc, fB[cc])
                    for (n, off, nm, kw) in ((L, 0, "dftL", cf.kwL), (LC, L, "dftC", cf.kwC)):
                        TCH = n // 128
                        for tch in range(TCH):
                            ps, psB = bank()

                            def s1(e, tch=tch, off=off, ps=ps):
                                for cc in range(CGC):
                                    last = e.matmul(ps[:, 0:2 * CG], f_[cc][:, off + tch * 128:off + (tch + 1) * 128], gd[:, cc, :], start=(cc == 0), stop=(cc == CGC - 1))
                                return last
                            P.op("pe", s1, r=fB + [gdB], w=[psB])
                            P.op("act", lambda e, tch=tch, ps=ps: e.activation(out=Z[:, tch, :], in_=ps[:, 0:2 * CG], func=AF.Copy), r=[psB], w=[ZB])
                        tsz = TCH * 2 * kw
                        for kt in range(n // kw):
                            k = tk % 2
                            tk += 1
                            P.dma("sp", tab[k][:, 0:tsz], ctile(nm, kt * 128 * tsz, 128 * tsz), r=[cstB], w=[tabB[k]])
                            tv = tab[k][:, 0:tsz].rearrange("p (c a k) -> p c a k", a=2, k=kw)
                            for mc in range(CGC):
                                ps, psB = bank()

                                def s2(e, mc=mc, ps=ps, tv=tv, TCH=TCH, kw=kw):
                                    for tch in range(TCH):
                                        e.matmul(ps[:, :kw], Z[:, tch, mc * 128:(mc + 1) * 128], tv[:, tch, 0, :], start=(tch == 0), stop=False)
                                        last = e.matmul(ps[:, :kw], Z[:, tch, CG + mc * 128:CG + (mc + 1) * 128], tv[:, tch, 1, :], start=False, stop=(tch == TCH - 1))
                                    return last
                                P.op("pe", s2, r=[ZB, tabB[k]], w=[psB])
                                P.op("act", lambda e, mc=mc, ps=ps, off=off, kt=kt, kw=kw, n=n: e.activation(out=yo[mc][:, off + kt * kw:off + (kt + 1) * kw], in_=ps[:, :kw],
                                                                                                             func=AF.Copy, scale=float((n * CG) ** -0.5)), r=[psB], w=[yoB[mc]])
                    for cc in range(CGC):
                        store_y(yo[cc], cf.NBh + cf.Hh + g * CGC + cc, yoB[cc])
                P.barrier()
                P.flush()

        def merge(l):
            NB, H, FCc = cf.NB, cf.H, cf.FW // 128
            with ExitStack() as ph:
                M = SB(ph, "M", [128, DC, T], BF16)
                MB = Buf()
                with ExitStack() as ph2:
                    Y = SB(ph2, "Y", [128, cf.YH, T], BF16)
                    YB = Buf()
                    P.dma("sp", Y[:], yhd4[:, cur['vc'], :, :], r=[yBv[cur['vc']]], w=[YB])
                    wl = [SB(ph2, "wl%d" % k, [128, (NB + H + FCc) * 128], BF16) for k in range(2)]
                    gt = [SB(ph2, "gt%d" % k, [128, 3, T], BF16) for k in range(2)]
                    t1 = SB(ph2, "t1", [128, 2, 512])
                    t2 = SB(ph2, "t2", [128, 2, 512])
                    wlB, gtB, t1B, t2B = ([Buf(), Buf()] for _ in range(4))
                    ysrc = []
                    for b_ in range(NB):
                        ysrc.append((b_ // cf.NBh) * cf.YH + b_ % cf.NBh)
                    for h_ in range(H):
                        ysrc.append((h_ // cf.Hh) * cf.YH + cf.NBh + h_ % cf.Hh)
                    for c_ in range(FCc):
                        ysrc.append((c_ // cf.FCh) * cf.YH + cf.NBh + cf.Hh + c_ % cf.FCh)
                    grp = ((0, NB), (NB, NB + H), (NB + H, NB + H + FCc))
                    cnt = 0
                    for c in range(DC):
                        k = c % 2
                        P.dma("sp", wl[k][:, 0:NB * 128], wtile(l, "w_out_lru", c), r=[wB[l]], w=[wlB[k]])
                        P.dma("sp", wl[k][:, NB * 128:(NB + H) * 128], wtile(l, "w_out_na", c), r=[wB[l]], w=[wlB[k]])
                        P.dma("sp", wl[k][:, (NB + H) * 128:], wtile(l, "w_out_fft", c), r=[wB[l]], w=[wlB[k]])
                        P.dma("sp", gt[k][:], GAT().rearrange("(g c p) t -> p g c t", g=3, p=128)[:, :, c, :], r=[gatBv[cur["vc"]]], w=[gtB[k]])
                        for ti, (t0, tn) in enumerate(cf.tiles):
                            pss = [bank() for _ in range(3)]

                            def mm(e, k=k, t0=t0, tn=tn, pss=pss):
                                for gi, (a, b) in enumerate(grp):
                                    for kk in range(a, b):
                                        last = e.matmul(pss[gi][0][:, :tn], wl[k][:, kk * 128:(kk + 1) * 128], Y[:, ysrc[kk], t0:t0 + tn], start=(kk == a), stop=(kk == b - 1))
                                return last
                            P.op("pe", mm, r=[wlB[k], YB], w=[p[1] for p in pss])
                            s_ = cnt % 2
                            cnt += 1
                            P.op("dve", lambda e, s_=s_, k=k, t0=t0, tn=tn, p=pss[0][0]: e.tensor_tensor(out=t1[:, s_, :tn], in0=p[:, :tn], in1=gt[k][:, 0, t0:t0 + tn], op=ALU.mult),
                                 r=[pss[0][1], gtB[k]], w=[t1B[s_]])
                            P.op("dve", lambda e, s_=s_, k=k, t0=t0, tn=tn, p=pss[1][0]: e.tensor_tensor(out=t2[:, s_, :tn], in0=p[:, :tn], in1=gt[k][:, 1, t0:t0 + tn], op=ALU.mult),
                                 r=[pss[1][1], gtB[k]], w=[t2B[s_]])
                            P.op("dve", lambda e, s_=s_, tn=tn: e.tensor_tensor(out=t1[:, s_, :tn], in0=t1[:, s_, :tn], in1=t2[:, s_, :tn], op=ALU.add), r=[t1B[s_], t2B[s_]], w=[t1B[s_]])
                            P.op("dve", lambda e, s_=s_, k=k, t0=t0, tn=tn, p=pss[2][0]: e.tensor_tensor(out=t2[:, s_, :tn], in0=p[:, :tn], in1=gt[k][:, 2, t0:t0 + tn], op=ALU.mult),
                                 r=[pss[2][1], gtB[k], t1B[s_]], w=[t2B[s_]])
                            P.op("dve", lambda e, s_=s_, c=c, t0=t0, tn=tn: e.tensor_tensor(out=M[:, c, t0:t0 + tn], in0=t1[:, s_, :tn], in1=t2[:, s_, :tn], op=ALU.add),
                                 r=[t1B[s_], t2B[s_]], w=[MB])
                    P.barrier()
                    P.flush()
                wo = [SB(ph, "wo%d" % k, [128, DC * 128], BF16) for k in range(2)]
                xc = SB(ph, "xc", [128, 2, 512])
                woB, xcB = [Buf(), Buf()], [Buf(), Buf()]
                cnt = 0
                for c in range(DC):
                    k = c % 2
                    P.dma("sp", wo[k][:], wtile(l, "w_o", c), r=[wB[l]], w=[woB[k]])
                    for ti, (t0, tn) in enumerate(cf.tiles):
                        ps, psB = bank()

                        def mm(e, k=k, t0=t0, tn=tn, ps=ps):
                            for kc in range(DC):
                                last = e.matmul(ps[:, :tn], wo[k][:, kc * 128:(kc + 1) * 128], M[:, kc, t0:t0 + tn], start=(kc == 0), stop=(kc == DC - 1))
                            return last
                        P.op("pe", mm, r=[woB[k], MB], w=[psB])
                        s_ = cnt % 2
                        cnt += 1
                        x_update(ph, 1, c, ti, t0, tn, ps, psB, xc, xcB, s_)
                P.barrier()
                P.flush()

        def final_norm():
            with ExitStack() as ph:
                xt = SB(ph, "xt", [128, DC, 512])
                sq = SB(ph, "sq", [128, 2, 512])
                rstd = SB(ph, "rstd", [128, 512])
                xtB, rB = Buf(), Buf()
                sqB = [Buf(), Buf()]
                for ti, (t0, tn) in enumerate(cf.tiles):
                    if t0 >= TL:
                        continue
                    P.dma("sp", xt[:, :, :tn], XD().rearrange("(c p) t -> p c t", p=128)[:, :, t0:t0 + tn],
                          r=[b for c in range(DC) for b in xr(c, ti)], w=[xtB])
                    sumsq_rstd(ph, xt, tn, xtB, rstd, rB, (sq, sqB))
                    for c in range(DC):
                        P.op("dve", lambda e, c=c, tn=tn: e.scalar_tensor_tensor(out=xt[:, c, :tn], in0=xt[:, c, :tn], scalar=V("final_g", c), in1=rstd[:, :tn],
                                                                                 op0=ALU.mult, op1=ALU.mult), r=[xtB, rB, cB], w=[xtB])
                    ev = P.dma("sp", out_d[cur["vc"] * D:(cur["vc"] + 1) * D, :].rearrange("(c p) t -> p c t", p=128)[:, :, t0:t0 + tn], xt[:, :, :tn], r=[xtB])
                    for e_ in P.ENG:
                        P._wait(e_, ev)
                P.barrier()
                P.flush()

        def layers():
            for l in range(DEPTH):
                for vc in range(8):
                    cur['vc'] = vc
                    if cf.stop < 2:
                        return
                    load_mods(l)
                    if cf.stop < 3:
                        return
                    ffn(l, 0, "ffn1_w13", "ffn1_w2")
                    if cf.dbg and l == 0 and vc == cf.dbgvc:
                        P.dma("sp", dbg["d_x1"], XD(), r=[b for c in range(DC) for ti in range(len(cf.tiles)) for b in xr(c, ti)])
                    if cf.stop < 4:
                        continue
                    in_proj(l)
                if cf.stop < 5:
                    return
                for b_ in range(4):
                    cur['b'] = b_
                    lru_phase(l)
                    if cf.stop < 6:
                        continue
                    na_phase(l)
                    if cf.stop < 7:
                        continue
                    fft_phase(l)
                if cf.dbg and l == 0:
                    P.dma("pool", dbg["d_y"], yhd.ap()[cf.dbgvc * cf.YH * 128:(cf.dbgvc + 1) * cf.YH * 128, :], r=[yBv[cf.dbgvc]])
                if cf.stop < 8:
                    return
                for vc in range(8):
                    cur['vc'] = vc
                    load_mods(l)
                    merge(l)
                    if cf.stop < 9:
                        continue
                    ffn(l, 2, "ffn2_w13", "ffn2_w2")
                    if cf.dbg and l == 0 and vc == cf.dbgvc:
                        P.dma("sp", dbg["d_x2"], XD(), r=[b for c in range(DC) for ti in range(len(cf.tiles)) for b in xr(c, ti)])
        layers()
        for vc in range(8):
            cur['vc'] = vc
            final_norm()
    return nc


_CACHE = {}


def run(cf, inputs):
    maps = host_inputs(cf, inputs)
    key = (cf.D, cf.L, cf.LC, cf.NG, cf.DEPTH, cf.dbg, cf.stop)
    if key not in _CACHE:
        _CACHE[key] = build(cf)
    res = run_bass_kernel_spmd(_CACHE[key], maps, core_ids=[0])
    o = res.results[0]["out"].reshape(8, cf.D, cf.TL)
    out = np.zeros((4, cf.L, cf.D), np.float32)
    for vc in range(8):
        b, s = vc // 2, vc % 2
        out[b, s * cf.TL:(s + 1) * cf.TL, :] = o[vc].T
    return out, res


def kernel(**inputs):
    cf = Cfg()
    out, _ = run(cf, inputs)
    return out
```

```python
import numpy as np
import os
import concourse.bass as bass
import concourse.mybir as mybir
from concourse.bass import ds
from concourse.bass_utils import run_bass_kernel_spmd
from contextlib import ExitStack

F32 = mybir.dt.float32
BF16 = mybir.dt.bfloat16
AF = mybir.ActivationFunctionType
ALU = mybir.AluOpType
EPS = 1e-6
NEG = -30000.0
PIECE = 16384 * 1024
WNAMES = ("ffn1_w13", "ffn1_w2", "w_in", "w_out_lru", "w_out_na", "w_out_fft", "w_o", "ffn2_w13", "ffn2_w2")


class Cfg:
    def __init__(s, D=4096, L=2048, LC=256, NG=4, DEPTH=2, dbg=False, stop=99):
        s.D, s.L, s.LC, s.NG, s.DEPTH, s.dbg = D, L, LC, NG, DEPTH, dbg
        s.stop = stop
        s.dbgvc = 3
        s.NCORE = 2
        s.NVC = 8 // s.NCORE
        s.NBAT = 4 // s.NCORE
        s.DC = D // 128
        s.LW = D // 4
        s.NB = s.LW // 128
        s.NAW = D // 2
        s.H = s.NAW // 128
        s.FW = D // 4
        s.CG = s.FW // NG
        s.CGC = s.CG // 128
        s.DFF = 3 * D // 2
        s.FC = s.DFF // 128
        s.INW = s.LW + 3 * s.NAW + s.FW + 3 * D
        s.NIN = s.INW // 128
        s.TL, s.TC = L // 2, LC // 2
        s.T = s.TL + s.TC
        s.LT = L + LC
        s.ROWS = L // 64
        s.NBh, s.Hh, s.NGh = s.NB, s.H, NG
        s.FCh = s.NGh * s.CGC
        s.MH = s.NBh + 3 * s.Hh + s.FCh
        s.YH = s.NBh + s.Hh + s.FCh
        s.MIXC = s.MH
        s.J = 9 * s.DC
        s.tiles = []
        t = 0
        while t < s.TL:
            n = min(512, s.TL - t)
            s.tiles.append((t, n))
            t += n
        s.tiles.append((s.TL, s.TC))
        s.wshape = {"ffn1_w13": (D, 2 * s.DFF), "ffn1_w2": (s.DFF, D), "w_in": (D, s.INW), "w_out_lru": (s.LW, D),
                    "w_out_na": (s.NAW, D), "w_out_fft": (s.FW, D), "w_o": (D, D), "ffn2_w13": (D, 2 * s.DFF),
                    "ffn2_w2": (s.DFF, D)}
        s.woff = {}
        off = 0
        for nm in WNAMES:
            K, N = s.wshape[nm]
            s.woff[nm] = off
            off += K * N
        s.EL = off
        assert s.EL % 16384 == 0
        s.kwL, s.kwC = min(512, L), min(512, LC)
        s.coff = {"dftL": 0, "dftC": L * 2 * L, "dftG": L * 2 * L + LC * 2 * LC}
        ec = s.coff["dftG"] + s.CG * 2 * s.CG
        s.EC = (ec + 16383) // 16384 * 16384
        c = 0
        s.vo = {}
        for nm, n in (("norm_g", DEPTH * 3 * s.DC), ("final_g", s.DC), ("conv_w", DEPTH * s.NBh * 4),
                      ("conv_b", DEPTH * s.NBh), ("ba", DEPTH * 2 * s.NBh), ("bi", DEPTH * 2 * s.NBh),
                      ("lam", DEPTH * 2 * s.NBh), ("permT", 128), ("bada", DEPTH * s.J), ("cv", s.DC * 5)):
            s.vo[nm] = c
            c += n
        s.NV = c


def pieces(E):
    out = []
    a = 0
    while a < E:
        b = min(E, a + PIECE)
        out.append((a, b))
        a = b
    return out


def tile_w(W):
    K, N = W.shape
    return np.ascontiguousarray(W.reshape(K // 128, 128, N // 128, 128).transpose(2, 1, 0, 3)).reshape(-1)


def shard_flat(flat, E):
    parts = [[] for _ in range(8)]
    for a, b in pieces(E):
        n = (b - a) // 8
        for r in range(8):
            parts[r].append(flat[a + r * n: a + (r + 1) * n])
    return [np.concatenate(p).reshape(-1, 2048) for p in parts]


def pvec(v):
    return np.ascontiguousarray(np.asarray(v, np.float32).reshape(-1, 128).T)


def host_consts(cf):
    L, LC, CG = cf.L, cf.LC, cf.CG
    flat = np.zeros(cf.EC, np.float32)

    def dft(n, kw):
        t = np.arange(n, dtype=np.int64)
        ang = 2.0 * np.pi * ((t[:, None] * t[None, :]) % n).astype(np.float64) / n
        tab = np.stack([np.cos(ang), -np.sin(ang)], 1)
        tab = tab.reshape(n // 128, 128, 2, n // kw, kw).transpose(3, 1, 0, 2, 4)
        return np.ascontiguousarray(tab).reshape(-1).astype(np.float32)
    flat[0: L * 2 * L] = dft(L, cf.kwL)
    flat[cf.coff["dftC"]: cf.coff["dftC"] + LC * 2 * LC] = dft(LC, cf.kwC)
    c = np.arange(CG, dtype=np.int64)
    ang = 2.0 * np.pi * ((c[:, None] * c[None, :]) % CG).astype(np.float64) / CG
    g = np.concatenate([np.cos(ang), np.sin(ang)], 1)
    g = g.reshape(cf.CGC, 128, 2 * CG).transpose(1, 0, 2)
    flat[cf.coff["dftG"]: cf.coff["dftG"] + CG * 2 * CG] = np.ascontiguousarray(g).reshape(-1)
    return flat


def rope_tables(cf, s):
    t = np.arange(cf.TL) + s * cf.TL
    row = (t // 64).astype(np.float64)
    col = (t % 64).astype(np.float64)
    d = np.arange(128)
    i = (d % 64) % 32
    inv = 10000.0 ** (-(2.0 * i) / 64.0)
    pos = np.where((d < 64)[:, None], row[None, :], col[None, :])
    ang = pos * inv[:, None]
    sgn = np.where((d % 64) < 32, -1.0, 1.0)[:, None]
    return np.concatenate([np.cos(ang), np.sin(ang) * sgn], 1).astype(np.float32)


def perm_T():
    m = np.arange(128)
    src = (m // 64) * 64 + ((m % 64) + 32) % 64
    PT = np.zeros((128, 128), np.float32)
    PT[src, m] = 1.0
    return PT


def bias_tiles(rpb_h):
    cols = np.arange(64)
    cs = np.clip(cols - 8, 0, 48)
    col_in = (cols[None, :] >= cs[:, None]) & (cols[None, :] < cs[:, None] + 16)
    cidx = np.clip(cols[None, :] - cols[:, None] + 15, 0, 30)

    def B(dl):
        if abs(dl) > 7:
            return np.full((64, 64), NEG, np.float32)
        b = rpb_h[dl + 7][cidx]
        b = np.where(col_in, b, NEG)
        return b.T.astype(np.float32)

    def U(d0, mask=()):
        t = np.zeros((128, 128), np.float32)
        for i in range(2):
            for j in range(2):
                blk = B(d0 + i - j)
                if (i, j) in mask:
                    blk = np.full((64, 64), NEG, np.float32)
                t[i * 64:(i + 1) * 64, j * 64:(j + 1) * 64] = blk
        return t
    tl = [U(d0) for d0 in (-6, -4, -2, 0, 2, 4, 6)]
    tl.append(U(-4, mask=((0, 1),)))
    tl.append(U(4, mask=((0, 0), (1, 0), (1, 1))))
    return np.concatenate(tl, 1)


def host_inputs(cf, inp):
    D, DC, DEPTH = cf.D, cf.DC, cf.DEPTH
    f = lambda k: np.asarray(inp[k], np.float32)
    x, c, ctx, c_ctx = f("x"), f("c"), f("ctx"), f("c_ctx")
    w_ada, b_ada = f("w_ada"), f("b_ada")
    wfl = []
    for l in range(DEPTH):
        flat = np.empty(cf.EL, np.float32)
        for nm in WNAMES:
            K, N = cf.wshape[nm]
            flat[cf.woff[nm]: cf.woff[nm] + K * N] = tile_w(f(nm)[l])
        wfl.append(flat.reshape(-1, 2048))
    cv = np.concatenate([c, c_ctx[None, :]], 0)
    cvT = np.ascontiguousarray(cv.T.reshape(DC, 128, 5).transpose(1, 0, 2)).reshape(128, DC * 5)
    PT = perm_T()
    m = {}
    for l in range(DEPTH):
        m["w%d" % l] = wfl[l]
        m["wada%d" % l] = np.ascontiguousarray(w_ada[l].reshape(DC, 128, cf.J, 128).transpose(2, 1, 0, 3)).reshape(cf.J * 128, DC * 128)
    m["cst"] = host_consts(cf).reshape(-1, 2048)
    vec = np.zeros((128, cf.NV), np.float32)

    def put(nm, arr):
        arr = np.asarray(arr, np.float32)
        vec[:, cf.vo[nm]: cf.vo[nm] + arr.shape[1]] = arr
    put("norm_g", pvec(f("norm_g").reshape(-1)))
    put("final_g", pvec(f("final_g")))
    put("conv_w", pvec(f("conv_w").reshape(DEPTH, 4, cf.NB, 128).transpose(0, 2, 1, 3).reshape(-1)))
    put("conv_b", pvec(f("conv_b").reshape(-1)))
    put("ba", pvec(f("lru_ba").reshape(-1)))
    put("bi", pvec(f("lru_bi").reshape(-1)))
    put("lam", pvec(f("lru_lam").reshape(-1)))
    put("permT", PT)
    put("bada", pvec(b_ada.reshape(-1)))
    m["vec"] = vec
    g = np.stack([f("lru_wa"), f("lru_wi")], 3)
    m["lruw"] = np.ascontiguousarray(g.transpose(4, 0, 1, 2, 3, 5)).reshape(128, -1)
    m["rope"] = np.concatenate([rope_tables(cf, 0), rope_tables(cf, 1)], 1)
    rpb = f("na_rpb")
    nab = np.stack([np.stack([bias_tiles(rpb[l, hh]) for hh in range(cf.H)], 0) for l in range(DEPTH)], 0)
    m["nab"] = np.ascontiguousarray(nab).reshape(DEPTH * cf.H * 128, 9 * 128)
    maps = []
    for core in range(cf.NCORE):
        mc = dict(m)
        xin = np.empty((cf.NVC, D, cf.T), np.float32)
        for v in range(cf.NVC):
            vc = core * cf.NVC + v
            b, s = vc // 2, vc % 2
            xin[v, :, :cf.TL] = x[b, s * cf.TL:(s + 1) * cf.TL, :].T
            xin[v, :, cf.TL:] = ctx[b, s * cf.TC:(s + 1) * cf.TC, :].T
        mc["xin"] = xin.reshape(cf.NVC * D, cf.T)
        cvl = np.zeros((5, D), np.float32)
        for bb in range(cf.NBAT):
            cvl[bb] = c[core * cf.NBAT + bb]
        cvl[4] = c_ctx
        vecc = vec.copy()
        vecc[:, cf.vo["cv"]: cf.vo["cv"] + DC * 5] = np.ascontiguousarray(cvl.T.reshape(DC, 128, 5).transpose(1, 0, 2)).reshape(128, DC * 5)
        mc["vec"] = vecc
        maps.append(mc)
    return maps


class Buf:
    __slots__ = ("w", "rd", "excl")

    def __init__(self, excl=False):
        self.w = None
        self.rd = {}
        self.excl = excl


class Rec:
    def __init__(self):
        self.calls = []

    def __getattr__(self, name):
        def f(*a, **k):
            self.calls.append((name, a, k))
            return self
        return f


def replay(e, calls):
    last = None
    for name, a, k in calls:
        last = getattr(e, name)(*a, **k)
    return last


class Prog:
    ENG = ("pe", "act", "dve", "pool", "sp")
    NDS = 6
    NCS = 4

    def __init__(self, nc, es):
        self.nc = nc
        self.q = {e: [] for e in self.ENG}
        self.sem = {e: es.enter_context(nc.semaphore("s_" + e)) for e in self.ENG}
        self.cnt = {e: 0 for e in self.ENG}
        self.seen = {e: {} for e in self.ENG}
        self.dsem = {}
        for qn in ("sp", "pool"):
            self.dsem[qn] = [[es.enter_context(nc.semaphore("d_%s%d" % (qn, i))), 0] for i in range(self.NDS)]
        self.dk = {"sp": 0, "pool": 0}
        self.csem = [[es.enter_context(nc.semaphore("c_%d" % i)), 0] for i in range(self.NCS)]
        self.ck = 0

    def _wait(self, eng, ev):
        if ev is None:
            return
        sem, val = ev
        k = id(sem)
        if self.seen[eng].get(k, 0) >= val:
            return
        self.seen[eng][k] = val
        self.q[eng].append(lambda e, s=sem, v=val: e.wait_ge(s, v))

    def _deps(self, eng, r, w):
        for b in r:
            self._wait(eng, b.w)
        for b in w:
            self._wait(eng, b.w)
            for ev in b.rd.values():
                self._wait(eng, ev)

    def _mark(self, ev, r, w):
        for b in w:
            b.w = ev
            b.rd = {}
        for b in r:
            if b.w is not ev:
                b.rd[id(ev[0])] = ev

    def op(self, eng, fn, r=(), w=()):
        if any(b.excl for b in r):
            w = list(w) + [b for b in r if b.excl]
            r = [b for b in r if not b.excl]
        self._deps(eng, r, w)
        self.cnt[eng] += 1
        s = self.sem[eng]
        ev = (s, self.cnt[eng])
        rec = Rec()
        fn(rec)
        self.q[eng].append(lambda e, calls=rec.calls, s=s: replay(e, calls).then_inc(s, 1))
        self._mark(ev, r, w)
        return ev

    def dma(self, qn, out, in_, r=(), w=(), fn=None):
        self._deps(qn, r, w)
        slot = self.dsem[qn][self.dk[qn] % self.NDS]
        self.dk[qn] += 1
        self._wait(qn, (slot[0], slot[1]))
        slot[1] += 16
        ev = (slot[0], slot[1])
        if fn is None:
            self.q[qn].append(lambda e, o=out, i=in_, s=slot[0]: e.dma_start(out=o, in_=i).then_inc(s, 16))
        else:
            self.q[qn].append(lambda e, f=fn, s=slot[0]: f(e).then_inc(s, 16))
        self._mark(ev, r, w)
        return ev

    def coll(self, groups, in_ap, out_ap, r=(), w=()):
        self._deps("pool", r, w)
        slot = self.csem[self.ck % self.NCS]
        self.ck += 1
        self._wait("pool", (slot[0], slot[1]))
        slot[1] += 1
        ev = (slot[0], slot[1])
        self.q["pool"].append(lambda e, i=in_ap, o=out_ap, s=slot[0], g=groups: e.collective_compute(
            "AllGather", ALU.bypass, replica_groups=g, ins=[i], outs=[o]).then_inc(s))
        self._mark(ev, r, w)
        return ev

    def pid(self, e, kind):
        if kind not in self._pidcache:
            v = e.partition_id()
            self._pidcache[kind] = (v % 2) if kind == "s" else (v // 2)
        return self._pidcache[kind]

    def all_events(self):
        evs = [(self.sem[e], self.cnt[e]) for e in self.ENG if self.cnt[e] > 0]
        for qn in self.dsem:
            for s, c in self.dsem[qn]:
                if c > 0:
                    evs.append((s, c))
        for s, c in self.csem:
            if c > 0:
                evs.append((s, c))
        return evs

    def barrier(self):
        evs = self.all_events()
        for e in self.ENG:
            for ev in evs:
                self._wait(e, ev)

    def flush(self):
        with self.nc.Block() as block:
            for name, deco in (("pe", block.tensor), ("act", block.scalar), ("dve", block.vector),
                               ("pool", block.gpsimd), ("sp", block.sync)):
                ops = self.q[name]
                self.q[name] = []

                def run(e, ops=ops):
                    self._pidcache = {}
                    for f in ops:
                        f(e)
                deco(run)


def build(cf):
    nc = bass.Bass("TRN2", target_bir_lowering=False)
    D, DC, T, TL, TC, L, LC, LT = cf.D, cf.DC, cf.T, cf.TL, cf.TC, cf.L, cf.LC, cf.LT
    DEPTH, FC, J = cf.DEPTH, cf.FC, cf.J
    FH = FC // 2
    ALL8 = [list(range(8))]
    PAIRS = [[0, 1], [2, 3], [4, 5], [6, 7]]

    def din(name, shape):
        return nc.dram_tensor(name, list(shape), F32, kind="ExternalInput").ap()
    NVC, NBAT = cf.NVC, cf.NBAT
    xin = din("xin", [NVC * D, T])
    w_in_d = [din("w%d" % l, [cf.EL // 2048, 2048]) for l in range(DEPTH)]
    wada_d = [din("wada%d" % l, [J * 128, DC * 128]) for l in range(DEPTH)]
    cst_d = din("cst", [cf.EC // 2048, 2048])
    vec_d = din("vec", [128, cf.NV])
    lruw_d = din("lruw", [128, DEPTH * 2 * cf.NBh * 2 * 128])
    rope_d = din("rope", [128, 4 * TL])
    nab_d = din("nab", [DEPTH * cf.Hh * 128, 9 * 128])
    out_d = nc.dram_tensor("out", [NVC * D, TL], F32, kind="ExternalOutput").ap()

    def dint(name, shape, dt):
        return nc.dram_tensor(name, list(shape), dt)
    xd_all = dint("xd", [NVC * D, T], F32).ap()

    wfull = [{nm: dint("wf%d_%s" % (l, nm), [cf.wshape[nm][0] * cf.wshape[nm][1] // 2048, 2048], BF16) for nm in WNAMES} for l in range(DEPTH)]
    cfull = dint("cfull", [cf.EC // 2048, 2048], BF16)
    mixl = dint("mixl", [NVC * cf.MH * 128, T], BF16)
    gat_all = dint("gat", [NVC * 3 * DC * 128, T], BF16).ap()
    yhd = dint("yhd", [NVC * cf.YH * 128, T], BF16)
    dbg = {}
    if cf.dbg:
        for nm, shp in (("d_x1", [D, T]), ("d_mix", [cf.MH * 128, T]), ("d_y", [cf.YH * 128, T]),
                        ("d_x2", [D, T]), ("d_mod", [128, 18 * DC])):
            dbg[nm] = nc.dram_tensor(nm, shp, F32, kind="ExternalOutput").ap()

    def wtile(l, nm, n, lo=0, hi=None):
        K, N = cf.wshape[nm]
        sz = K * 128
        off = n * sz
        ap = wfull[l][nm].ap().rearrange("r e -> (r e)")[off: off + sz].rearrange("(p f) -> p f", p=128)
        return ap if hi is None else ap[:, lo:hi]

    def ctile(nm, off, sz, p=128):
        o = cf.coff[nm] + off
        return cfull.ap().rearrange("r e -> (r e)")[o: o + sz].rearrange("(p f) -> p f", p=p)

    with ExitStack() as es:
        P = Prog(nc, es)

        sbn = [0]

        def SB(st, name, shape, dt=F32):
            sbn[0] += 1
            return st.enter_context(nc.sbuf_tensor("%s_s%d" % (name, sbn[0]), list(shape), dt))
        banks = [es.enter_context(nc.psum_tensor("ps%d" % i, [128, 512], F32)) for i in range(8)]
        bankB = [Buf(excl=True) for _ in range(8)]
        bk = [0]

        def bank():
            i = bk[0] % 8
            bk[0] += 1
            return banks[i], bankB[i]
        vec = SB(es, "vec", [128, cf.NV])
        onesf = SB(es, "onesf", [128, 128])
        onesb = SB(es, "onesb", [128, 128], BF16)
        identb = SB(es, "identb", [128, 128], BF16)
        permb = SB(es, "permb", [128, 128], BF16)
        mt = SB(es, "mt", [128, 2, 3, 3, DC])
        nsp8 = SB(es, "nsp8", [128, DEPTH * 2 * cf.NBh])
        cB, mtB, xBall = Buf(), Buf(), Buf()
        xBv = [[[Buf() for _ in cf.tiles] for _ in range(DC)] for _ in range(NVC)]
        cur = {'vc': 0}
        XD = lambda: xd_all[cur['vc'] * D:(cur['vc'] + 1) * D, :]
        GAT = lambda: gat_all[cur['vc'] * 3 * DC * 128:(cur['vc'] + 1) * 3 * DC * 128, :]
        mlall = [SB(es, 'mlall%d' % l, [128, 5, J]) for l in range(DEPTH)]
        mlB = Buf()
        vo = cf.vo

        def V(nm, a, n=1):
            return vec[:, vo[nm] + a: vo[nm] + a + n]

        with ExitStack() as ph:
            tmpf = SB(ph, "tmpf", [128, 128])
            e_ = SB(ph, "e_", [128, DEPTH * 2 * cf.NBh])
            p_ = SB(ph, "p_", [128, DEPTH * 2 * cf.NBh])
            t_ = SB(ph, "t_", [128, DEPTH * 2 * cf.NBh])
            P.dma("sp", vec[:], vec_d, w=[cB])
            P.op("pool", lambda e: e.memset(onesf[:], 1.0), w=[cB])
            P.op("pool", lambda e: e.memset(onesb[:], 1.0), w=[cB])
            P.op("pool", lambda e: e.affine_select(out=tmpf[:], in_=onesf[:], pattern=[[-1, 128]], compare_op=ALU.is_equal,
                                                   fill=0.0, base=0, channel_multiplier=1), r=[cB], w=[cB])
            P.op("dve", lambda e: e.tensor_copy(out=identb[:], in_=tmpf[:]), r=[cB], w=[cB])
            P.op("dve", lambda e: e.tensor_copy(out=permb[:], in_=V("permT", 0, 128)), r=[cB], w=[cB])
            nl = DEPTH * 2 * cf.NBh
            P.op("act", lambda e: e.activation(out=e_[:], in_=V("lam", 0, nl), func=AF.Exp, scale=-1.0), r=[cB], w=[cB])
            P.op("dve", lambda e: e.tensor_scalar(out=p_[:], in0=e_[:], scalar1=-0.2, scalar2=0.25, op0=ALU.mult, op1=ALU.add), r=[cB], w=[cB])
            for cst in (1.0 / 3.0, 0.5, 1.0):
                P.op("dve", lambda e: e.tensor_tensor(out=t_[:], in0=e_[:], in1=p_[:], op=ALU.mult), r=[cB], w=[cB])
                P.op("dve", lambda e, c=cst: e.tensor_scalar(out=p_[:], in0=t_[:], scalar1=-1.0, scalar2=c, op0=ALU.mult, op1=ALU.add), r=[cB], w=[cB])
            P.op("dve", lambda e: e.tensor_tensor(out=t_[:], in0=e_[:], in1=p_[:], op=ALU.mult), r=[cB], w=[cB])
            P.op("dve", lambda e: e.tensor_scalar(out=nsp8[:], in0=t_[:], scalar1=-8.0, scalar2=None, op0=ALU.mult), r=[cB], w=[cB])
            for r0 in range(0, NVC * D, 512):
                P.dma("sp", xd_all[r0:r0 + 512, :], xin[r0:r0 + 512, :], w=[xBall])
            wB = [Buf() for _ in range(DEPTH)]
            cstB = Buf()

            def cast(src_d, so, full, E, B_):
                R = E // 2048
                r0 = 0
                while r0 < R:
                    r1 = min(R, r0 + 1024)
                    P.dma("pool", full.ap()[r0:r1, :], src_d[so + r0:so + r1, :], w=[B_])
                    r0 = r1
            cast(cst_d, 0, cfull, cf.EC, cstB)
            for l in range(DEPTH):
                for nm in WNAMES:
                    cast(w_in_d[l], cf.woff[nm] // 2048, wfull[l][nm], cf.wshape[nm][0] * cf.wshape[nm][1], wB[l])
            P.barrier()
            P.flush()

        with ExitStack() as ph:
            sc = SB(ph, "sc", [128, DC, 8])
            wt = [SB(ph, "wt%d" % i, [128, DC * 128]) for i in range(2)]
            wtB = [Buf(), Buf()]
            scB = Buf()
            P.op("pool", lambda e: e.memset(sc[:], 0.0), w=[scB])
            P.op("act", lambda e: e.activation(out=sc[:, :, 0:5], in_=V("cv", 0, DC * 5).rearrange("p (c o) -> p c o", o=5), func=AF.Silu), r=[cB, scB], w=[scB])
            for l in range(DEPTH):
                for j in range(J):
                    k = (l * J + j) % 2
                    P.dma("sp", wt[k][:], wada_d[l][j * 128:(j + 1) * 128, :], w=[wtB[k]])
                    ps, psB = bank()

                    def mm(e, k=k, ps=ps):
                        for kc in range(DC):
                            last = e.matmul(ps[:, 0:8], wt[k][:, kc * 128:(kc + 1) * 128], sc[:, kc, :],
                                            start=(kc == 0), stop=(kc == DC - 1))
                        return last
                    P.op("pe", mm, r=[wtB[k], scB], w=[psB])
                    P.op("dve", lambda e, ps=ps, l=l, j=j: e.tensor_scalar(out=mlall[l][:, :, j], in0=ps[:, 0:5], scalar1=V("bada", l * J + j),
                                                                            scalar2=None, op0=ALU.add), r=[psB, cB], w=[mlB])
            P.barrier()
            P.flush()

        def load_mods(l):
            bb = cur['vc'] // 2
            for ci, row in enumerate((bb, 4)):
                mv = mlall[l][:, row, :]
                for i in range(3):
                    sh = mv[:, (3 * i) * DC:(3 * i + 1) * DC]
                    scl = mv[:, (3 * i + 1) * DC:(3 * i + 2) * DC]
                    gt = mv[:, (3 * i + 2) * DC:(3 * i + 3) * DC]
                    g = V("norm_g", (l * 3 + i) * DC, DC)
                    P.op("dve", lambda e, ci=ci, i=i, scl=scl, g=g: e.scalar_tensor_tensor(out=mt[:, ci, 0, i, :], in0=scl, scalar=1.0, in1=g,
                                                                                           op0=ALU.add, op1=ALU.mult), r=[mlB, cB], w=[mtB])
                    P.op("dve", lambda e, ci=ci, i=i, sh=sh: e.tensor_copy(out=mt[:, ci, 1, i, :], in_=sh), r=[mlB], w=[mtB])
                    P.op("dve", lambda e, ci=ci, i=i, gt=gt: e.tensor_scalar(out=mt[:, ci, 2, i, :], in0=gt, scalar1=(1.0 if i == 1 else 0.5),
                                                                             scalar2=None, op0=ALU.mult), r=[mlB], w=[mtB])

        def xr(c, ti):
            return [xBv[cur['vc']][c][ti], xBall]

        def sumsq_rstd(ph, xt, tn, xtB, rstd, rB, scratch):
            sq, sqB = scratch
            ps, psB = bank()
            for c in range(DC):
                k = c % 2
                P.op("act", lambda e, c=c, k=k: e.activation(out=sq[:, k, :tn], in_=xt[:, c, :tn], func=AF.Square), r=[xtB], w=[sqB[k]])
                P.op("pe", lambda e, c=c, k=k, ps=ps: e.matmul(ps[:, :tn], onesf[:], sq[:, k, :tn], start=(c == 0), stop=(c == DC - 1)),
                     r=[sqB[k], cB], w=[psB])
            P.op("act", lambda e, ps=ps: e.activation(out=rstd[:, :tn], in_=ps[:, :tn], func=AF.Sqrt, scale=1.0 / D, bias=EPS), r=[psB], w=[rB])
            P.op("dve", lambda e: e.reciprocal(out=rstd[:, :tn], in_=rstd[:, :tn]), r=[rB], w=[rB])

        def norm_mod(i, h, hB):
            with ExitStack() as ph:
                xt = SB(ph, "xt", [128, DC, 512])
                sq = SB(ph, "sq", [128, 2, 512])
                rstd = SB(ph, "rstd", [128, 512])
                tmp = SB(ph, "tmp", [128, 2, 512])
                xtB, rB = Buf(), Buf()
                sqB, tmpB = [Buf(), Buf()], [Buf(), Buf()]
                for ti, (t0, tn) in enumerate(cf.tiles):
                    ci = 1 if t0 >= TL else 0
                    P.dma("sp", xt[:, :, :tn], XD().rearrange("(c p) t -> p c t", p=128)[:, :, t0:t0 + tn],
                          r=[b for c in range(DC) for b in xr(c, ti)], w=[xtB])
                    sumsq_rstd(ph, xt, tn, xtB, rstd, rB, (sq, sqB))
                    for c in range(DC):
                        k = c % 2
                        P.op("dve", lambda e, c=c, k=k, ci=ci: e.scalar_tensor_tensor(out=tmp[:, k, :tn], in0=xt[:, c, :tn], scalar=mt[:, ci, 0, i, c:c + 1],
                                                                                      in1=rstd[:, :tn], op0=ALU.mult, op1=ALU.mult),
                             r=[xtB, rB, mtB], w=[tmpB[k]])
                        P.op("act", lambda e, c=c, k=k, ci=ci, t0=t0: e.activation(out=h[:, c, t0:t0 + tn], in_=tmp[:, k, :tn], func=AF.Identity,
                                                                                   bias=mt[:, ci, 1, i, c:c + 1]), r=[tmpB[k], mtB], w=[hB])
                P.barrier()
                P.flush()

        def x_update(st, i, c, ti, t0, tn, ps, psB, xc, xcB, k):
            ci = 1 if t0 >= TL else 0
            P.dma("sp", xc[:, k, :tn], XD()[c * 128:(c + 1) * 128, t0:t0 + tn], r=xr(c, ti), w=[xcB[k]])
            P.op("dve", lambda e: e.scalar_tensor_tensor(out=xc[:, k, :tn], in0=ps[:, :tn], scalar=mt[:, ci, 2, i, c:c + 1], in1=xc[:, k, :tn],
                                                         op0=ALU.mult, op1=ALU.add), r=[psB, xcB[k], mtB], w=[xcB[k]])
            P.dma("pool", XD()[c * 128:(c + 1) * 128, t0:t0 + tn], xc[:, k, :tn], r=[xcB[k]], w=[xBv[cur['vc']][c][ti]])

        def ffn(l, i, n13, n2):
            with ExitStack() as ph:
                h = SB(ph, "h", [128, DC, T], BF16)
                hB = Buf()
                norm_mod(i, h, hB)
                G = SB(ph, "G", [128, FH, T], BF16)
                GB = Buf()
                wa = [SB(ph, "wa%d" % k, [128, DC * 128], BF16) for k in range(2)]
                wb = [SB(ph, "wb%d" % k, [128, DC * 128], BF16) for k in range(2)]
                w2 = [SB(ph, "w2%d" % k, [128, FH * 128], BF16) for k in range(2)]
                sa = SB(ph, "sa", [128, 2, 512])
                xc = SB(ph, "xc", [128, 2, 512])
                waB, wbB, w2B, saB, xcB = ([Buf(), Buf()] for _ in range(5))
                cnt = 0
                for half in range(2):
                    for jj in range(FH):
                        j = half * FH + jj
                        k = jj % 2
                        P.dma("sp", wa[k][:], wtile(l, n13, j), r=[wB[l]], w=[waB[k]])
                        P.dma("sp", wb[k][:], wtile(l, n13, FC + j), r=[wB[l]], w=[wbB[k]])
                        for ti, (t0, tn) in enumerate(cf.tiles):
                            pa, paB = bank()
                            pb, pbB = bank()

                            def mm(e, k=k, t0=t0, tn=tn, pa=pa, pb=pb):
                                for kc in range(DC):
                                    e.matmul(pa[:, :tn], wa[k][:, kc * 128:(kc + 1) * 128], h[:, kc, t0:t0 + tn], start=(kc == 0), stop=(kc == DC - 1))
                                for kc in range(DC):
                                    last = e.matmul(pb[:, :tn], wb[k][:, kc * 128:(kc + 1) * 128], h[:, kc, t0:t0 + tn], start=(kc == 0), stop=(kc == DC - 1))
                                return last
                            P.op("pe", mm, r=[waB[k], wbB[k], hB], w=[paB, pbB])
                            s_ = cnt % 2
                            cnt += 1
                            P.op("act", lambda e, s_=s_, tn=tn, pa=pa: e.activation(out=sa[:, s_, :tn], in_=pa[:, :tn], func=AF.Silu), r=[paB], w=[saB[s_]])
                            P.op("dve", lambda e, s_=s_, tn=tn, t0=t0, jj=jj, pb=pb: e.tensor_tensor(out=G[:, jj, t0:t0 + tn], in0=sa[:, s_, :tn], in1=pb[:, :tn], op=ALU.mult),
                                 r=[saB[s_], pbB], w=[GB])
                    for c in range(DC):
                        k = c % 2
                        P.dma("sp", w2[k][:], wtile(l, n2, c, half * FH * 128, (half + 1) * FH * 128), r=[wB[l]], w=[w2B[k]])
                        for ti, (t0, tn) in enumerate(cf.tiles):
                            ps, psB = bank()

                            def mm2(e, k=k, t0=t0, tn=tn, ps=ps):
                                for kk in range(FH):
                                    last = e.matmul(ps[:, :tn], w2[k][:, kk * 128:(kk + 1) * 128], G[:, kk, t0:t0 + tn], start=(kk == 0), stop=(kk == FH - 1))
                                return last
                            P.op("pe", mm2, r=[w2B[k], GB], w=[psB])
                            s_ = cnt % 2
                            cnt += 1
                            x_update(ph, i, c, ti, t0, tn, ps, psB, xc, xcB, s_)
                P.barrier()
                P.flush()

        def mixdst(n):
            NB, H = cf.NB, cf.H
            if n < NB:
                return n // cf.NBh, n % cf.NBh, None
            if n < NB + 3 * H:
                kind = (n - NB) // H
                hq = (n - NB) % H
                return hq // cf.Hh, cf.NBh + kind * cf.Hh + hq % cf.Hh, kind
            fc = n - NB - 3 * H
            return fc // cf.FCh, cf.NBh + 3 * cf.Hh + fc % cf.FCh, None

        mixBv = [Buf() for _ in range(NVC)]
        gatBv = [Buf() for _ in range(NVC)]
        yBv = [Buf() for _ in range(NVC)]

        def in_proj(l):
            with ExitStack() as ph:
                h = SB(ph, "h", [128, DC, T], BF16)
                hB = Buf()
                norm_mod(1, h, hB)
                w = [SB(ph, "w%d" % k, [128, DC * 128], BF16) for k in range(2)]
                st = [SB(ph, "st%d" % k, [128, T], BF16) for k in range(3)]
                qb = SB(ph, "qb", [128, 2, 512], BF16)
                t1 = SB(ph, "t1", [128, 2, 512])
                t2 = SB(ph, "t2", [128, 2, 512])
                rope = SB(ph, "rope", [128, 2 * TL])
                wBf, stB, qbB, t1B, t2B = [Buf(), Buf()], [Buf() for _ in range(3)], [Buf(), Buf()], [Buf(), Buf()], [Buf(), Buf()]
                rpB = Buf()
                s_ = cur["vc"] % 2
                P.dma("sp", rope[:], rope_d[:, s_ * 2 * TL:(s_ + 1) * 2 * TL], w=[rpB])
                cnt = 0
                for n in range(cf.NIN):
                    k = n % 2
                    sk = n % 3
                    P.dma("sp", w[k][:], wtile(l, "w_in", n), r=[wB[l]], w=[wBf[k]])
                    isg = n >= cf.MIXC
                    kind = None if isg else mixdst(n)[2]
                    for ti, (t0, tn) in enumerate(cf.tiles):
                        ps, psB = bank()

                        def mm(e, k=k, t0=t0, tn=tn, ps=ps):
                            for kc in range(DC):
                                last = e.matmul(ps[:, :tn], w[k][:, kc * 128:(kc + 1) * 128], h[:, kc, t0:t0 + tn], start=(kc == 0), stop=(kc == DC - 1))
                            return last
                        P.op("pe", mm, r=[wBf[k], hB], w=[psB])
                        if isg:
                            P.op("act", lambda e, sk=sk, t0=t0, tn=tn, ps=ps: e.activation(out=st[sk][:, t0:t0 + tn], in_=ps[:, :tn], func=AF.Sigmoid), r=[psB], w=[stB[sk]])
                        elif kind in (0, 1) and t0 < TL:
                            s_ = cnt % 2
                            cnt += 1
                            P.op("act", lambda e, s_=s_, tn=tn, ps=ps: e.activation(out=qb[:, s_, :tn], in_=ps[:, :tn], func=AF.Copy), r=[psB], w=[qbB[s_]])
                            p2, p2B = bank()
                            P.op("pe", lambda e, s_=s_, tn=tn, p2=p2: e.matmul(p2[:, :tn], permb[:], qb[:, s_, :tn], start=True, stop=True), r=[qbB[s_], cB], w=[p2B])
                            P.op("dve", lambda e, s_=s_, tn=tn, t0=t0, ps=ps: e.tensor_tensor(out=t1[:, s_, :tn], in0=ps[:, :tn], in1=rope[:, t0:t0 + tn], op=ALU.mult),
                                 r=[psB, rpB], w=[t1B[s_]])
                            P.op("dve", lambda e, s_=s_, tn=tn, t0=t0, p2=p2: e.tensor_tensor(out=t2[:, s_, :tn], in0=p2[:, :tn], in1=rope[:, TL + t0:TL + t0 + tn], op=ALU.mult),
                                 r=[p2B, rpB], w=[t2B[s_]])
                            P.op("dve", lambda e, s_=s_, tn=tn, t0=t0, sk=sk: e.tensor_tensor(out=st[sk][:, t0:t0 + tn], in0=t1[:, s_, :tn], in1=t2[:, s_, :tn], op=ALU.add),
                                 r=[t1B[s_], t2B[s_]], w=[stB[sk]])
                        else:
                            P.op("act", lambda e, sk=sk, t0=t0, tn=tn, ps=ps: e.activation(out=st[sk][:, t0:t0 + tn], in_=ps[:, :tn], func=AF.Copy), r=[psB], w=[stB[sk]])
                    if isg:
                        g0 = (n - cf.MIXC) * 128
                        P.dma("pool", GAT()[g0:g0 + 128, :], st[sk][:], r=[stB[sk]], w=[gatBv[cur["vc"]]])
                    else:
                        hf, idx, _ = mixdst(n)
                        r0 = (cur["vc"] * cf.MH + idx) * 128
                        P.dma("pool", mixl.ap()[r0:r0 + 128, :], st[sk][:], r=[stB[sk]], w=[mixBv[cur["vc"]]])
                if cf.dbg and l == 0 and cur['vc'] == cf.dbgvc:
                    P.dma("pool", dbg["d_mix"], mixl.ap()[cur["vc"] * cf.MH * 128:(cur["vc"] + 1) * cf.MH * 128, :], r=[mixBv[cur["vc"]]])
                P.barrier()
                P.flush()

        mixl4 = mixl.ap().rearrange("(v m p) t -> p v m t", v=NVC, p=128)
        yhd4 = yhd.ap().rearrange("(v m p) t -> p v m t", v=NVC, p=128)

        def load_mix(dst, idx, B_):
            bb = cur['b']
            for r in range(2):
                P.dma("sp", dst[:, r * TL:(r + 1) * TL], mixl4[:, 2 * bb + r, idx, 0:TL], r=[mixBv[2 * bb + r]], w=[B_])
                P.dma("sp", dst[:, L + r * TC:L + (r + 1) * TC], mixl4[:, 2 * bb + r, idx, TL:T], r=[mixBv[2 * bb + r]], w=[B_])

        def store_y(src, idx, B_):
            bb = cur['b']
            for j in range(2):
                P.dma("sp", yhd4[:, 2 * bb + j, idx, 0:TL], src[:, j * TL:(j + 1) * TL], r=[B_], w=[yBv[2 * bb + j]])
                P.dma("sp", yhd4[:, 2 * bb + j, idx, TL:T], src[:, L + j * TC:L + (j + 1) * TC], r=[B_], w=[yBv[2 * bb + j]])

        def lru_phase(l):
            with ExitStack() as ph:
                u = SB(ph, "u", [128, LT], BF16)
                vf = SB(ph, "vf", [128, LT])
                vb = SB(ph, "vb", [128, LT], BF16)
                rt = SB(ph, "rt", [128, LT])
                it = SB(ph, "it", [128, LT])
                at = SB(ph, "at", [128, LT])
                tm = SB(ph, "tm", [128, LT])
                hh = [SB(ph, "hh%d" % d, [128, LT]) for d in range(2)]
                yo = SB(ph, "yo", [128, LT], BF16)
                lw = SB(ph, "lw", [128, 2 * 2 * 128])
                lwb = SB(ph, "lwb", [128, 2 * 2 * 128], BF16)
                uB, vfB, vbB, rB, iB, aB, tB, yoB, lwB = (Buf() for _ in range(9))
                hB2 = [Buf(), Buf()]
                segs = ((0, L), (L, LT))
                for blk in range(cf.NBh):
                    load_mix(u, blk, uB)
                    cw = lambda j_, blk=blk: V("conv_w", (l * cf.NBh + blk) * 4 + j_)
                    P.op("dve", lambda e, cw=cw, blk=blk: e.tensor_scalar(out=vf[:], in0=u[:], scalar1=cw(2), scalar2=V("conv_b", l * cf.NBh + blk),
                                                                          op0=ALU.mult, op1=ALU.add), r=[uB, cB], w=[vfB])
                    for j_, off in ((0, -2), (1, -1), (3, 1)):
                        for s0, s1 in segs:
                            a, b = s0 + max(0, -off), s1 - max(0, off)
                            P.op("dve", lambda e, cw=cw, j_=j_, a=a, b=b, off=off: e.scalar_tensor_tensor(out=vf[:, a:b], in0=u[:, a + off:b + off], scalar=cw(j_),
                                                                                                          in1=vf[:, a:b], op0=ALU.mult, op1=ALU.add), r=[uB, cB, vfB], w=[vfB])
                    P.op("act", lambda e: e.activation(out=vb[:], in_=vf[:], func=AF.Copy), r=[vfB], w=[vbB])
                    for d in range(2):
                        wi0 = ((l * 2 + d) * cf.NBh + blk) * 2 * 128
                        P.dma("sp", lw[:, 0:256], lruw_d[:, wi0:wi0 + 256], w=[lwB])
                        P.op("dve", lambda e: e.tensor_copy(out=lwb[:, 0:256], in_=lw[:, 0:256]), r=[lwB], w=[lwB])
                        vi = (l * 2 + d) * cf.NBh + blk
                        t0 = 0
                        while t0 < LT:
                            tn = min(512, LT - t0)
                            pr, prB = bank()
                            pi, piB = bank()
                            P.op("pe", lambda e, t0=t0, tn=tn, pr=pr: e.matmul(pr[:, :tn], lwb[:, 0:128], vb[:, t0:t0 + tn], start=True, stop=True), r=[lwB, vbB], w=[prB])
                            P.op("pe", lambda e, t0=t0, tn=tn, pi=pi: e.matmul(pi[:, :tn], lwb[:, 128:256], vb[:, t0:t0 + tn], start=True, stop=True), r=[lwB, vbB], w=[piB])
                            P.op("act", lambda e, t0=t0, tn=tn, pr=pr, vi=vi: e.activation(out=rt[:, t0:t0 + tn], in_=pr[:, :tn], func=AF.Sigmoid, bias=V("ba", vi)), r=[prB, cB], w=[rB])
                            P.op("act", lambda e, t0=t0, tn=tn, pi=pi, vi=vi: e.activation(out=it[:, t0:t0 + tn], in_=pi[:, :tn], func=AF.Sigmoid, bias=V("bi", vi)), r=[piB, cB], w=[iB])
                            t0 += tn
                        P.op("act", lambda e, vi=vi: e.activation(out=at[:], in_=rt[:], func=AF.Exp, scale=nsp8[:, vi:vi + 1]), r=[rB, cB], w=[aB])
                        P.op("act", lambda e: e.activation(out=tm[:], in_=at[:], func=AF.Square), r=[aB], w=[tB])
                        P.op("act", lambda e: e.activation(out=tm[:], in_=tm[:], func=AF.Sqrt, scale=-1.0, bias=1.0), r=[tB], w=[tB])
                        P.op("dve", lambda e: e.tensor_tensor(out=it[:], in0=it[:], in1=vf[:], op=ALU.mult), r=[iB, vfB], w=[iB])
                        P.op("dve", lambda e: e.tensor_tensor(out=it[:], in0=it[:], in1=tm[:], op=ALU.mult), r=[iB, tB], w=[iB])
                        hd = hh[d]
                        if d == 0:
                            P.op("dve", lambda e, hd=hd: e.tensor_tensor_scan(out=hd[:, L:LT], data0=at[:, L:LT], data1=it[:, L:LT], initial=0.0, op0=ALU.mult, op1=ALU.add),
                                 r=[aB, iB], w=[hB2[d]])
                            P.op("dve", lambda e, hd=hd: e.tensor_tensor_scan(out=hd[:, 0:L], data0=at[:, 0:L], data1=it[:, 0:L], initial=hd[:, LT - 1:LT], op0=ALU.mult, op1=ALU.add),
                                 r=[aB, iB, hB2[d]], w=[hB2[d]])
                        else:
                            P.op("dve", lambda e, hd=hd: e.tensor_tensor_scan(out=hd[:, L:LT][:, ::-1], data0=at[:, L:LT][:, ::-1], data1=it[:, L:LT][:, ::-1], initial=0.0,
                                                                              op0=ALU.mult, op1=ALU.add), r=[aB, iB], w=[hB2[d]])
                            P.op("dve", lambda e, hd=hd: e.tensor_tensor_scan(out=hd[:, 0:L][:, ::-1], data0=at[:, 0:L][:, ::-1], data1=it[:, 0:L][:, ::-1], initial=hd[:, L:L + 1],
                                                                              op0=ALU.mult, op1=ALU.add), r=[aB, iB, hB2[d]], w=[hB2[d]])
                    P.op("dve", lambda e: e.tensor_tensor(out=yo[:], in0=hh[0][:], in1=hh[1][:], op=ALU.add), r=hB2, w=[yoB])
                    store_y(yo, blk, yoB)
                P.barrier()
                P.flush()

        def na_phase(l):
            scale = 128.0 ** -0.5
            NCK = LT // 128
            with ExitStack() as ph:
                qs = [SB(ph, "q%d" % k, [128, LT], BF16) for k in range(2)]
                ks = [SB(ph, "k%d" % k, [128, LT], BF16) for k in range(2)]
                vs = [SB(ph, "v%d" % k, [128, LT], BF16) for k in range(2)]
                bs = [SB(ph, "b%d" % k, [128, 9 * 128]) for k in range(2)]
                vt = SB(ph, "vt", [128, NCK, 128], BF16)
                yo = SB(ph, "yo", [128, LT], BF16)
                ssb = [SB(ph, "ssb%d" % k, [128, 5 * 128]) for k in range(2)]
                pt = [SB(ph, "pt%d" % k, [128, 5 * 128 + max(LC, 512 - 640 + 640)], BF16) for k in range(2)]
                rd = [SB(ph, "rd%d" % k, [128, 256]) for k in range(2)]
                qB, kB, vB, bB = ([Buf(), Buf()] for _ in range(4))
                vtB, yoB = Buf(), Buf()
                ssB, ptB, rdB = [Buf(), Buf()], [Buf(), Buf()], [Buf(), Buf()]
                cnt = 0
                NCC = LC // 128
                for hh in range(cf.Hh):
                    k = hh % 2
                    q, kk_, v, bt = qs[k], ks[k], vs[k], bs[k]
                    load_mix(q, cf.NBh + hh, qB[k])
                    load_mix(kk_, cf.NBh + cf.Hh + hh, kB[k])
                    load_mix(v, cf.NBh + 2 * cf.Hh + hh, vB[k])
                    r0 = (l * cf.Hh + hh) * 128
                    P.dma("sp", bt[:], nab_d[r0:r0 + 128, :], w=[bB[k]])
                    c0 = 0
                    while c0 < NCK:
                        n = min(4, NCK - c0)
                        ps, psB = bank()

                        def tr(e, c0=c0, n=n, ps=ps, v=v):
                            for j in range(n):
                                last = e.matmul(ps[:, j * 128:(j + 1) * 128], v[:, (c0 + j) * 128:(c0 + j + 1) * 128], identb[:], start=True, stop=True)
                            return last
                        P.op("pe", tr, r=[vB[k], cB], w=[psB])
                        P.op("act", lambda e, c0=c0, n=n, ps=ps: e.activation(out=vt[:, c0:c0 + n, :].rearrange("p c d -> p (c d)"), in_=ps[:, :n * 128], func=AF.Copy),
                             r=[psB], w=[vtB])
                        c0 += n
                    for pr_ in range(cf.ROWS // 2):
                        r = 2 * pr_
                        if r < 4:
                            base, tl = 0, [(2 * c - r + 6) // 2 for c in range(4)]
                        elif r >= cf.ROWS - 4:
                            base = cf.ROWS - 8
                            tl = [(base + 2 * c - r + 6) // 2 for c in range(4)]
                        else:
                            base, tl = r - 4, [7, 2, 3, 4, 8]
                        nch = len(tl)
                        s_ = cnt % 2
                        cnt += 1
                        pA, pAB = bank()
                        pB_, pBB = bank()

                        def smm(e, base=base, nch=nch, r=r, pA=pA, pB_=pB_, q=q, kk_=kk_):
                            for c in range(nch):
                                dst = pA[:, c * 128:(c + 1) * 128] if c < 4 else pB_[:, 0:128]
                                e.matmul(dst, kk_[:, (base + 2 * c) * 64:(base + 2 * c) * 64 + 128], q[:, r * 64:r * 64 + 128], start=True, stop=True)
                            for cc in range(NCC):
                                last = e.matmul(pB_[:, 128 + cc * 128:256 + cc * 128], kk_[:, L + cc * 128:L + (cc + 1) * 128], q[:, r * 64:r * 64 + 128], start=True, stop=True)
                            return last
                        P.op("pe", smm, r=[qB[k], kB[k]], w=[pAB, pBB])
                        for c in range(nch):
                            src = pA[:, c * 128:(c + 1) * 128] if c < 4 else pB_[:, 0:128]
                            P.op("dve", lambda e, c=c, src=src, s_=s_, ti_=tl[c], bt=bt: e.scalar_tensor_tensor(out=ssb[s_][:, c * 128:(c + 1) * 128], in0=src, scalar=scale,
                                                                                                              in1=bt[:, ti_ * 128:(ti_ + 1) * 128], op0=ALU.mult, op1=ALU.add),
                                 r=[pAB if c < 4 else pBB, bB[k]], w=[ssB[s_]])
                        P.op("act", lambda e, s_=s_, nch=nch: e.activation(out=pt[s_][:, 0:nch * 128], in_=ssb[s_][:, 0:nch * 128], func=AF.Exp), r=[ssB[s_]], w=[ptB[s_]])
                        P.op("act", lambda e, s_=s_, pB_=pB_: e.activation(out=pt[s_][:, 640:640 + LC], in_=pB_[:, 128:128 + LC], func=AF.Exp, scale=scale), r=[pBB], w=[ptB[s_]])
                        pO, pOB = bank()
                        pD, pDB = bank()

                        def pv(e, base=base, nch=nch, s_=s_, pO=pO, pD=pD):
                            tot = nch + NCC
                            for m_, lhs in ((pO, None), (pD, onesb)):
                                for c in range(tot):
                                    if c < nch:
                                        rhs = pt[s_][:, c * 128:(c + 1) * 128]
                                        vch = base // 2 + c
                                    else:
                                        rhs = pt[s_][:, 640 + (c - nch) * 128:640 + (c - nch + 1) * 128]
                                        vch = L // 128 + (c - nch)
                                    last = e.matmul(m_[:, 0:128], vt[:, vch, :] if lhs is None else lhs[:], rhs, start=(c == 0), stop=(c == tot - 1))
                            return last
                        P.op("pe", pv, r=[ptB[s_], vtB, cB], w=[pOB, pDB])
                        P.op("dve", lambda e, s_=s_, pD=pD: e.reciprocal(out=rd[s_][:, 0:128], in_=pD[:, 0:128]), r=[pDB], w=[rdB[s_]])
                        P.op("dve", lambda e, s_=s_, pO=pO, r=r: e.tensor_tensor(out=yo[:, r * 64:r * 64 + 128], in0=pO[:, 0:128], in1=rd[s_][:, 0:128], op=ALU.mult),
                             r=[pOB, rdB[s_]], w=[yoB])
                    s_ = cnt % 2
                    cnt += 1
                    pA, pAB = bank()

                    def cmm(e, pA=pA, q=q, kk_=kk_):
                        for cc in range(NCC):
                            last = e.matmul(pA[:, cc * LC:(cc + 1) * LC], kk_[:, L + cc * 128:L + (cc + 1) * 128], q[:, L:LT], start=True, stop=True)
                        return last
                    P.op("pe", cmm, r=[qB[k], kB[k]], w=[pAB])
                    P.op("act", lambda e, s_=s_, pA=pA: e.activation(out=pt[s_][:, 0:NCC * LC], in_=pA[:, 0:NCC * LC], func=AF.Exp, scale=scale), r=[pAB], w=[ptB[s_]])
                    pO, pOB = bank()
                    pD, pDB = bank()

                    def cpv(e, s_=s_, pO=pO, pD=pD):
                        for m_, lhs in ((pO, None), (pD, onesb)):
                            for cc in range(NCC):
                                last = e.matmul(m_[:, 0:LC], vt[:, L // 128 + cc, :] if lhs is None else lhs[:], pt[s_][:, cc * LC:(cc + 1) * LC],
                                                start=(cc == 0), stop=(cc == NCC - 1))
                        return last
                    P.op("pe", cpv, r=[ptB[s_], vtB, cB], w=[pOB, pDB])
                    P.op("dve", lambda e, s_=s_, pD=pD: e.reciprocal(out=rd[s_][:, 0:LC], in_=pD[:, 0:LC]), r=[pDB], w=[rdB[s_]])
                    P.op("dve", lambda e, s_=s_, pO=pO: e.tensor_tensor(out=yo[:, L:LT], in0=pO[:, 0:LC], in1=rd[s_][:, 0:LC], op=ALU.mult), r=[pOB, rdB[s_]], w=[yoB])
                    store_y(yo, cf.NBh + hh, yoB)
                P.barrier()
                P.flush()

        def fft_phase(l):
            CG, CGC = cf.CG, cf.CGC
            with ExitStack() as ph:
                f_ = [SB(ph, "f%d" % c, [128, LT], BF16) for c in range(CGC)]
                yo = [SB(ph, "yo%d" % c, [128, LT], BF16) for c in range(CGC)]
                gd = SB(ph, "gd", [128, CGC, 2 * CG], BF16)
                Z = SB(ph, "Z", [128, L // 128, 2 * CG], BF16)
                tab = [SB(ph, "tab%d" % k, [128, (L // 128) * 2 * cf.kwL], BF16) for k in range(2)]
                fB, yoB = [Buf() for _ in range(CGC)], [Buf() for _ in range(CGC)]
                gdB, ZB = Buf(), Buf()
                tabB = [Buf(), Buf()]
                P.dma("sp", gd[:].rearrange("p c m -> p (c m)"), ctile("dftG", 0, CG * 2 * CG), r=[cstB], w=[gdB])
                tk = 0
                for g in range(cf.NGh):
                    for cc in range(CGC):
                        load_mix(f_[cc], cf.NBh + 3 * cf.Hh + g * CGC + cc, fB[cc])
                    for (n, off, nm, kw) in ((L, 0, "dftL", cf.kwL), (LC, L, "dftC", cf.kwC)):
                        TCH = n // 128
                        for tch in range(TCH):
                            ps, psB = bank()

                            def s1(e, tch=tch, off=off, ps=ps):
                                for cc in range(CGC):
                                    last = e.matmul(ps[:, 0:2 * CG], f_[cc][:, off + tch * 128:off + (tch + 1) * 128], gd[:, cc, :], start=(cc == 0), stop=(cc == CGC - 1))
                                return last
                            P.op("pe", s1, r=fB + [gdB], w=[psB])
                            P.op("act", lambda e, tch=tch, ps=ps: e.activation(out=Z[:, tch, :], in_=ps[:, 0:2 * CG], func=AF.Copy), r=[psB], w=[ZB])
                        tsz = TCH * 2 * kw
                        for kt in range(n // kw):
                            k = tk % 2
                            tk += 1
                            P.dma("sp", tab[k][:, 0:tsz], ctile(nm, kt * 128 * tsz, 128 * tsz), r=[cstB], w=[tabB[k]])
                            tv = tab[k][:, 0:tsz].rearrange("p (c a k) -> p c a k", a=2, k=kw)
                            for mc in range(CGC):
                                ps, psB = bank()

                                def s2(e, mc=mc, ps=ps, tv=tv, TCH=TCH, kw=kw):
                                    for tch in range(TCH):
                                        e.matmul(ps[:, :kw], Z[:, tch, mc * 128:(mc + 1) * 128], tv[:, tch, 0, :], start=(tch == 0), stop=False)
                                        last = e.matmul(ps[:, :kw], Z[:, tch, CG + mc * 128:CG + (mc + 1) * 128], tv[:, tch, 1, :], start=False, stop=(tch == TCH - 1))
                                    return last
                                P.op("pe", s2, r=[ZB, tabB[k]], w=[psB])
                                P.op("act", lambda e, mc=mc, ps=ps, off=off, kt=kt, kw=kw, n=n: e.activation(out=yo[mc][:, off + kt * kw:off + (kt + 1) * kw], in_=ps[:, :kw],
                                                                                                             func=AF.Copy, scale=float((n * CG) ** -0.5)), r=[psB], w=[yoB[mc]])
                    for cc in range(CGC):
                        store_y(yo[cc], cf.NBh + cf.Hh + g * CGC + cc, yoB[cc])
                P.barrier()
                P.flush()

        def merge(l):
            NB, H, FCc = cf.NB, cf.H, cf.FW // 128
            with ExitStack() as ph:
                M = SB(ph, "M", [128, DC, T], BF16)
                MB = Buf()
                with ExitStack() as ph2:
                    Y = SB(ph2, "Y", [128, cf.YH, T], BF16)
                    YB = Buf()
                    P.dma("sp", Y[:], yhd4[:, cur['vc'], :, :], r=[yBv[cur['vc']]], w=[YB])
                    wl = [SB(ph2, "wl%d" % k, [128, (NB + H + FCc) * 128], BF16) for k in range(2)]
                    gt = [SB(ph2, "gt%d" % k, [128, 3, T], BF16) for k in range(2)]
                    t1 = SB(ph2, "t1", [128, 2, 512])
                    t2 = SB(ph2, "t2", [128, 2, 512])
                    wlB, gtB, t1B, t2B = ([Buf(), Buf()] for _ in range(4))
                    ysrc = []
                    for b_ in range(NB):
                        ysrc.append((b_ // cf.NBh) * cf.YH + b_ % cf.NBh)
                    for h_ in range(H):
                        ysrc.append((h_ // cf.Hh) * cf.YH + cf.NBh + h_ % cf.Hh)
                    for c_ in range(FCc):
                        ysrc.append((c_ // cf.FCh) * cf.YH + cf.NBh + cf.Hh + c_ % cf.FCh)
                    grp = ((0, NB), (NB, NB + H), (NB + H, NB + H + FCc))
                    cnt = 0
                    for c in range(DC):
                        k = c % 2
                        P.dma("sp", wl[k][:, 0:NB * 128], wtile(l, "w_out_lru", c), r=[wB[l]], w=[wlB[k]])
                        P.dma("sp", wl[k][:, NB * 128:(NB + H) * 128], wtile(l, "w_out_na", c), r=[wB[l]], w=[wlB[k]])
                        P.dma("sp", wl[k][:, (NB + H) * 128:], wtile(l, "w_out_fft", c), r=[wB[l]], w=[wlB[k]])
                        P.dma("sp", gt[k][:], GAT().rearrange("(g c p) t -> p g c t", g=3, p=128)[:, :, c, :], r=[gatBv[cur["vc"]]], w=[gtB[k]])
                        for ti, (t0, tn) in enumerate(cf.tiles):
                            pss = [bank() for _ in range(3)]

                            def mm(e, k=k, t0=t0, tn=tn, pss=pss):
                                for gi, (a, b) in enumerate(grp):
                                    for kk in range(a, b):
                                        last = e.matmul(pss[gi][0][:, :tn], wl[k][:, kk * 128:(kk + 1) * 128], Y[:, ysrc[kk], t0:t0 + tn], start=(kk == a), stop=(kk == b - 1))
                                return last
                            P.op("pe", mm, r=[wlB[k], YB], w=[p[1] for p in pss])
                            s_ = cnt % 2
                            cnt += 1
                            P.op("dve", lambda e, s_=s_, k=k, t0=t0, tn=tn, p=pss[0][0]: e.tensor_tensor(out=t1[:, s_, :tn], in0=p[:, :tn], in1=gt[k][:, 0, t0:t0 + tn], op=ALU.mult),
                                 r=[pss[0][1], gtB[k]], w=[t1B[s_]])
                            P.op("dve", lambda e, s_=s_, k=k, t0=t0, tn=tn, p=pss[1][0]: e.tensor_tensor(out=t2[:, s_, :tn], in0=p[:, :tn], in1=gt[k][:, 1, t0:t0 + tn], op=ALU.mult),
                                 r=[pss[1][1], gtB[k]], w=[t2B[s_]])
                            P.op("dve", lambda e, s_=s_, tn=tn: e.tensor_tensor(out=t1[:, s_, :tn], in0=t1[:, s_, :tn], in1=t2[:, s_, :tn], op=ALU.add), r=[t1B[s_], t2B[s_]], w=[t1B[s_]])
                            P.op("dve", lambda e, s_=s_, k=k, t0=t0, tn=tn, p=pss[2][0]: e.tensor_tensor(out=t2[:, s_, :tn], in0=p[:, :tn], in1=gt[k][:, 2, t0:t0 + tn], op=ALU.mult),
                                 r=[pss[2][1], gtB[k], t1B[s_]], w=[t2B[s_]])
                            P.op("dve", lambda e, s_=s_, c=c, t0=t0, tn=tn: e.tensor_tensor(out=M[:, c, t0:t0 + tn], in0=t1[:, s_, :tn], in1=t2[:, s_, :tn], op=ALU.add),
                                 r=[t1B[s_], t2B[s_]], w=[MB])
                    P.barrier()
                    P.flush()
                wo = [SB(ph, "wo%d" % k, [128, DC * 128], BF16) for k in range(2)]
                xc = SB(ph, "xc", [128, 2, 512])
                woB, xcB = [Buf(), Buf()], [Buf(), Buf()]
                cnt = 0
                for c in range(DC):
                    k = c % 2
                    P.dma("sp", wo[k][:], wtile(l, "w_o", c), r=[wB[l]], w=[woB[k]])
                    for ti, (t0, tn) in enumerate(cf.tiles):
                        ps, psB = bank()

                        def mm(e, k=k, t0=t0, tn=tn, ps=ps):
                            for kc in range(DC):
                                last = e.matmul(ps[:, :tn], wo[k][:, kc * 128:(kc + 1) * 128], M[:, kc, t0:t0 + tn], start=(kc == 0), stop=(kc == DC - 1))
                            return last
                        P.op("pe", mm, r=[woB[k], MB], w=[psB])
                        s_ = cnt % 2
                        cnt += 1
                        x_update(ph, 1, c, ti, t0, tn, ps, psB, xc, xcB, s_)
                P.barrier()
                P.flush()

        def final_norm():
            with ExitStack() as ph:
                xt = SB(ph, "xt", [128, DC, 512])
                sq = SB(ph, "sq", [128, 2, 512])
                rstd = SB(ph, "rstd", [128, 512])
                xtB, rB = Buf(), Buf()
                sqB = [Buf(), Buf()]
                for ti, (t0, tn) in enumerate(cf.tiles):
                    if t0 >= TL:
                        continue
                    P.dma("sp", xt[:, :, :tn], XD().rearrange("(c p) t -> p c t", p=128)[:, :, t0:t0 + tn],
                          r=[b for c in range(DC) for b in xr(c, ti)], w=[xtB])
                    sumsq_rstd(ph, xt, tn, xtB, rstd, rB, (sq, sqB))
                    for c in range(DC):
                        P.op("dve", lambda e, c=c, tn=tn: e.scalar_tensor_tensor(out=xt[:, c, :tn], in0=xt[:, c, :tn], scalar=V("final_g", c), in1=rstd[:, :tn],
                                                                                 op0=ALU.mult, op1=ALU.mult), r=[xtB, rB, cB], w=[xtB])
                    ev = P.dma("sp", out_d[cur["vc"] * D:(cur["vc"] + 1) * D, :].rearrange("(c p) t -> p c t", p=128)[:, :, t0:t0 + tn], xt[:, :, :tn], r=[xtB])
                    for e_ in P.ENG:
                        P._wait(e_, ev)
                P.barrier()
                P.flush()

        def layers():
            for l in range(DEPTH):
                for vc in range(NVC):
                    cur['vc'] = vc
                    if cf.stop < 2:
                        return
                    load_mods(l)
                    if cf.stop < 3:
                        return
                    ffn(l, 0, "ffn1_w13", "ffn1_w2")
                    if cf.dbg and l == 0 and vc == cf.dbgvc:
                        P.dma("sp", dbg["d_x1"], XD(), r=[b for c in range(DC) for ti in range(len(cf.tiles)) for b in xr(c, ti)])
                    if cf.stop < 4:
                        continue
                    in_proj(l)
                if cf.stop < 5:
                    return
                for b_ in range(NBAT):
                    cur['b'] = b_
                    lru_phase(l)
                    if cf.stop < 6:
                        continue
                    na_phase(l)
                    if cf.stop < 7:
                        continue
                    fft_phase(l)
                if cf.dbg and l == 0:
                    P.dma("pool", dbg["d_y"], yhd.ap()[cf.dbgvc * cf.YH * 128:(cf.dbgvc + 1) * cf.YH * 128, :], r=[yBv[cf.dbgvc]])
                if cf.stop < 8:
                    return
                for vc in range(NVC):
                    cur['vc'] = vc
                    load_mods(l)
                    merge(l)
                    if cf.stop < 9:
                        continue
                    ffn(l, 2, "ffn2_w13", "ffn2_w2")
                    if cf.dbg and l == 0 and vc == cf.dbgvc:
                        P.dma("sp", dbg["d_x2"], XD(), r=[b for c in range(DC) for ti in range(len(cf.tiles)) for b in xr(c, ti)])
        layers()
        for vc in range(NVC):
            cur['vc'] = vc
            final_norm()
    return nc


_CACHE = {}


def run(cf, inputs):
    maps = host_inputs(cf, inputs)
    key = (cf.D, cf.L, cf.LC, cf.NG, cf.DEPTH, cf.dbg, cf.stop)
    if key not in _CACHE:
        _CACHE[key] = build(cf)
    res = run_bass_kernel_spmd(_CACHE[key], maps, core_ids=list(range(cf.NCORE)))
    out = np.zeros((4, cf.L, cf.D), np.float32)
    for core in range(cf.NCORE):
        o = res.results[core]["out"].reshape(cf.NVC, cf.D, cf.TL)
        for v in range(cf.NVC):
            vc = core * cf.NVC + v
            b, s = vc // 2, vc % 2
            out[b, s * cf.TL:(s + 1) * cf.TL, :] = o[v].T
    return out, res


def kernel(**inputs):
    cf = Cfg()
    out, _ = run(cf, inputs)
    return out
```

```python
import numpy as np
import os
import concourse.bass as bass
import concourse.mybir as mybir
from concourse.bass import ds
from concourse.bass_utils import run_bass_kernel_spmd
from contextlib import ExitStack

F32 = mybir.dt.float32
BF16 = mybir.dt.bfloat16
AF = mybir.ActivationFunctionType
ALU = mybir.AluOpType
EPS = 1e-6
NEG = -30000.0
PIECE = 16384 * 1024
WNAMES = ("ffn1_w13", "ffn1_w2", "w_in", "w_out_lru", "w_out_na", "w_out_fft", "w_o", "ffn2_w13", "ffn2_w2")


class Cfg:
    def __init__(s, D=4096, L=2048, LC=256, NG=4, DEPTH=2, dbg=False, stop=99):
        s.D, s.L, s.LC, s.NG, s.DEPTH, s.dbg = D, L, LC, NG, DEPTH, dbg
        s.stop = stop
        s.dbgvc = 3
        s.NCORE = 4
        s.NVC = 8 // s.NCORE
        s.NBAT = 4 // s.NCORE
        s.DC = D // 128
        s.LW = D // 4
        s.NB = s.LW // 128
        s.NAW = D // 2
        s.H = s.NAW // 128
        s.FW = D // 4
        s.CG = s.FW // NG
        s.CGC = s.CG // 128
        s.DFF = 3 * D // 2
        s.FC = s.DFF // 128
        s.INW = s.LW + 3 * s.NAW + s.FW + 3 * D
        s.NIN = s.INW // 128
        s.TL, s.TC = L // 2, LC // 2
        s.T = s.TL + s.TC
        s.LT = L + LC
        s.ROWS = L // 64
        s.NBh, s.Hh, s.NGh = s.NB, s.H, NG
        s.FCh = s.NGh * s.CGC
        s.MH = s.NBh + 3 * s.Hh + s.FCh
        s.YH = s.NBh + s.Hh + s.FCh
        s.MIXC = s.MH
        s.J = 9 * s.DC
        s.tiles = []
        t = 0
        while t < s.TL:
            n = min(512, s.TL - t)
            s.tiles.append((t, n))
            t += n
        s.tiles.append((s.TL, s.TC))
        s.wshape = {"ffn1_w13": (D, 2 * s.DFF), "ffn1_w2": (s.DFF, D), "w_in": (D, s.INW), "w_out_lru": (s.LW, D),
                    "w_out_na": (s.NAW, D), "w_out_fft": (s.FW, D), "w_o": (D, D), "ffn2_w13": (D, 2 * s.DFF),
                    "ffn2_w2": (s.DFF, D)}
        s.woff = {}
        off = 0
        for nm in WNAMES:
            K, N = s.wshape[nm]
            s.woff[nm] = off
            off += K * N
        s.EL = off
        assert s.EL % 16384 == 0
        s.kwL, s.kwC = min(512, L), min(512, LC)
        s.coff = {"dftL": 0, "dftC": L * 2 * L, "dftG": L * 2 * L + LC * 2 * LC}
        ec = s.coff["dftG"] + s.CG * 2 * s.CG
        s.EC = (ec + 16383) // 16384 * 16384
        c = 0
        s.vo = {}
        for nm, n in (("norm_g", DEPTH * 3 * s.DC), ("final_g", s.DC), ("conv_w", DEPTH * s.NBh * 4),
                      ("conv_b", DEPTH * s.NBh), ("ba", DEPTH * 2 * s.NBh), ("bi", DEPTH * 2 * s.NBh),
                      ("lam", DEPTH * 2 * s.NBh), ("permT", 128), ("bada", DEPTH * s.J), ("cv", s.DC * 5)):
            s.vo[nm] = c
            c += n
        s.NV = c


def pieces(E):
    out = []
    a = 0
    while a < E:
        b = min(E, a + PIECE)
        out.append((a, b))
        a = b
    return out


def tile_w(W):
    K, N = W.shape
    return np.ascontiguousarray(W.reshape(K // 128, 128, N // 128, 128).transpose(2, 1, 0, 3)).reshape(-1)


def shard_flat(flat, E):
    parts = [[] for _ in range(8)]
    for a, b in pieces(E):
        n = (b - a) // 8
        for r in range(8):
            parts[r].append(flat[a + r * n: a + (r + 1) * n])
    return [np.concatenate(p).reshape(-1, 2048) for p in parts]


def pvec(v):
    return np.ascontiguousarray(np.asarray(v, np.float32).reshape(-1, 128).T)


def host_consts(cf):
    L, LC, CG = cf.L, cf.LC, cf.CG
    flat = np.zeros(cf.EC, np.float32)

    def dft(n, kw):
        t = np.arange(n, dtype=np.int64)
        ang = 2.0 * np.pi * ((t[:, None] * t[None, :]) % n).astype(np.float64) / n
        tab = np.stack([np.cos(ang), -np.sin(ang)], 1)
        tab = tab.reshape(n // 128, 128, 2, n // kw, kw).transpose(3, 1, 0, 2, 4)
        return np.ascontiguousarray(tab).reshape(-1).astype(np.float32)
    flat[0: L * 2 * L] = dft(L, cf.kwL)
    flat[cf.coff["dftC"]: cf.coff["dftC"] + LC * 2 * LC] = dft(LC, cf.kwC)
    c = np.arange(CG, dtype=np.int64)
    ang = 2.0 * np.pi * ((c[:, None] * c[None, :]) % CG).astype(np.float64) / CG
    g = np.concatenate([np.cos(ang), np.sin(ang)], 1)
    g = g.reshape(cf.CGC, 128, 2 * CG).transpose(1, 0, 2)
    flat[cf.coff["dftG"]: cf.coff["dftG"] + CG * 2 * CG] = np.ascontiguousarray(g).reshape(-1)
    return flat


def rope_tables(cf, s):
    t = np.arange(cf.TL) + s * cf.TL
    row = (t // 64).astype(np.float64)
    col = (t % 64).astype(np.float64)
    d = np.arange(128)
    i = (d % 64) % 32
    inv = 10000.0 ** (-(2.0 * i) / 64.0)
    pos = np.where((d < 64)[:, None], row[None, :], col[None, :])
    ang = pos * inv[:, None]
    sgn = np.where((d % 64) < 32, -1.0, 1.0)[:, None]
    return np.concatenate([np.cos(ang), np.sin(ang) * sgn], 1).astype(np.float32)


def perm_T():
    m = np.arange(128)
    src = (m // 64) * 64 + ((m % 64) + 32) % 64
    PT = np.zeros((128, 128), np.float32)
    PT[src, m] = 1.0
    return PT


def bias_tiles(rpb_h):
    cols = np.arange(64)
    cs = np.clip(cols - 8, 0, 48)
    col_in = (cols[None, :] >= cs[:, None]) & (cols[None, :] < cs[:, None] + 16)
    cidx = np.clip(cols[None, :] - cols[:, None] + 15, 0, 30)

    def B(dl):
        if abs(dl) > 7:
            return np.full((64, 64), NEG, np.float32)
        b = rpb_h[dl + 7][cidx]
        b = np.where(col_in, b, NEG)
        return b.T.astype(np.float32)

    def U(d0, mask=()):
        t = np.zeros((128, 128), np.float32)
        for i in range(2):
            for j in range(2):
                blk = B(d0 + i - j)
                if (i, j) in mask:
                    blk = np.full((64, 64), NEG, np.float32)
                t[i * 64:(i + 1) * 64, j * 64:(j + 1) * 64] = blk
        return t
    tl = [U(d0) for d0 in (-6, -4, -2, 0, 2, 4, 6)]
    tl.append(U(-4, mask=((0, 1),)))
    tl.append(U(4, mask=((0, 0), (1, 0), (1, 1))))
    return np.concatenate(tl, 1)


def host_inputs(cf, inp):
    D, DC, DEPTH = cf.D, cf.DC, cf.DEPTH
    f = lambda k: np.asarray(inp[k], np.float32)
    x, c, ctx, c_ctx = f("x"), f("c"), f("ctx"), f("c_ctx")
    w_ada, b_ada = f("w_ada"), f("b_ada")
    wfl = []
    for l in range(DEPTH):
        flat = np.empty(cf.EL, np.float32)
        for nm in WNAMES:
            K, N = cf.wshape[nm]
            flat[cf.woff[nm]: cf.woff[nm] + K * N] = tile_w(f(nm)[l])
        wfl.append(flat.reshape(-1, 2048))
    cv = np.concatenate([c, c_ctx[None, :]], 0)
    cvT = np.ascontiguousarray(cv.T.reshape(DC, 128, 5).transpose(1, 0, 2)).reshape(128, DC * 5)
    PT = perm_T()
    m = {}
    for l in range(DEPTH):
        m["w%d" % l] = wfl[l]
        m["wada%d" % l] = np.ascontiguousarray(w_ada[l].reshape(DC, 128, cf.J, 128).transpose(2, 1, 0, 3)).reshape(cf.J * 128, DC * 128)
    m["cst"] = host_consts(cf).reshape(-1, 2048)
    vec = np.zeros((128, cf.NV), np.float32)

    def put(nm, arr):
        arr = np.asarray(arr, np.float32)
        vec[:, cf.vo[nm]: cf.vo[nm] + arr.shape[1]] = arr
    put("norm_g", pvec(f("norm_g").reshape(-1)))
    put("final_g", pvec(f("final_g")))
    put("conv_w", pvec(f("conv_w").reshape(DEPTH, 4, cf.NB, 128).transpose(0, 2, 1, 3).reshape(-1)))
    put("conv_b", pvec(f("conv_b").reshape(-1)))
    put("ba", pvec(f("lru_ba").reshape(-1)))
    put("bi", pvec(f("lru_bi").reshape(-1)))
    put("lam", pvec(f("lru_lam").reshape(-1)))
    put("permT", PT)
    put("bada", pvec(b_ada.reshape(-1)))
    m["vec"] = vec
    g = np.stack([f("lru_wa"), f("lru_wi")], 3)
    m["lruw"] = np.ascontiguousarray(g.transpose(4, 0, 1, 2, 3, 5)).reshape(128, -1)
    m["rope"] = np.concatenate([rope_tables(cf, 0), rope_tables(cf, 1)], 1)
    rpb = f("na_rpb")
    nab = np.stack([np.stack([bias_tiles(rpb[l, hh]) for hh in range(cf.H)], 0) for l in range(DEPTH)], 0)
    m["nab"] = np.ascontiguousarray(nab).reshape(DEPTH * cf.H * 128, 9 * 128)
    maps = []
    for core in range(cf.NCORE):
        mc = dict(m)
        xin = np.empty((cf.NVC, D, cf.T), np.float32)
        for v in range(cf.NVC):
            vc = core * cf.NVC + v
            b, s = vc // 2, vc % 2
            xin[v, :, :cf.TL] = x[b, s * cf.TL:(s + 1) * cf.TL, :].T
            xin[v, :, cf.TL:] = ctx[b, s * cf.TC:(s + 1) * cf.TC, :].T
        mc["xin"] = xin.reshape(cf.NVC * D, cf.T)
        cvl = np.zeros((5, D), np.float32)
        for bb in range(cf.NBAT):
            cvl[bb] = c[core * cf.NBAT + bb]
        cvl[4] = c_ctx
        vecc = vec.copy()
        vecc[:, cf.vo["cv"]: cf.vo["cv"] + DC * 5] = np.ascontiguousarray(cvl.T.reshape(DC, 128, 5).transpose(1, 0, 2)).reshape(128, DC * 5)
        mc["vec"] = vecc
        maps.append(mc)
    return maps


class Buf:
    __slots__ = ("w", "rd", "excl")

    def __init__(self, excl=False):
        self.w = None
        self.rd = {}
        self.excl = excl


class Rec:
    def __init__(self):
        self.calls = []

    def __getattr__(self, name):
        def f(*a, **k):
            self.calls.append((name, a, k))
            return self
        return f


def replay(e, calls):
    last = None
    for name, a, k in calls:
        last = getattr(e, name)(*a, **k)
    return last


class Prog:
    ENG = ("pe", "act", "dve", "pool", "sp")
    NDS = 6
    NCS = 4

    def __init__(self, nc, es):
        self.nc = nc
        self.q = {e: [] for e in self.ENG}
        self.sem = {e: es.enter_context(nc.semaphore("s_" + e)) for e in self.ENG}
        self.cnt = {e: 0 for e in self.ENG}
        self.seen = {e: {} for e in self.ENG}
        self.dsem = {}
        for qn in ("sp", "pool"):
            self.dsem[qn] = [[es.enter_context(nc.semaphore("d_%s%d" % (qn, i))), 0] for i in range(self.NDS)]
        self.dk = {"sp": 0, "pool": 0}
        self.csem = [[es.enter_context(nc.semaphore("c_%d" % i)), 0] for i in range(self.NCS)]
        self.ck = 0

    def _wait(self, eng, ev):
        if ev is None:
            return
        sem, val = ev
        k = id(sem)
        if self.seen[eng].get(k, 0) >= val:
            return
        self.seen[eng][k] = val
        self.q[eng].append(lambda e, s=sem, v=val: e.wait_ge(s, v))

    def _deps(self, eng, r, w):
        for b in r:
            self._wait(eng, b.w)
        for b in w:
            self._wait(eng, b.w)
            for ev in b.rd.values():
                self._wait(eng, ev)

    def _mark(self, ev, r, w):
        for b in w:
            b.w = ev
            b.rd = {}
        for b in r:
            if b.w is not ev:
                b.rd[id(ev[0])] = ev

    def op(self, eng, fn, r=(), w=()):
        if any(b.excl for b in r):
            w = list(w) + [b for b in r if b.excl]
            r = [b for b in r if not b.excl]
        self._deps(eng, r, w)
        self.cnt[eng] += 1
        s = self.sem[eng]
        ev = (s, self.cnt[eng])
        rec = Rec()
        fn(rec)
        self.q[eng].append(lambda e, calls=rec.calls, s=s: replay(e, calls).then_inc(s, 1))
        self._mark(ev, r, w)
        return ev

    def dma(self, qn, out, in_, r=(), w=(), fn=None):
        self._deps(qn, r, w)
        slot = self.dsem[qn][self.dk[qn] % self.NDS]
        self.dk[qn] += 1
        self._wait(qn, (slot[0], slot[1]))
        slot[1] += 16
        ev = (slot[0], slot[1])
        if fn is None:
            self.q[qn].append(lambda e, o=out, i=in_, s=slot[0]: e.dma_start(out=o, in_=i).then_inc(s, 16))
        else:
            self.q[qn].append(lambda e, f=fn, s=slot[0]: f(e).then_inc(s, 16))
        self._mark(ev, r, w)
        return ev

    def coll(self, groups, in_ap, out_ap, r=(), w=()):
        self._deps("pool", r, w)
        slot = self.csem[self.ck % self.NCS]
        self.ck += 1
        self._wait("pool", (slot[0], slot[1]))
        slot[1] += 1
        ev = (slot[0], slot[1])
        self.q["pool"].append(lambda e, i=in_ap, o=out_ap, s=slot[0], g=groups: e.collective_compute(
            "AllGather", ALU.bypass, replica_groups=g, ins=[i], outs=[o]).then_inc(s))
        self._mark(ev, r, w)
        return ev

    def pid(self, e, kind):
        if kind not in self._pidcache:
            v = e.partition_id()
            self._pidcache[kind] = (v % 2) if kind == "s" else (v // 2)
        return self._pidcache[kind]

    def all_events(self):
        evs = [(self.sem[e], self.cnt[e]) for e in self.ENG if self.cnt[e] > 0]
        for qn in self.dsem:
            for s, c in self.dsem[qn]:
                if c > 0:
                    evs.append((s, c))
        for s, c in self.csem:
            if c > 0:
                evs.append((s, c))
        return evs

    def barrier(self):
        evs = self.all_events()
        for e in self.ENG:
            for ev in evs:
                self._wait(e, ev)

    def flush(self):
        with self.nc.Block() as block:
            for name, deco in (("pe", block.tensor), ("act", block.scalar), ("dve", block.vector),
                               ("pool", block.gpsimd), ("sp", block.sync)):
                ops = self.q[name]
                self.q[name] = []

                def run(e, ops=ops):
                    self._pidcache = {}
                    for f in ops:
                        f(e)
                deco(run)


def build(cf):
    nc = bass.Bass("TRN2", target_bir_lowering=False)
    D, DC, T, TL, TC, L, LC, LT = cf.D, cf.DC, cf.T, cf.TL, cf.TC, cf.L, cf.LC, cf.LT
    DEPTH, FC, J = cf.DEPTH, cf.FC, cf.J
    FH = FC // 2
    ALL8 = [list(range(8))]
    PAIRS = [[0, 1], [2, 3], [4, 5], [6, 7]]

    def din(name, shape):
        return nc.dram_tensor(name, list(shape), F32, kind="ExternalInput").ap()
    NVC, NBAT = cf.NVC, cf.NBAT
    xin = din("xin", [NVC * D, T])
    w_in_d = [din("w%d" % l, [cf.EL // 2048, 2048]) for l in range(DEPTH)]
    wada_d = [din("wada%d" % l, [J * 128, DC * 128]) for l in range(DEPTH)]
    cst_d = din("cst", [cf.EC // 2048, 2048])
    vec_d = din("vec", [128, cf.NV])
    lruw_d = din("lruw", [128, DEPTH * 2 * cf.NBh * 2 * 128])
    rope_d = din("rope", [128, 4 * TL])
    nab_d = din("nab", [DEPTH * cf.Hh * 128, 9 * 128])
    out_d = nc.dram_tensor("out", [NVC * D, TL], F32, kind="ExternalOutput").ap()

    def dint(name, shape, dt):
        return nc.dram_tensor(name, list(shape), dt)
    xd_all = dint("xd", [NVC * D, T], F32).ap()

    wfull = [{nm: dint("wf%d_%s" % (l, nm), [cf.wshape[nm][0] * cf.wshape[nm][1] // 2048, 2048], BF16) for nm in WNAMES} for l in range(DEPTH)]
    cfull = dint("cfull", [cf.EC // 2048, 2048], BF16)
    mixl = dint("mixl", [NVC * cf.MH * 128, T], BF16)
    gat_all = dint("gat", [NVC * 3 * DC * 128, T], BF16).ap()
    yhd = dint("yhd", [NVC * cf.YH * 128, T], BF16)
    dbg = {}
    if cf.dbg:
        for nm, shp in (("d_x1", [D, T]), ("d_mix", [cf.MH * 128, T]), ("d_y", [cf.YH * 128, T]),
                        ("d_x2", [D, T]), ("d_mod", [128, 18 * DC])):
            dbg[nm] = nc.dram_tensor(nm, shp, F32, kind="ExternalOutput").ap()

    def wtile(l, nm, n, lo=0, hi=None):
        K, N = cf.wshape[nm]
        sz = K * 128
        off = n * sz
        ap = wfull[l][nm].ap().rearrange("r e -> (r e)")[off: off + sz].rearrange("(p f) -> p f", p=128)
        return ap if hi is None else ap[:, lo:hi]

    def ctile(nm, off, sz, p=128):
        o = cf.coff[nm] + off
        return cfull.ap().rearrange("r e -> (r e)")[o: o + sz].rearrange("(p f) -> p f", p=p)

    with ExitStack() as es:
        P = Prog(nc, es)

        sbn = [0]

        def SB(st, name, shape, dt=F32):
            sbn[0] += 1
            return st.enter_context(nc.sbuf_tensor("%s_s%d" % (name, sbn[0]), list(shape), dt))
        banks = [es.enter_context(nc.psum_tensor("ps%d" % i, [128, 512], F32)) for i in range(8)]
        bankB = [Buf(excl=True) for _ in range(8)]
        bk = [0]

        def bank():
            i = bk[0] % 8
            bk[0] += 1
            return banks[i], bankB[i]
        vec = SB(es, "vec", [128, cf.NV])
        onesf = SB(es, "onesf", [128, 128])
        onesb = SB(es, "onesb", [128, 128], BF16)
        identb = SB(es, "identb", [128, 128], BF16)
        permb = SB(es, "permb", [128, 128], BF16)
        mt = SB(es, "mt", [128, 2, 3, 3, DC])
        nsp8 = SB(es, "nsp8", [128, DEPTH * 2 * cf.NBh])
        cB, mtB, xBall = Buf(), Buf(), Buf()
        xBv = [[[Buf() for _ in cf.tiles] for _ in range(DC)] for _ in range(NVC)]
        cur = {'vc': 0}
        XD = lambda: xd_all[cur['vc'] * D:(cur['vc'] + 1) * D, :]
        GAT = lambda: gat_all[cur['vc'] * 3 * DC * 128:(cur['vc'] + 1) * 3 * DC * 128, :]
        mlall = [SB(es, 'mlall%d' % l, [128, 5, J]) for l in range(DEPTH)]
        mlB = Buf()
        vo = cf.vo

        def V(nm, a, n=1):
            return vec[:, vo[nm] + a: vo[nm] + a + n]

        with ExitStack() as ph:
            tmpf = SB(ph, "tmpf", [128, 128])
            e_ = SB(ph, "e_", [128, DEPTH * 2 * cf.NBh])
            p_ = SB(ph, "p_", [128, DEPTH * 2 * cf.NBh])
            t_ = SB(ph, "t_", [128, DEPTH * 2 * cf.NBh])
            P.dma("sp", vec[:], vec_d, w=[cB])
            P.op("pool", lambda e: e.memset(onesf[:], 1.0), w=[cB])
            P.op("pool", lambda e: e.memset(onesb[:], 1.0), w=[cB])
            P.op("pool", lambda e: e.affine_select(out=tmpf[:], in_=onesf[:], pattern=[[-1, 128]], compare_op=ALU.is_equal,
                                                   fill=0.0, base=0, channel_multiplier=1), r=[cB], w=[cB])
            P.op("dve", lambda e: e.tensor_copy(out=identb[:], in_=tmpf[:]), r=[cB], w=[cB])
            P.op("dve", lambda e: e.tensor_copy(out=permb[:], in_=V("permT", 0, 128)), r=[cB], w=[cB])
            nl = DEPTH * 2 * cf.NBh
            P.op("act", lambda e: e.activation(out=e_[:], in_=V("lam", 0, nl), func=AF.Exp, scale=-1.0), r=[cB], w=[cB])
            P.op("dve", lambda e: e.tensor_scalar(out=p_[:], in0=e_[:], scalar1=-0.2, scalar2=0.25, op0=ALU.mult, op1=ALU.add), r=[cB], w=[cB])
            for cst in (1.0 / 3.0, 0.5, 1.0):
                P.op("dve", lambda e: e.tensor_tensor(out=t_[:], in0=e_[:], in1=p_[:], op=ALU.mult), r=[cB], w=[cB])
                P.op("dve", lambda e, c=cst: e.tensor_scalar(out=p_[:], in0=t_[:], scalar1=-1.0, scalar2=c, op0=ALU.mult, op1=ALU.add), r=[cB], w=[cB])
            P.op("dve", lambda e: e.tensor_tensor(out=t_[:], in0=e_[:], in1=p_[:], op=ALU.mult), r=[cB], w=[cB])
            P.op("dve", lambda e: e.tensor_scalar(out=nsp8[:], in0=t_[:], scalar1=-8.0, scalar2=None, op0=ALU.mult), r=[cB], w=[cB])
            for r0 in range(0, NVC * D, 512):
                P.dma("sp", xd_all[r0:r0 + 512, :], xin[r0:r0 + 512, :], w=[xBall])
            wB = [Buf() for _ in range(DEPTH)]
            cstB = Buf()

            def cast(src_d, so, full, E, B_):
                R = E // 2048
                r0 = 0
                while r0 < R:
                    r1 = min(R, r0 + 1024)
                    P.dma("pool", full.ap()[r0:r1, :], src_d[so + r0:so + r1, :], w=[B_])
                    r0 = r1
            cast(cst_d, 0, cfull, cf.EC, cstB)
            for l in range(DEPTH):
                for nm in WNAMES:
                    cast(w_in_d[l], cf.woff[nm] // 2048, wfull[l][nm], cf.wshape[nm][0] * cf.wshape[nm][1], wB[l])
            P.barrier()
            P.flush()

        with ExitStack() as ph:
            sc = SB(ph, "sc", [128, DC, 8])
            wt = [SB(ph, "wt%d" % i, [128, DC * 128]) for i in range(2)]
            wtB = [Buf(), Buf()]
            scB = Buf()
            P.op("pool", lambda e: e.memset(sc[:], 0.0), w=[scB])
            P.op("act", lambda e: e.activation(out=sc[:, :, 0:5], in_=V("cv", 0, DC * 5).rearrange("p (c o) -> p c o", o=5), func=AF.Silu), r=[cB, scB], w=[scB])
            for l in range(DEPTH):
                for j in range(J):
                    k = (l * J + j) % 2
                    P.dma("sp", wt[k][:], wada_d[l][j * 128:(j + 1) * 128, :], w=[wtB[k]])
                    ps, psB = bank()

                    def mm(e, k=k, ps=ps):
                        for kc in range(DC):
                            last = e.matmul(ps[:, 0:8], wt[k][:, kc * 128:(kc + 1) * 128], sc[:, kc, :],
                                            start=(kc == 0), stop=(kc == DC - 1))
                        return last
                    P.op("pe", mm, r=[wtB[k], scB], w=[psB])
                    P.op("dve", lambda e, ps=ps, l=l, j=j: e.tensor_scalar(out=mlall[l][:, :, j], in0=ps[:, 0:5], scalar1=V("bada", l * J + j),
                                                                            scalar2=None, op0=ALU.add), r=[psB, cB], w=[mlB])
            P.barrier()
            P.flush()

        def load_mods(l):
            bb = cur['vc'] // 2
            for ci, row in enumerate((bb, 4)):
                mv = mlall[l][:, row, :]
                for i in range(3):
                    sh = mv[:, (3 * i) * DC:(3 * i + 1) * DC]
                    scl = mv[:, (3 * i + 1) * DC:(3 * i + 2) * DC]
                    gt = mv[:, (3 * i + 2) * DC:(3 * i + 3) * DC]
                    g = V("norm_g", (l * 3 + i) * DC, DC)
                    P.op("dve", lambda e, ci=ci, i=i, scl=scl, g=g: e.scalar_tensor_tensor(out=mt[:, ci, 0, i, :], in0=scl, scalar=1.0, in1=g,
                                                                                           op0=ALU.add, op1=ALU.mult), r=[mlB, cB], w=[mtB])
                    P.op("dve", lambda e, ci=ci, i=i, sh=sh: e.tensor_copy(out=mt[:, ci, 1, i, :], in_=sh), r=[mlB], w=[mtB])
                    P.op("dve", lambda e, ci=ci, i=i, gt=gt: e.tensor_scalar(out=mt[:, ci, 2, i, :], in0=gt, scalar1=(1.0 if i == 1 else 0.5),
                                                                             scalar2=None, op0=ALU.mult), r=[mlB], w=[mtB])

        def xr(c, ti):
            return [xBv[cur['vc']][c][ti], xBall]

        def sumsq_rstd(ph, xt, tn, xtB, rstd, rB, scratch):
            sq, sqB = scratch
            ps, psB = bank()
            for c in range(DC):
                k = c % 2
                P.op("act", lambda e, c=c, k=k: e.activation(out=sq[:, k, :tn], in_=xt[:, c, :tn], func=AF.Square), r=[xtB], w=[sqB[k]])
                P.op("pe", lambda e, c=c, k=k, ps=ps: e.matmul(ps[:, :tn], onesf[:], sq[:, k, :tn], start=(c == 0), stop=(c == DC - 1)),
                     r=[sqB[k], cB], w=[psB])
            P.op("act", lambda e, ps=ps: e.activation(out=rstd[:, :tn], in_=ps[:, :tn], func=AF.Sqrt, scale=1.0 / D, bias=EPS), r=[psB], w=[rB])
            P.op("dve", lambda e: e.reciprocal(out=rstd[:, :tn], in_=rstd[:, :tn]), r=[rB], w=[rB])

        def norm_mod(i, h, hB):
            with ExitStack() as ph:
                xt = SB(ph, "xt", [128, DC, 512])
                sq = SB(ph, "sq", [128, 2, 512])
                rstd = SB(ph, "rstd", [128, 512])
                tmp = SB(ph, "tmp", [128, 2, 512])
                xtB, rB = Buf(), Buf()
                sqB, tmpB = [Buf(), Buf()], [Buf(), Buf()]
                for ti, (t0, tn) in enumerate(cf.tiles):
                    ci = 1 if t0 >= TL else 0
                    P.dma("sp", xt[:, :, :tn], XD().rearrange("(c p) t -> p c t", p=128)[:, :, t0:t0 + tn],
                          r=[b for c in range(DC) for b in xr(c, ti)], w=[xtB])
                    sumsq_rstd(ph, xt, tn, xtB, rstd, rB, (sq, sqB))
                    for c in range(DC):
                        k = c % 2
                        P.op("dve", lambda e, c=c, k=k, ci=ci: e.scalar_tensor_tensor(out=tmp[:, k, :tn], in0=xt[:, c, :tn], scalar=mt[:, ci, 0, i, c:c + 1],
                                                                                      in1=rstd[:, :tn], op0=ALU.mult, op1=ALU.mult),
                             r=[xtB, rB, mtB], w=[tmpB[k]])
                        P.op("act", lambda e, c=c, k=k, ci=ci, t0=t0: e.activation(out=h[:, c, t0:t0 + tn], in_=tmp[:, k, :tn], func=AF.Identity,
                                                                                   bias=mt[:, ci, 1, i, c:c + 1]), r=[tmpB[k], mtB], w=[hB])
                P.barrier()
                P.flush()

        def x_update(st, i, c, ti, t0, tn, ps, psB, xc, xcB, k):
            ci = 1 if t0 >= TL else 0
            P.dma("sp", xc[:, k, :tn], XD()[c * 128:(c + 1) * 128, t0:t0 + tn], r=xr(c, ti), w=[xcB[k]])
            P.op("dve", lambda e: e.scalar_tensor_tensor(out=xc[:, k, :tn], in0=ps[:, :tn], scalar=mt[:, ci, 2, i, c:c + 1], in1=xc[:, k, :tn],
                                                         op0=ALU.mult, op1=ALU.add), r=[psB, xcB[k], mtB], w=[xcB[k]])
            P.dma("pool", XD()[c * 128:(c + 1) * 128, t0:t0 + tn], xc[:, k, :tn], r=[xcB[k]], w=[xBv[cur['vc']][c][ti]])

        def ffn(l, i, n13, n2):
            with ExitStack() as ph:
                h = SB(ph, "h", [128, DC, T], BF16)
                hB = Buf()
                norm_mod(i, h, hB)
                G = SB(ph, "G", [128, FH, T], BF16)
                GB = Buf()
                wa = [SB(ph, "wa%d" % k, [128, DC * 128], BF16) for k in range(2)]
                wb = [SB(ph, "wb%d" % k, [128, DC * 128], BF16) for k in range(2)]
                w2 = [SB(ph, "w2%d" % k, [128, FH * 128], BF16) for k in range(2)]
                sa = SB(ph, "sa", [128, 2, 512])
                xc = SB(ph, "xc", [128, 2, 512])
                waB, wbB, w2B, saB, xcB = ([Buf(), Buf()] for _ in range(5))
                cnt = 0
                for half in range(2):
                    for jj in range(FH):
                        j = half * FH + jj
                        k = jj % 2
                        P.dma("sp", wa[k][:], wtile(l, n13, j), r=[wB[l]], w=[waB[k]])
                        P.dma("sp", wb[k][:], wtile(l, n13, FC + j), r=[wB[l]], w=[wbB[k]])
                        for ti, (t0, tn) in enumerate(cf.tiles):
                            pa, paB = bank()
                            pb, pbB = bank()

                            def mm(e, k=k, t0=t0, tn=tn, pa=pa, pb=pb):
                                for kc in range(DC):
                                    e.matmul(pa[:, :tn], wa[k][:, kc * 128:(kc + 1) * 128], h[:, kc, t0:t0 + tn], start=(kc == 0), stop=(kc == DC - 1))
                                for kc in range(DC):
                                    last = e.matmul(pb[:, :tn], wb[k][:, kc * 128:(kc + 1) * 128], h[:, kc, t0:t0 + tn], start=(kc == 0), stop=(kc == DC - 1))
                                return last
                            P.op("pe", mm, r=[waB[k], wbB[k], hB], w=[paB, pbB])
                            s_ = cnt % 2
                            cnt += 1
                            P.op("act", lambda e, s_=s_, tn=tn, pa=pa: e.activation(out=sa[:, s_, :tn], in_=pa[:, :tn], func=AF.Silu), r=[paB], w=[saB[s_]])
                            P.op("dve", lambda e, s_=s_, tn=tn, t0=t0, jj=jj, pb=pb: e.tensor_tensor(out=G[:, jj, t0:t0 + tn], in0=sa[:, s_, :tn], in1=pb[:, :tn], op=ALU.mult),
                                 r=[saB[s_], pbB], w=[GB])
                    for c in range(DC):
                        k = c % 2
                        P.dma("sp", w2[k][:], wtile(l, n2, c, half * FH * 128, (half + 1) * FH * 128), r=[wB[l]], w=[w2B[k]])
                        for ti, (t0, tn) in enumerate(cf.tiles):
                            ps, psB = bank()

                            def mm2(e, k=k, t0=t0, tn=tn, ps=ps):
                                for kk in range(FH):
                                    last = e.matmul(ps[:, :tn], w2[k][:, kk * 128:(kk + 1) * 128], G[:, kk, t0:t0 + tn], start=(kk == 0), stop=(kk == FH - 1))
                                return last
                            P.op("pe", mm2, r=[w2B[k], GB], w=[psB])
                            s_ = cnt % 2
                            cnt += 1
                            x_update(ph, i, c, ti, t0, tn, ps, psB, xc, xcB, s_)
                P.barrier()
                P.flush()

        def mixdst(n):
            NB, H = cf.NB, cf.H
            if n < NB:
                return n // cf.NBh, n % cf.NBh, None
            if n < NB + 3 * H:
                kind = (n - NB) // H
                hq = (n - NB) % H
                return hq // cf.Hh, cf.NBh + kind * cf.Hh + hq % cf.Hh, kind
            fc = n - NB - 3 * H
            return fc // cf.FCh, cf.NBh + 3 * cf.Hh + fc % cf.FCh, None

        mixBv = [Buf() for _ in range(NVC)]
        gatBv = [Buf() for _ in range(NVC)]
        yBv = [Buf() for _ in range(NVC)]

        def in_proj(l):
            with ExitStack() as ph:
                h = SB(ph, "h", [128, DC, T], BF16)
                hB = Buf()
                norm_mod(1, h, hB)
                w = [SB(ph, "w%d" % k, [128, DC * 128], BF16) for k in range(2)]
                st = [SB(ph, "st%d" % k, [128, T], BF16) for k in range(3)]
                qb = SB(ph, "qb", [128, 2, 512], BF16)
                t1 = SB(ph, "t1", [128, 2, 512])
                t2 = SB(ph, "t2", [128, 2, 512])
                rope = SB(ph, "rope", [128, 2 * TL])
                wBf, stB, qbB, t1B, t2B = [Buf(), Buf()], [Buf() for _ in range(3)], [Buf(), Buf()], [Buf(), Buf()], [Buf(), Buf()]
                rpB = Buf()
                s_ = cur["vc"] % 2
                P.dma("sp", rope[:], rope_d[:, s_ * 2 * TL:(s_ + 1) * 2 * TL], w=[rpB])
                cnt = 0
                for n in range(cf.NIN):
                    k = n % 2
                    sk = n % 3
                    P.dma("sp", w[k][:], wtile(l, "w_in", n), r=[wB[l]], w=[wBf[k]])
                    isg = n >= cf.MIXC
                    kind = None if isg else mixdst(n)[2]
                    for ti, (t0, tn) in enumerate(cf.tiles):
                        ps, psB = bank()

                        def mm(e, k=k, t0=t0, tn=tn, ps=ps):
                            for kc in range(DC):
                                last = e.matmul(ps[:, :tn], w[k][:, kc * 128:(kc + 1) * 128], h[:, kc, t0:t0 + tn], start=(kc == 0), stop=(kc == DC - 1))
                            return last
                        P.op("pe", mm, r=[wBf[k], hB], w=[psB])
                        if isg:
                            P.op("act", lambda e, sk=sk, t0=t0, tn=tn, ps=ps: e.activation(out=st[sk][:, t0:t0 + tn], in_=ps[:, :tn], func=AF.Sigmoid), r=[psB], w=[stB[sk]])
                        elif kind in (0, 1) and t0 < TL:
                            s_ = cnt % 2
                            cnt += 1
                            P.op("act", lambda e, s_=s_, tn=tn, ps=ps: e.activation(out=qb[:, s_, :tn], in_=ps[:, :tn], func=AF.Copy), r=[psB], w=[qbB[s_]])
                            p2, p2B = bank()
                            P.op("pe", lambda e, s_=s_, tn=tn, p2=p2: e.matmul(p2[:, :tn], permb[:], qb[:, s_, :tn], start=True, stop=True), r=[qbB[s_], cB], w=[p2B])
                            P.op("dve", lambda e, s_=s_, tn=tn, t0=t0, ps=ps: e.tensor_tensor(out=t1[:, s_, :tn], in0=ps[:, :tn], in1=rope[:, t0:t0 + tn], op=ALU.mult),
                                 r=[psB, rpB], w=[t1B[s_]])
                            P.op("dve", lambda e, s_=s_, tn=tn, t0=t0, p2=p2: e.tensor_tensor(out=t2[:, s_, :tn], in0=p2[:, :tn], in1=rope[:, TL + t0:TL + t0 + tn], op=ALU.mult),
                                 r=[p2B, rpB], w=[t2B[s_]])
                            P.op("dve", lambda e, s_=s_, tn=tn, t0=t0, sk=sk: e.tensor_tensor(out=st[sk][:, t0:t0 + tn], in0=t1[:, s_, :tn], in1=t2[:, s_, :tn], op=ALU.add),
                                 r=[t1B[s_], t2B[s_]], w=[stB[sk]])
                        else:
                            P.op("act", lambda e, sk=sk, t0=t0, tn=tn, ps=ps: e.activation(out=st[sk][:, t0:t0 + tn], in_=ps[:, :tn], func=AF.Copy), r=[psB], w=[stB[sk]])
                    if isg:
                        g0 = (n - cf.MIXC) * 128
                        P.dma("pool", GAT()[g0:g0 + 128, :], st[sk][:], r=[stB[sk]], w=[gatBv[cur["vc"]]])
                    else:
                        hf, idx, _ = mixdst(n)
                        r0 = (cur["vc"] * cf.MH + idx) * 128
                        P.dma("pool", mixl.ap()[r0:r0 + 128, :], st[sk][:], r=[stB[sk]], w=[mixBv[cur["vc"]]])
                if cf.dbg and l == 0 and cur['vc'] == cf.dbgvc:
                    P.dma("pool", dbg["d_mix"], mixl.ap()[cur["vc"] * cf.MH * 128:(cur["vc"] + 1) * cf.MH * 128, :], r=[mixBv[cur["vc"]]])
                P.barrier()
                P.flush()

        mixl4 = mixl.ap().rearrange("(v m p) t -> p v m t", v=NVC, p=128)
        yhd4 = yhd.ap().rearrange("(v m p) t -> p v m t", v=NVC, p=128)

        def load_mix(dst, idx, B_):
            bb = cur['b']
            for r in range(2):
                P.dma("sp", dst[:, r * TL:(r + 1) * TL], mixl4[:, 2 * bb + r, idx, 0:TL], r=[mixBv[2 * bb + r]], w=[B_])
                P.dma("sp", dst[:, L + r * TC:L + (r + 1) * TC], mixl4[:, 2 * bb + r, idx, TL:T], r=[mixBv[2 * bb + r]], w=[B_])

        def store_y(src, idx, B_):
            bb = cur['b']
            for j in range(2):
                P.dma("sp", yhd4[:, 2 * bb + j, idx, 0:TL], src[:, j * TL:(j + 1) * TL], r=[B_], w=[yBv[2 * bb + j]])
                P.dma("sp", yhd4[:, 2 * bb + j, idx, TL:T], src[:, L + j * TC:L + (j + 1) * TC], r=[B_], w=[yBv[2 * bb + j]])

        def lru_phase(l):
            with ExitStack() as ph:
                u = SB(ph, "u", [128, LT], BF16)
                vf = SB(ph, "vf", [128, LT])
                vb = SB(ph, "vb", [128, LT], BF16)
                rt = SB(ph, "rt", [128, LT])
                it = SB(ph, "it", [128, LT])
                at = SB(ph, "at", [128, LT])
                tm = SB(ph, "tm", [128, LT])
                hh = [SB(ph, "hh%d" % d, [128, LT]) for d in range(2)]
                yo = SB(ph, "yo", [128, LT], BF16)
                lw = SB(ph, "lw", [128, 2 * 2 * 128])
                lwb = SB(ph, "lwb", [128, 2 * 2 * 128], BF16)
                uB, vfB, vbB, rB, iB, aB, tB, yoB, lwB = (Buf() for _ in range(9))
                hB2 = [Buf(), Buf()]
                segs = ((0, L), (L, LT))
                for blk in range(cf.NBh):
                    load_mix(u, blk, uB)
                    cw = lambda j_, blk=blk: V("conv_w", (l * cf.NBh + blk) * 4 + j_)
                    P.op("dve", lambda e, cw=cw, blk=blk: e.tensor_scalar(out=vf[:], in0=u[:], scalar1=cw(2), scalar2=V("conv_b", l * cf.NBh + blk),
                                                                          op0=ALU.mult, op1=ALU.add), r=[uB, cB], w=[vfB])
                    for j_, off in ((0, -2), (1, -1), (3, 1)):
                        for s0, s1 in segs:
                            a, b = s0 + max(0, -off), s1 - max(0, off)
                            P.op("dve", lambda e, cw=cw, j_=j_, a=a, b=b, off=off: e.scalar_tensor_tensor(out=vf[:, a:b], in0=u[:, a + off:b + off], scalar=cw(j_),
                                                                                                          in1=vf[:, a:b], op0=ALU.mult, op1=ALU.add), r=[uB, cB, vfB], w=[vfB])
                    P.op("act", lambda e: e.activation(out=vb[:], in_=vf[:], func=AF.Copy), r=[vfB], w=[vbB])
                    for d in range(2):
                        wi0 = ((l * 2 + d) * cf.NBh + blk) * 2 * 128
                        P.dma("sp", lw[:, 0:256], lruw_d[:, wi0:wi0 + 256], w=[lwB])
                        P.op("dve", lambda e: e.tensor_copy(out=lwb[:, 0:256], in_=lw[:, 0:256]), r=[lwB], w=[lwB])
                        vi = (l * 2 + d) * cf.NBh + blk
                        t0 = 0
                        while t0 < LT:
                            tn = min(512, LT - t0)
                            pr, prB = bank()
                            pi, piB = bank()
                            P.op("pe", lambda e, t0=t0, tn=tn, pr=pr: e.matmul(pr[:, :tn], lwb[:, 0:128], vb[:, t0:t0 + tn], start=True, stop=True), r=[lwB, vbB], w=[prB])
                            P.op("pe", lambda e, t0=t0, tn=tn, pi=pi: e.matmul(pi[:, :tn], lwb[:, 128:256], vb[:, t0:t0 + tn], start=True, stop=True), r=[lwB, vbB], w=[piB])
                            P.op("act", lambda e, t0=t0, tn=tn, pr=pr, vi=vi: e.activation(out=rt[:, t0:t0 + tn], in_=pr[:, :tn], func=AF.Sigmoid, bias=V("ba", vi)), r=[prB, cB], w=[rB])
                            P.op("act", lambda e, t0=t0, tn=tn, pi=pi, vi=vi: e.activation(out=it[:, t0:t0 + tn], in_=pi[:, :tn], func=AF.Sigmoid, bias=V("bi", vi)), r=[piB, cB], w=[iB])
                            t0 += tn
                        P.op("act", lambda e, vi=vi: e.activation(out=at[:], in_=rt[:], func=AF.Exp, scale=nsp8[:, vi:vi + 1]), r=[rB, cB], w=[aB])
                        P.op("act", lambda e: e.activation(out=tm[:], in_=at[:], func=AF.Square), r=[aB], w=[tB])
                        P.op("act", lambda e: e.activation(out=tm[:], in_=tm[:], func=AF.Sqrt, scale=-1.0, bias=1.0), r=[tB], w=[tB])
                        P.op("dve", lambda e: e.tensor_tensor(out=it[:], in0=it[:], in1=vf[:], op=ALU.mult), r=[iB, vfB], w=[iB])
                        P.op("dve", lambda e: e.tensor_tensor(out=it[:], in0=it[:], in1=tm[:], op=ALU.mult), r=[iB, tB], w=[iB])
                        hd = hh[d]
                        if d == 0:
                            P.op("dve", lambda e, hd=hd: e.tensor_tensor_scan(out=hd[:, L:LT], data0=at[:, L:LT], data1=it[:, L:LT], initial=0.0, op0=ALU.mult, op1=ALU.add),
                                 r=[aB, iB], w=[hB2[d]])
                            P.op("dve", lambda e, hd=hd: e.tensor_tensor_scan(out=hd[:, 0:L], data0=at[:, 0:L], data1=it[:, 0:L], initial=hd[:, LT - 1:LT], op0=ALU.mult, op1=ALU.add),
                                 r=[aB, iB, hB2[d]], w=[hB2[d]])
                        else:
                            P.op("dve", lambda e, hd=hd: e.tensor_tensor_scan(out=hd[:, L:LT][:, ::-1], data0=at[:, L:LT][:, ::-1], data1=it[:, L:LT][:, ::-1], initial=0.0,
                                                                              op0=ALU.mult, op1=ALU.add), r=[aB, iB], w=[hB2[d]])
                            P.op("dve", lambda e, hd=hd: e.tensor_tensor_scan(out=hd[:, 0:L][:, ::-1], data0=at[:, 0:L][:, ::-1], data1=it[:, 0:L][:, ::-1], initial=hd[:, L:L + 1],
                                                                              op0=ALU.mult, op1=ALU.add), r=[aB, iB, hB2[d]], w=[hB2[d]])
                    P.op("dve", lambda e: e.tensor_tensor(out=yo[:], in0=hh[0][:], in1=hh[1][:], op=ALU.add), r=hB2, w=[yoB])
                    store_y(yo, blk, yoB)
                P.barrier()
                P.flush()

        def na_phase(l):
            scale = 128.0 ** -0.5
            NCK = LT // 128
            with ExitStack() as ph:
                qs = [SB(ph, "q%d" % k, [128, LT], BF16) for k in range(2)]
                ks = [SB(ph, "k%d" % k, [128, LT], BF16) for k in range(2)]
                vs = [SB(ph, "v%d" % k, [128, LT], BF16) for k in range(2)]
                bs = [SB(ph, "b%d" % k, [128, 9 * 128]) for k in range(2)]
                vt = SB(ph, "vt", [128, NCK, 128], BF16)
                yo = SB(ph, "yo", [128, LT], BF16)
                ssb = [SB(ph, "ssb%d" % k, [128, 5 * 128]) for k in range(2)]
                pt = [SB(ph, "pt%d" % k, [128, 5 * 128 + max(LC, 512 - 640 + 640)], BF16) for k in range(2)]
                rd = [SB(ph, "rd%d" % k, [128, 256]) for k in range(2)]
                qB, kB, vB, bB = ([Buf(), Buf()] for _ in range(4))
                vtB, yoB = Buf(), Buf()
                ssB, ptB, rdB = [Buf(), Buf()], [Buf(), Buf()], [Buf(), Buf()]
                cnt = 0
                NCC = LC // 128
                for hh in range(cf.Hh):
                    k = hh % 2
                    q, kk_, v, bt = qs[k], ks[k], vs[k], bs[k]
                    load_mix(q, cf.NBh + hh, qB[k])
                    load_mix(kk_, cf.NBh + cf.Hh + hh, kB[k])
                    load_mix(v, cf.NBh + 2 * cf.Hh + hh, vB[k])
                    r0 = (l * cf.Hh + hh) * 128
                    P.dma("sp", bt[:], nab_d[r0:r0 + 128, :], w=[bB[k]])
                    c0 = 0
                    while c0 < NCK:
                        n = min(4, NCK - c0)
                        ps, psB = bank()

                        def tr(e, c0=c0, n=n, ps=ps, v=v):
                            for j in range(n):
                                last = e.matmul(ps[:, j * 128:(j + 1) * 128], v[:, (c0 + j) * 128:(c0 + j + 1) * 128], identb[:], start=True, stop=True)
                            return last
                        P.op("pe", tr, r=[vB[k], cB], w=[psB])
                        P.op("act", lambda e, c0=c0, n=n, ps=ps: e.activation(out=vt[:, c0:c0 + n, :].rearrange("p c d -> p (c d)"), in_=ps[:, :n * 128], func=AF.Copy),
                             r=[psB], w=[vtB])
                        c0 += n
                    for pr_ in range(cf.ROWS // 2):
                        r = 2 * pr_
                        if r < 4:
                            base, tl = 0, [(2 * c - r + 6) // 2 for c in range(4)]
                        elif r >= cf.ROWS - 4:
                            base = cf.ROWS - 8
                            tl = [(base + 2 * c - r + 6) // 2 for c in range(4)]
                        else:
                            base, tl = r - 4, [7, 2, 3, 4, 8]
                        nch = len(tl)
                        s_ = cnt % 2
                        cnt += 1
                        pA, pAB = bank()
                        pB_, pBB = bank()

                        def smm(e, base=base, nch=nch, r=r, pA=pA, pB_=pB_, q=q, kk_=kk_):
                            for c in range(nch):
                                dst = pA[:, c * 128:(c + 1) * 128] if c < 4 else pB_[:, 0:128]
                                e.matmul(dst, kk_[:, (base + 2 * c) * 64:(base + 2 * c) * 64 + 128], q[:, r * 64:r * 64 + 128], start=True, stop=True)
                            for cc in range(NCC):
                                last = e.matmul(pB_[:, 128 + cc * 128:256 + cc * 128], kk_[:, L + cc * 128:L + (cc + 1) * 128], q[:, r * 64:r * 64 + 128], start=True, stop=True)
                            return last
                        P.op("pe", smm, r=[qB[k], kB[k]], w=[pAB, pBB])
                        for c in range(nch):
                            src = pA[:, c * 128:(c + 1) * 128] if c < 4 else pB_[:, 0:128]
                            P.op("dve", lambda e, c=c, src=src, s_=s_, ti_=tl[c], bt=bt: e.scalar_tensor_tensor(out=ssb[s_][:, c * 128:(c + 1) * 128], in0=src, scalar=scale,
                                                                                                              in1=bt[:, ti_ * 128:(ti_ + 1) * 128], op0=ALU.mult, op1=ALU.add),
                                 r=[pAB if c < 4 else pBB, bB[k]], w=[ssB[s_]])
                        P.op("act", lambda e, s_=s_, nch=nch: e.activation(out=pt[s_][:, 0:nch * 128], in_=ssb[s_][:, 0:nch * 128], func=AF.Exp), r=[ssB[s_]], w=[ptB[s_]])
                        P.op("act", lambda e, s_=s_, pB_=pB_: e.activation(out=pt[s_][:, 640:640 + LC], in_=pB_[:, 128:128 + LC], func=AF.Exp, scale=scale), r=[pBB], w=[ptB[s_]])
                        pO, pOB = bank()
                        pD, pDB = bank()

                        def pv(e, base=base, nch=nch, s_=s_, pO=pO, pD=pD):
                            tot = nch + NCC
                            for m_, lhs in ((pO, None), (pD, onesb)):
                                for c in range(tot):
                                    if c < nch:
                                        rhs = pt[s_][:, c * 128:(c + 1) * 128]
                                        vch = base // 2 + c
                                    else:
                                        rhs = pt[s_][:, 640 + (c - nch) * 128:640 + (c - nch + 1) * 128]
                                        vch = L // 128 + (c - nch)
                                    last = e.matmul(m_[:, 0:128], vt[:, vch, :] if lhs is None else lhs[:], rhs, start=(c == 0), stop=(c == tot - 1))
                            return last
                        P.op("pe", pv, r=[ptB[s_], vtB, cB], w=[pOB, pDB])
                        P.op("dve", lambda e, s_=s_, pD=pD: e.reciprocal(out=rd[s_][:, 0:128], in_=pD[:, 0:128]), r=[pDB], w=[rdB[s_]])
                        P.op("dve", lambda e, s_=s_, pO=pO, r=r: e.tensor_tensor(out=yo[:, r * 64:r * 64 + 128], in0=pO[:, 0:128], in1=rd[s_][:, 0:128], op=ALU.mult),
                             r=[pOB, rdB[s_]], w=[yoB])
                    s_ = cnt % 2
                    cnt += 1
                    pA, pAB = bank()

                    def cmm(e, pA=pA, q=q, kk_=kk_):
                        for cc in range(NCC):
                            last = e.matmul(pA[:, cc * LC:(cc + 1) * LC], kk_[:, L + cc * 128:L + (cc + 1) * 128], q[:, L:LT], start=True, stop=True)
                        return last
                    P.op("pe", cmm, r=[qB[k], kB[k]], w=[pAB])
                    P.op("act", lambda e, s_=s_, pA=pA: e.activation(out=pt[s_][:, 0:NCC * LC], in_=pA[:, 0:NCC * LC], func=AF.Exp, scale=scale), r=[pAB], w=[ptB[s_]])
                    pO, pOB = bank()
                    pD, pDB = bank()

                    def cpv(e, s_=s_, pO=pO, pD=pD):
                        for m_, lhs in ((pO, None), (pD, onesb)):
                            for cc in range(NCC):
                                last = e.matmul(m_[:, 0:LC], vt[:, L // 128 + cc, :] if lhs is None else lhs[:], pt[s_][:, cc * LC:(cc + 1) * LC],
                                                start=(cc == 0), stop=(cc == NCC - 1))
                        return last
                    P.op("pe", cpv, r=[ptB[s_], vtB, cB], w=[pOB, pDB])
                    P.op("dve", lambda e, s_=s_, pD=pD: e.reciprocal(out=rd[s_][:, 0:LC], in_=pD[:, 0:LC]), r=[pDB], w=[rdB[s_]])
                    P.op("dve", lambda e, s_=s_, pO=pO: e.tensor_tensor(out=yo[:, L:LT], in0=pO[:, 0:LC], in1=rd[s_][:, 0:LC], op=ALU.mult), r=[pOB, rdB[s_]], w=[yoB])
                    store_y(yo, cf.NBh + hh, yoB)
                P.barrier()
                P.flush()

        def fft_phase(l):
            CG, CGC = cf.CG, cf.CGC
            with ExitStack() as ph:
                f_ = [SB(ph, "f%d" % c, [128, LT], BF16) for c in range(CGC)]
                yo = [SB(ph, "yo%d" % c, [128, LT], BF16) for c in range(CGC)]
                gd = SB(ph, "gd", [128, CGC, 2 * CG], BF16)
                Z = SB(ph, "Z", [128, L // 128, 2 * CG], BF16)
                tab = [SB(ph, "tab%d" % k, [128, (L // 128) * 2 * cf.kwL], BF16) for k in range(2)]
                fB, yoB = [Buf() for _ in range(CGC)], [Buf() for _ in range(CGC)]
                gdB, ZB = Buf(), Buf()
                tabB = [Buf(), Buf()]
                P.dma("sp", gd[:].rearrange("p c m -> p (c m)"), ctile("dftG", 0, CG * 2 * CG), r=[cstB], w=[gdB])
                tk = 0
                for g in range(cf.NGh):
                    for cc in range(CGC):
                        load_mix(f_[cc], cf.NBh + 3 * cf.Hh + g * CGC + cc, fB[cc])
                    for (n, off, nm, kw) in ((L, 0, "dftL", cf.kwL), (LC, L, "dftC", cf.kwC)):
                        TCH = n // 128
                        for tch in range(TCH):
                            ps, psB = bank()

                            def s1(e, tch=tch, off=off, ps=ps):
                                for cc in range(CGC):
                                    last = e.matmul(ps[:, 0:2 * CG], f_[cc][:, off + tch * 128:off + (tch + 1) * 128], gd[:, cc, :], start=(cc == 0), stop=(cc == CGC - 1))
                                return last
                            P.op("pe", s1, r=fB + [gdB], w=[psB])
                            P.op("act", lambda e, tch=tch, ps=ps: e.activation(out=Z[:, tch, :], in_=ps[:, 0:2 * CG], func=AF.Copy), r=[psB], w=[ZB])
                        tsz = TCH * 2 * kw
                        for kt in range(n // kw):
                            k = tk % 2
                            tk += 1
                            P.dma("sp", tab[k][:, 0:tsz], ctile(nm, kt * 128 * tsz, 128 * tsz), r=[cstB], w=[tabB[k]])
                            tv = tab[k][:, 0:tsz].rearrange("p (c a k) -> p c a k", a=2, k=kw)
                            for mc in range(CGC):
                                ps, psB = bank()

                                def s2(e, mc=mc, ps=ps, tv=tv, TCH=TCH, kw=kw):
                                    for tch in range(TCH):
                                        e.matmul(ps[:, :kw], Z[:, tch, mc * 128:(mc + 1) * 128], tv[:, tch, 0, :], start=(tch == 0), stop=False)
                                        last = e.matmul(ps[:, :kw], Z[:, tch, CG + mc * 128:CG + (mc + 1) * 128], tv[:, tch, 1, :], start=False, stop=(tch == TCH - 1))
                                    return last
                                P.op("pe", s2, r=[ZB, tabB[k]], w=[psB])
                                P.op("act", lambda e, mc=mc, ps=ps, off=off, kt=kt, kw=kw, n=n: e.activation(out=yo[mc][:, off + kt * kw:off + (kt + 1) * kw], in_=ps[:, :kw],
                                                                                                             func=AF.Copy, scale=float((n * CG) ** -0.5)), r=[psB], w=[yoB[mc]])
                    for cc in range(CGC):
                        store_y(yo[cc], cf.NBh + cf.Hh + g * CGC + cc, yoB[cc])
                P.barrier()
                P.flush()

        def merge(l):
            NB, H, FCc = cf.NB, cf.H, cf.FW // 128
            with ExitStack() as ph:
                M = SB(ph, "M", [128, DC, T], BF16)
                MB = Buf()
                with ExitStack() as ph2:
                    Y = SB(ph2, "Y", [128, cf.YH, T], BF16)
                    YB = Buf()
                    P.dma("sp", Y[:], yhd4[:, cur['vc'], :, :], r=[yBv[cur['vc']]], w=[YB])
                    wl = [SB(ph2, "wl%d" % k, [128, (NB + H + FCc) * 128], BF16) for k in range(2)]
                    gt = [SB(ph2, "gt%d" % k, [128, 3, T], BF16) for k in range(2)]
                    t1 = SB(ph2, "t1", [128, 2, 512])
                    t2 = SB(ph2, "t2", [128, 2, 512])
                    wlB, gtB, t1B, t2B = ([Buf(), Buf()] for _ in range(4))
                    ysrc = []
                    for b_ in range(NB):
                        ysrc.append((b_ // cf.NBh) * cf.YH + b_ % cf.NBh)
                    for h_ in range(H):
                        ysrc.append((h_ // cf.Hh) * cf.YH + cf.NBh + h_ % cf.Hh)
                    for c_ in range(FCc):
                        ysrc.append((c_ // cf.FCh) * cf.YH + cf.NBh + cf.Hh + c_ % cf.FCh)
                    grp = ((0, NB), (NB, NB + H), (NB + H, NB + H + FCc))
                    cnt = 0
                    for c in range(DC):
                        k = c % 2
                        P.dma("sp", wl[k][:, 0:NB * 128], wtile(l, "w_out_lru", c), r=[wB[l]], w=[wlB[k]])
                        P.dma("sp", wl[k][:, NB * 128:(NB + H) * 128], wtile(l, "w_out_na", c), r=[wB[l]], w=[wlB[k]])
                        P.dma("sp", wl[k][:, (NB + H) * 128:], wtile(l, "w_out_fft", c), r=[wB[l]], w=[wlB[k]])
                        P.dma("sp", gt[k][:], GAT().rearrange("(g c p) t -> p g c t", g=3, p=128)[:, :, c, :], r=[gatBv[cur["vc"]]], w=[gtB[k]])
                        for ti, (t0, tn) in enumerate(cf.tiles):
                            pss = [bank() for _ in range(3)]

                            def mm(e, k=k, t0=t0, tn=tn, pss=pss):
                                for gi, (a, b) in enumerate(grp):
                                    for kk in range(a, b):
                                        last = e.matmul(pss[gi][0][:, :tn], wl[k][:, kk * 128:(kk + 1) * 128], Y[:, ysrc[kk], t0:t0 + tn], start=(kk == a), stop=(kk == b - 1))
                                return last
                            P.op("pe", mm, r=[wlB[k], YB], w=[p[1] for p in pss])
                            s_ = cnt % 2
                            cnt += 1
                            P.op("dve", lambda e, s_=s_, k=k, t0=t0, tn=tn, p=pss[0][0]: e.tensor_tensor(out=t1[:, s_, :tn], in0=p[:, :tn], in1=gt[k][:, 0, t0:t0 + tn], op=ALU.mult),
                                 r=[pss[0][1], gtB[k]], w=[t1B[s_]])
                            P.op("dve", lambda e, s_=s_, k=k, t0=t0, tn=tn, p=pss[1][0]: e.tensor_tensor(out=t2[:, s_, :tn], in0=p[:, :tn], in1=gt[k][:, 1, t0:t0 + tn], op=ALU.mult),
                                 r=[pss[1][1], gtB[k]], w=[t2B[s_]])
                            P.op("dve", lambda e, s_=s_, tn=tn: e.tensor_tensor(out=t1[:, s_, :tn], in0=t1[:, s_, :tn], in1=t2[:, s_, :tn], op=ALU.add), r=[t1B[s_], t2B[s_]], w=[t1B[s_]])
                            P.op("dve", lambda e, s_=s_, k=k, t0=t0, tn=tn, p=pss[2][0]: e.tensor_tensor(out=t2[:, s_, :tn], in0=p[:, :tn], in1=gt[k][:, 2, t0:t0 + tn], op=ALU.mult),
                                 r=[pss[2][1], gtB[k], t1B[s_]], w=[t2B[s_]])
                            P.op("dve", lambda e, s_=s_, c=c, t0=t0, tn=tn: e.tensor_tensor(out=M[:, c, t0:t0 + tn], in0=t1[:, s_, :tn], in1=t2[:, s_, :tn], op=ALU.add),
                                 r=[t1B[s_], t2B[s_]], w=[MB])
                    P.barrier()
                    P.flush()
                wo = [SB(ph, "wo%d" % k, [128, DC * 128], BF16) for k in range(2)]
                xc = SB(ph, "xc", [128, 2, 512])
                woB, xcB = [Buf(), Buf()], [Buf(), Buf()]
                cnt = 0
                for c in range(DC):
                    k = c % 2
                    P.dma("sp", wo[k][:], wtile(l, "w_o", c), r=[wB[l]], w=[woB[k]])
                    for ti, (t0, tn) in enumerate(cf.tiles):
                        ps, psB = bank()

                        def mm(e, k=k, t0=t0, tn=tn, ps=ps):
                            for kc in range(DC):
                                last = e.matmul(ps[:, :tn], wo[k][:, kc * 128:(kc + 1) * 128], M[:, kc, t0:t0 + tn], start=(kc == 0), stop=(kc == DC - 1))
                            return last
                        P.op("pe", mm, r=[woB[k], MB], w=[psB])
                        s_ = cnt % 2
                        cnt += 1
                        x_update(ph, 1, c, ti, t0, tn, ps, psB, xc, xcB, s_)
                P.barrier()
                P.flush()

        def final_norm():
            with ExitStack() as ph:
                xt = SB(ph, "xt", [128, DC, 512])
                sq = SB(ph, "sq", [128, 2, 512])
                rstd = SB(ph, "rstd", [128, 512])
                xtB, rB = Buf(), Buf()
                sqB = [Buf(), Buf()]
                for ti, (t0, tn) in enumerate(cf.tiles):
                    if t0 >= TL:
                        continue
                    P.dma("sp", xt[:, :, :tn], XD().rearrange("(c p) t -> p c t", p=128)[:, :, t0:t0 + tn],
                          r=[b for c in range(DC) for b in xr(c, ti)], w=[xtB])
                    sumsq_rstd(ph, xt, tn, xtB, rstd, rB, (sq, sqB))
                    for c in range(DC):
                        P.op("dve", lambda e, c=c, tn=tn: e.scalar_tensor_tensor(out=xt[:, c, :tn], in0=xt[:, c, :tn], scalar=V("final_g", c), in1=rstd[:, :tn],
                                                                                 op0=ALU.mult, op1=ALU.mult), r=[xtB, rB, cB], w=[xtB])
                    ev = P.dma("sp", out_d[cur["vc"] * D:(cur["vc"] + 1) * D, :].rearrange("(c p) t -> p c t", p=128)[:, :, t0:t0 + tn], xt[:, :, :tn], r=[xtB])
                    for e_ in P.ENG:
                        P._wait(e_, ev)
                P.barrier()
                P.flush()

        def layers():
            for l in range(DEPTH):
                for vc in range(NVC):
                    cur['vc'] = vc
                    if cf.stop < 2:
                        return
                    load_mods(l)
                    if cf.stop < 3:
                        return
                    ffn(l, 0, "ffn1_w13", "ffn1_w2")
                    if cf.dbg and l == 0 and vc == cf.dbgvc:
                        P.dma("sp", dbg["d_x1"], XD(), r=[b for c in range(DC) for ti in range(len(cf.tiles)) for b in xr(c, ti)])
                    if cf.stop < 4:
                        continue
                    in_proj(l)
                if cf.stop < 5:
                    return
                for b_ in range(NBAT):
                    cur['b'] = b_
                    lru_phase(l)
                    if cf.stop < 6:
                        continue
                    na_phase(l)
                    if cf.stop < 7:
                        continue
                    fft_phase(l)
                if cf.dbg and l == 0:
                    P.dma("pool", dbg["d_y"], yhd.ap()[cf.dbgvc * cf.YH * 128:(cf.dbgvc + 1) * cf.YH * 128, :], r=[yBv[cf.dbgvc]])
                if cf.stop < 8:
                    return
                for vc in range(NVC):
                    cur['vc'] = vc
                    load_mods(l)
                    merge(l)
                    if cf.stop < 9:
                        continue
                    ffn(l, 2, "ffn2_w13", "ffn2_w2")
                    if cf.dbg and l == 0 and vc == cf.dbgvc:
                        P.dma("sp", dbg["d_x2"], XD(), r=[b for c in range(DC) for ti in range(len(cf.tiles)) for b in xr(c, ti)])
        layers()
        for vc in range(NVC):
            cur['vc'] = vc
            final_norm()
    return nc


_CACHE = {}


def run(cf, inputs):
    maps = host_inputs(cf, inputs)
    key = (cf.D, cf.L, cf.LC, cf.NG, cf.DEPTH, cf.dbg, cf.stop)
    if key not in _CACHE:
        _CACHE[key] = build(cf)
    res = run_bass_kernel_spmd(_CACHE[key], maps, core_ids=list(range(cf.NCORE)))
    out = np.zeros((4, cf.L, cf.D), np.float32)
    for core in range(cf.NCORE):
        o = res.results[core]["out"].reshape(cf.NVC, cf.D, cf.TL)
        for v in range(cf.NVC):
            vc = core * cf.NVC + v
            b, s = vc // 2, vc % 2
            out[b, s * cf.TL:(s + 1) * cf.TL, :] = o[v].T
    return out, res


def kernel(**inputs):
    cf = Cfg()
    out, _ = run(cf, inputs)
    return out
```
